# Optimizing a Trainium2 kernel written in Bass

```python
import jax
import jax.numpy as jnp
from jax import lax
import numpy as np

D_MODEL = 2048
BATCH = 4
SEQ = 2048
DEPTH = 2

HEAD_DIM = 64
MIX_HEADS = D_MODEL // HEAD_DIM
MIX_W = MIX_HEADS * HEAD_DIM
FFN_HIDDEN = 5632
PLE_DIM = 256
ROPE_THETA = 10000.0
NORM_EPS = 1e-6
GRID_W = 64

A_HEADS = 12
A_W = A_HEADS * HEAD_DIM
A_BRANCHES = ((128, 1), (512, 4), (2048, 16))
A_BLOCK = 64

B_HEADS = MIX_HEADS - A_HEADS
B_W = B_HEADS * HEAD_DIM
RW_DECAY_LORA = 64
RW_AAA_LORA = 64
RW_GATE_LORA = 192
RW_GN_EPS = 64e-5
RW_COLS = 3 * B_W + 2 * RW_DECAY_LORA + 2 * RW_AAA_LORA + RW_GATE_LORA
RW_SPLITS = (B_W, 2 * B_W, 3 * B_W, 3 * B_W + 2 * RW_DECAY_LORA,
             3 * B_W + 2 * RW_DECAY_LORA + 2 * RW_AAA_LORA)
AB_COLS = 3 * A_W + RW_COLS

C_HEADS = 16
C_KV_HEADS = 2
C_GROUP = C_HEADS // C_KV_HEADS
C_W = C_HEADS * HEAD_DIM
C_KV_W = C_KV_HEADS * HEAD_DIM
C_HALF = 128
C_BLOCK = 128

D_HEADS = MIX_HEADS - C_HEADS
D_W = D_HEADS * HEAD_DIM
NA_KH = 8
NA_KW = 16
NA_QCOLS = 16
CD_SPLITS = (C_W, C_W + C_KV_W, C_W + 2 * C_KV_W, C_W + 2 * C_KV_W + D_W,
             C_W + 2 * C_KV_W + 2 * D_W)
CD_COLS = C_W + 2 * C_KV_W + 3 * D_W

kernel_name = 'hybrid_bidir_encoder_block'


def rms_norm(x, g):
    xf = x.astype(jnp.float32)
    y = xf * lax.rsqrt(jnp.mean(xf * xf, axis=-1, keepdims=True) + NORM_EPS)
    return (y * g.astype(jnp.float32)).astype(x.dtype)


def swiglu(x, w1, w3, w2):
    return (jax.nn.silu(x @ w1) * (x @ w3)) @ w2


def rope(x, pos):
    half = x.shape[-1] // 2
    inv_freq = ROPE_THETA ** (-jnp.arange(half, dtype=jnp.float32) / half)
    ang = pos.astype(jnp.float32)[:, None] * inv_freq[None, :]
    cos = jnp.cos(ang)[None, :, None, :]
    sin = jnp.sin(ang)[None, :, None, :]
    xf = x.astype(jnp.float32)
    x1, x2 = xf[..., :half], xf[..., half:]
    return jnp.concatenate([x1 * cos - x2 * sin, x2 * cos + x1 * sin], axis=-1).astype(x.dtype)


def banded_attention(q, k, v, half, block, sink=None):
    b, hk, g, n, hd = q.shape
    nb = -(-n // block)
    npad = nb * block
    span = block + 2 * half
    qb = jnp.pad(q, ((0, 0), (0, 0), (0, 0), (0, npad - n), (0, 0))).reshape(b, hk, g, nb, block, hd)
    kv_pad = ((0, 0), (0, 0), (half, npad - n + half), (0, 0))
    kidx = jnp.arange(nb)[:, None] * block + jnp.arange(span)[None, :]
    kb = jnp.pad(k, kv_pad)[:, :, kidx]
    vb = jnp.pad(v, kv_pad)[:, :, kidx]
    qpos = jnp.arange(npad).reshape(nb, block)
    kpos = kidx - half
    in_range = (kpos >= 0) & (kpos < n)
    same = qpos[:, :, None] == kpos[:, None, :]
    mask = (jnp.abs(qpos[:, :, None] - kpos[:, None, :]) <= half) & (in_range[:, None, :] | same)
    s = jnp.einsum('bhgnqd,bhnkd->bhgnqk', qb, kb, preferred_element_type=jnp.float32) * (hd ** -0.5)
    s = jnp.where(mask, s, -jnp.inf)
    m = jnp.max(s, axis=-1)
    if sink is not None:
        sk = sink.astype(jnp.float32)[None, :, :, None, None]
        m = jnp.maximum(m, sk)
    p = jnp.exp(s - m[..., None])
    den = jnp.sum(p, axis=-1)
    if sink is not None:
        den = den + jnp.exp(sk - m)
    o = jnp.einsum('bhgnqk,bhnkd->bhgnqd', p, vb.astype(jnp.float32)) / den[..., None]
    lse = m + jnp.log(den)
    o = o.reshape(b, hk, g, npad, hd)[:, :, :, :n]
    lse = lse.reshape(b, hk, g, npad)[..., :n]
    return o, lse


def dilated_attention(q, k, v):
    b, n, h, hd = q.shape
    outs, lses = [], []
    for window, dil in A_BRANCHES:
        half = window // (2 * dil)
        m = n // dil

        def strided(t):
            return t.reshape(b, m, dil, h, hd).transpose(0, 3, 2, 1, 4).reshape(b, h * dil, m, hd)

        o, lse = banded_attention(strided(q)[:, :, None], strided(k), strided(v), half, A_BLOCK)
        outs.append(o[:, :, 0].reshape(b, h, dil, m, hd).transpose(0, 3, 2, 1, 4).reshape(b, n, h, hd))
        lses.append(lse[:, :, 0].reshape(b, h, dil, m).transpose(0, 3, 2, 1).reshape(b, n, h))
    wts = jax.nn.softmax(jnp.stack(lses, axis=0), axis=0)
    return jnp.einsum('kbsh,kbshd->bshd', wts, jnp.stack(outs, axis=0))


def wkv7_scan(r, w, k, v, a, bb):
    n_, h_, d_ = r.shape[1:]

    def step(state, inp):
        r_t, w_t, k_t, v_t, a_t, b_t = inp
        sa = jnp.einsum('nhvk,nhk->nhv', state, a_t)
        state = (state * w_t[:, :, None, :] + sa[..., None] * b_t[:, :, None, :]
                 + v_t[..., None] * k_t[:, :, None, :])
        return state, jnp.einsum('nhvk,nhk->nhv', state, r_t)

    s0 = jnp.zeros((n_, h_, d_, d_), jnp.float32)
    _, y = lax.scan(step, s0, (r, w, k, v, a, bb))
    return y


def rwkv7_bidir(z, mu_prev, mu_next, w0, w2, a0, a2, g2, k_k, k_a, r_k, ln_w, ln_b):
    b, n, _ = z.shape
    z = z.astype(jnp.float32)
    z_prev = jnp.pad(z, ((0, 0), (1, 0), (0, 0)))[:, :-1]
    z_next = jnp.pad(z, ((0, 0), (0, 1), (0, 0)))[:, 1:]
    z = z + mu_prev * (z_prev - z) + mu_next * (z_next - z)
    r, k, v, wl, al, gl = jnp.split(z, RW_SPLITS, axis=-1)
    wl = wl.reshape(b, n, 2, RW_DECAY_LORA)
    al = al.reshape(b, n, 2, RW_AAA_LORA)
    w = -jax.nn.softplus(-(w0 + jnp.einsum('bsel,elc->bsec', jnp.tanh(wl), w2))) - 0.5
    decay = jnp.exp(-jnp.exp(w))
    a = jax.nn.sigmoid(a0 + jnp.einsum('bsel,elc->bsec', al, a2))
    g = jax.nn.sigmoid(gl) @ g2

    def hv(t):
        return t.reshape(t.shape[:-1] + (B_HEADS, HEAD_DIM))

    kk = hv(k * k_k)
    kk = kk / jnp.maximum(jnp.sqrt(jnp.sum(kk * kk, axis=-1, keepdims=True)), 1e-12)
    k_dir = hv(k[:, :, None] * (1.0 + (a - 1.0) * k_a))
    a = hv(a)
    decay = hv(decay)
    r = hv(r)
    v = hv(v)

    def dirs(t_f, t_b):
        t = jnp.stack([t_f, jnp.flip(t_b, axis=1)], axis=0)
        return t.reshape(2 * b, n, B_HEADS, HEAD_DIM).swapaxes(0, 1)

    y = wkv7_scan(dirs(r, r), dirs(decay[:, :, 0], decay[:, :, 1]),
                  dirs(k_dir[:, :, 0], k_dir[:, :, 1]), dirs(v, v), dirs(-kk, -kk),
                  dirs(kk * a[:, :, 0], kk * a[:, :, 1]))
    y = y.swapaxes(0, 1).reshape(2, b, n, B_HEADS, HEAD_DIM)
    y = y[0] + jnp.flip(y[1], axis=1)
    mean = jnp.mean(y, axis=-1, keepdims=True)
    var = jnp.mean(jnp.square(y - mean), axis=-1, keepdims=True)
    yn = (y - mean) * lax.rsqrt(var + RW_GN_EPS) * hv(ln_w) + hv(ln_b)
    bonus = jnp.sum(r[:, :, None] * k_dir * r_k, axis=-1, keepdims=True) * v[:, :, None]
    return (yn + jnp.sum(bonus, axis=2)).reshape(b, n, B_W) * g


def neighborhood_attention(q, k, v, rpb):
    b, n, h, hd = q.shape
    rows = n // GRID_W
    kh = min(NA_KH, rows)
    ncb = GRID_W // NA_QCOLS
    span_c = NA_QCOLS + NA_KW

    def grid(t):
        return t.reshape(b, rows, GRID_W, h, hd).transpose(0, 3, 1, 2, 4)

    qg = grid(q).reshape(b, h, rows, ncb, NA_QCOLS, hd)
    qrow = jnp.arange(rows)
    krow = jnp.clip(qrow - NA_KH // 2, 0, rows - kh)[:, None] + jnp.arange(kh)[None, :]
    kcol = (jnp.clip(jnp.arange(ncb) * NA_QCOLS - NA_KW // 2, 0, GRID_W - span_c)[:, None]
            + jnp.arange(span_c)[None, :])
    ridx = krow[:, None, :, None]
    cidx = kcol[None, :, None, :]
    kb = grid(k)[:, :, ridx, cidx]
    vb = grid(v)[:, :, ridx, cidx]
    qcol = jnp.arange(ncb)[:, None] * NA_QCOLS + jnp.arange(NA_QCOLS)[None, :]
    qc0 = jnp.clip(qcol - NA_KW // 2, 0, GRID_W - NA_KW)
    cmask = (kcol[:, None, :] >= qc0[..., None]) & (kcol[:, None, :] < qc0[..., None] + NA_KW)
    ri = (krow - qrow[:, None]) + NA_KH - 1
    ci = jnp.clip(kcol[:, None, :] - qcol[..., None], -(NA_KW - 1), NA_KW - 1) + NA_KW - 1
    bias = rpb.astype(jnp.float32)[:, ri[:, None, None, :, None], ci[None, :, :, None, :]]
    s = jnp.einsum('bhrcqd,bhrckwd->bhrcqkw', qg, kb, preferred_element_type=jnp.float32) * (hd ** -0.5)
    s = jnp.where(cmask[:, :, None, :], s + bias[None], -jnp.inf)
    p = jax.nn.softmax(s.reshape(s.shape[:-2] + (kh * span_c,)), axis=-1).reshape(s.shape)
    o = jnp.einsum('bhrcqkw,bhrckwd->bhrcqd', p, vb.astype(jnp.float32))
    return o.reshape(b, h, rows, GRID_W, hd).transpose(0, 2, 3, 1, 4).reshape(b, n, h, hd)


def mix_ab(h, pos, w_in, w_out, mu_prev, mu_next, w0, w2, a0, a2, g2, k_k, k_a, r_k, ln_w, ln_b):
    b, n, _ = h.shape
    z = h @ w_in
    qa, ka, va, zb = jnp.split(z, (A_W, 2 * A_W, 3 * A_W), axis=-1)
    heads = lambda t: t.reshape(b, n, -1, HEAD_DIM)
    oa = dilated_attention(rope(heads(qa), pos), rope(heads(ka), pos), heads(va)).reshape(b, n, A_W)
    ob = rwkv7_bidir(zb, mu_prev, mu_next, w0, w2, a0, a2, g2, k_k, k_a, r_k, ln_w, ln_b)
    return jnp.concatenate([oa, ob], axis=-1).astype(h.dtype) @ w_out


def mix_cd(h, pos, w_in, w_out, sink, rpb):
    b, n, _ = h.shape
    z = h @ w_in
    qc, kc, vc, qd, kd, vd = jnp.split(z, CD_SPLITS, axis=-1)
    heads = lambda t: t.reshape(b, n, -1, HEAD_DIM)
    qc = rope(heads(qc), pos).reshape(b, n, C_KV_HEADS, C_GROUP, HEAD_DIM).transpose(0, 2, 3, 1, 4)
    kc = rope(heads(kc), pos).transpose(0, 2, 1, 3)
    vc = heads(vc).transpose(0, 2, 1, 3)
    oc, _ = banded_attention(qc, kc, vc, C_HALF, C_BLOCK, sink)
    oc = oc.transpose(0, 3, 1, 2, 4).reshape(b, n, C_W)
    od = neighborhood_attention(heads(qd), heads(kd), heads(vd), rpb).reshape(b, n, D_W)
    return jnp.concatenate([oc, od], axis=-1).astype(h.dtype) @ w_out


def setup_inputs(seed: int = 0) -> dict:
    key = jax.random.key(seed)
    ks = iter(jax.random.split(key, 40))
    n_ab = (DEPTH + 1) // 2
    n_cd = DEPTH // 2
    D, F = D_MODEL, FFN_HIDDEN

    def nrm(shape, scale):
        return scale * jax.random.normal(next(ks), shape, jnp.float32)

    def gain(shape):
        return 1.0 + nrm(shape, 0.02)

    def unif(shape, lo, hi):
        return jax.random.uniform(next(ks), shape, jnp.float32, minval=lo, maxval=hi)

    return {
        'x': nrm((BATCH, SEQ, D), 1.0),
        'p': nrm((DEPTH, BATCH, SEQ, PLE_DIM), 1.0),
        'ffn1_norm': gain((DEPTH, D)),
        'ffn1_w1': nrm((DEPTH, D, F), D ** -0.5),
        'ffn1_w3': nrm((DEPTH, D, F), D ** -0.5),
        'ffn1_w2': nrm((DEPTH, F, D), F ** -0.5),
        'mix_norm': gain((DEPTH, D)),
        'ffn2_norm': gain((DEPTH, D)),
        'ffn2_w1': nrm((DEPTH, D, F), D ** -0.5),
        'ffn2_w3': nrm((DEPTH, D, F), D ** -0.5),
        'ffn2_w2': nrm((DEPTH, F, D), F ** -0.5),
        'ple_norm': gain((DEPTH, D)),
        'ple_w_gate': nrm((DEPTH, D, D), D ** -0.5),
        'ple_w_proj': nrm((DEPTH, PLE_DIM, D), PLE_DIM ** -0.5),
        'ab_w_in': nrm((n_ab, D, AB_COLS), D ** -0.5),
        'ab_w_out': nrm((n_ab, MIX_W, D), MIX_W ** -0.5),
        'rw_mu_prev': unif((n_ab, RW_COLS), 0.0, 0.5),
        'rw_mu_next': unif((n_ab, RW_COLS), 0.0, 0.5),
        'rw_w0': unif((n_ab, 2, B_W), -4.0, 0.0),
        'rw_w2': nrm((n_ab, 2, RW_DECAY_LORA, B_W), RW_DECAY_LORA ** -0.5),
        'rw_a0': nrm((n_ab, 2, B_W), 0.1),
        'rw_a2': nrm((n_ab, 2, RW_AAA_LORA, B_W), RW_AAA_LORA ** -0.5),
        'rw_g2': nrm((n_ab, RW_GATE_LORA, B_W), RW_GATE_LORA ** -0.5),
        'rw_k_k': 0.85 + nrm((n_ab, B_W), 0.02),
        'rw_k_a': gain((n_ab, B_W)),
        'rw_r_k': nrm((n_ab, B_HEADS, HEAD_DIM), 0.1),
        'rw_ln_w': gain((n_ab, B_W)),
        'rw_ln_b': nrm((n_ab, B_W), 0.02),
        'cd_w_in': nrm((n_cd, D, CD_COLS), D ** -0.5),
        'cd_w_out': nrm((n_cd, MIX_W, D), MIX_W ** -0.5),
        'c_sink': nrm((n_cd, C_KV_HEADS, C_GROUP), 0.5),
        'd_rpb': nrm((n_cd, D_HEADS, 2 * NA_KH - 1, 2 * NA_KW - 1), 0.1),
        'final_norm': gain((D,)),
    }


def reference(x, p, ffn1_norm, ffn1_w1, ffn1_w3, ffn1_w2, mix_norm, ffn2_norm, ffn2_w1, ffn2_w3,
              ffn2_w2, ple_norm, ple_w_gate, ple_w_proj, ab_w_in, ab_w_out, rw_mu_prev, rw_mu_next,
              rw_w0, rw_w2, rw_a0, rw_a2, rw_g2, rw_k_k, rw_k_a, rw_r_k, rw_ln_w, rw_ln_b,
              cd_w_in, cd_w_out, c_sink, d_rpb, final_norm):
    pos = jnp.arange(x.shape[1])
    for i in range(DEPTH):
        j = i // 2
        x = x + 0.5 * swiglu(rms_norm(x, ffn1_norm[i]), ffn1_w1[i], ffn1_w3[i], ffn1_w2[i])
        h = rms_norm(x, mix_norm[i])
        if i % 2 == 0:
            x = x + mix_ab(h, pos, ab_w_in[j], ab_w_out[j], rw_mu_prev[j], rw_mu_next[j], rw_w0[j],
                           rw_w2[j], rw_a0[j], rw_a2[j], rw_g2[j], rw_k_k[j], rw_k_a[j], rw_r_k[j],
                           rw_ln_w[j], rw_ln_b[j])
        else:
            x = x + mix_cd(h, pos, cd_w_in[j], cd_w_out[j], c_sink[j], d_rpb[j])
        x = x + 0.5 * swiglu(rms_norm(x, ffn2_norm[i]), ffn2_w1[i], ffn2_w3[i], ffn2_w2[i])
        gate = jax.nn.sigmoid(rms_norm(x, ple_norm[i]) @ ple_w_gate[i])
        x = x + (p[i] @ ple_w_proj[i]) * gate
    return rms_norm(x, final_norm)
```

```python
import numpy as np
import ml_dtypes
import concourse.bass as bass
import concourse.mybir as mybir
from concourse.bass_utils import run_bass_kernel_spmd

F32 = mybir.dt.float32
BF16 = mybir.dt.bfloat16
AF = mybir.ActivationFunctionType
ALU = mybir.AluOpType
AX = mybir.AxisListType

D_MODEL = 2048
FFN_HIDDEN = 5632
NORM_EPS = 1e-6
NCORES = 8


_DRAM_REG = {}
_UNIQ = [0]


def uniq(name):
    _UNIQ[0] += 1
    return "%s_u%d" % (name, _UNIQ[0])


def dram_reg(nc, name, shape, dt, kind):
    key = (id(nc), name)
    if key not in _DRAM_REG:
        if kind == "Internal":
            _DRAM_REG[key] = nc.dram_tensor(name, list(shape), dt).ap()
        else:
            _DRAM_REG[key] = nc.dram_tensor(name, list(shape), dt, kind=kind).ap()
    return _DRAM_REG[key]


class Prog:
    ENGS = ("pe", "act", "dve", "pool", "sp")

    def __init__(self, nc):
        self.nc = nc
        self.streams = {e: [] for e in self.ENGS}
        self.cnt = {}
        self.waited = {}
        self.last_w = {}
        self.readers = {}
        self.dma_slots = []

    def _deps(self, eng, r, w):
        evs = []
        for k in list(r) + list(w):
            ev = self.last_w.get(k)
            if ev is not None:
                evs.append(ev)
        for k in w:
            evs.extend(self.readers.get(k, ()))
        need = {}
        for (sk, c) in evs:
            if sk == "pe" and eng == "pe":
                continue
            if c > need.get(sk, 0):
                need[sk] = c
        for sk, c in need.items():
            if self.waited.get((eng, sk), 0) < c:
                self.waited[(eng, sk)] = c
                self.streams[eng].append(("wait", sk, c))

    def _commit(self, ev, r, w):
        for k in r:
            self.readers.setdefault(k, []).append(ev)
        for k in w:
            self.last_w[k] = ev
            self.readers[k] = []

    def op(self, eng, fns, r=(), w=()):
        if not isinstance(fns, (list, tuple)):
            fns = [fns]
        self._deps(eng, r, w)
        self.cnt[eng] = self.cnt.get(eng, 0) + 1
        ev = (eng, self.cnt[eng])
        self.streams[eng].append(("op", list(fns), eng, 1))
        self._commit(ev, r, w)
        return ev

    def dma(self, q, slot, out, in_, r=(), w=(), slow=False):
        sk = "dma_" + slot
        if sk not in self.cnt:
            self.cnt[sk] = 0
            self.dma_slots.append(sk)
        self._deps(q, r, w)
        self.cnt[sk] += 16
        ev = (sk, self.cnt[sk])
        if slow:
            self.streams[q].append(("op", [lambda e, out=out, in_=in_: e.dma_start(out=out, in_=in_, allow_slow_non_contiguous=True)], sk, 16))
        else:
            self.streams[q].append(("op", [lambda e, out=out, in_=in_: e.dma_start(out=out, in_=in_)], sk, 16))
        self._commit(ev, r, w)
        return ev

    def finish(self, out_events, eng="sp"):
        for (sk, c) in out_events:
            if self.waited.get((eng, sk), 0) < c:
                self.waited[(eng, sk)] = c
                self.streams[eng].append(("wait", sk, c))

    def barrier(self):
        for eng in self.ENGS:
            for sk, c in self.cnt.items():
                if c > 0 and sk != eng and self.waited.get((eng, sk), 0) < c:
                    self.waited[(eng, sk)] = c
                    self.streams[eng].append(("wait", sk, c))

    def cc(self, kind, groups, src, dst, R, CR):
        self.barrier()
        for c0 in range(0, R, CR):
            n = min(CR, R - c0)
            d0 = (c0 // CR) * 2 * CR
            self.cnt["cc"] = self.cnt.get("cc", 0) + 1
            k = self.cnt["cc"]
            self.streams["pool"].append(("op", [lambda e, c0=c0, n=n, d0=d0: e.collective_compute(kind, ALU.bypass, replica_groups=groups, ins=[src[c0:c0 + n, :]], outs=[dst[d0:d0 + 2 * n, :]])], "cc", 1))
            self.streams["pool"].append(("wait", "cc", k))
            self.waited[("pool", "cc")] = k
        self.barrier()

    def cc_chunk(self, kind, groups, src, dst, c0, n, CR, wait_slots):
        for sk in wait_slots:
            c = self.cnt.get(sk, 0)
            if c > self.waited.get(("pool", sk), 0):
                self.waited[("pool", sk)] = c
                self.streams["pool"].append(("wait", sk, c))
        d0 = (c0 // CR) * 2 * CR
        self.cnt["cc"] = self.cnt.get("cc", 0) + 1
        k = self.cnt["cc"]
        self.streams["pool"].append(("op", [lambda e: e.collective_compute(kind, ALU.bypass, replica_groups=groups, ins=[src[c0:c0 + n, :]], outs=[dst[d0:d0 + 2 * n, :]])], "cc", 1))
        self.streams["pool"].append(("wait", "cc", k))
        self.waited[("pool", "cc")] = k

    def emit(self):
        nc = self.nc
        import contextlib
        if not hasattr(self, "semstack"):
            self.semstack = contextlib.ExitStack()
            self.sems = {}
        for sk in list(self.cnt.keys()):
            if sk not in self.sems:
                self.sems[sk] = self.semstack.enter_context(nc.semaphore("s_" + sk))
        sems = self.sems
        with nc.Block() as block:
            def run(stream):
                def f(e):
                    for it in stream:
                        if it[0] == "wait":
                            e.wait_ge(sems[it[1]], it[2])
                        else:
                            _, fns, sk, inc = it
                            ins = None
                            for fn in fns:
                                ins = fn(e)
                            ins.then_inc(sems[sk], inc)
                return f

            block.tensor(run(self.streams["pe"]))
            block.scalar(run(self.streams["act"]))
            block.vector(run(self.streams["dve"]))
            block.gpsimd(run(self.streams["pool"]))
            block.sync(run(self.streams["sp"]))
        self.streams = {e: [] for e in self.ENGS}


class TokKernel:
    def __init__(self, ntok=1024, d=D_MODEL, nc=None, P=None):
        import contextlib
        self.shared = nc is not None
        self.nc = nc if self.shared else bass.Bass("TRN2", target_bir_lowering=False)
        self.P = P if self.shared else Prog(self.nc)
        self.st = contextlib.ExitStack()
        self.ntok = ntok
        self.d = d
        self.kc = d // 128
        self.ntc = ntok // 512
        self.uid = 0
        nc = self.nc
        self.xs = self.sb("xs", [128, self.kc, ntok], F32)
        self.hT = self.sb("hT", [128, self.kc, ntok], BF16)
        self.ones = self.sb("ones", [128, 128], F32)
        self.P.op("pool", lambda e: e.memset(self.ones[:], 1.0), w=["ones"])
        self.ones_r = self.sb("ones_r", [128, 128], mybir.dt.float32r)
        self.P.op("dve", lambda e: e.tensor_copy(out=self.ones_r[:], in_=self.ones[:]), r=["ones"], w=["ones_r"])
        self.epsT = self.sb("epsT", [128, 1], F32)
        self.P.op("pool", lambda e: e.memset(self.epsT[:], NORM_EPS), w=["eps"])
        self.sq = [self.sb("sq%d" % i, [128, 512], mybir.dt.float32r) for i in range(2)]

        self.sq_i = 0
        self.rstd = self.sb("rstd", [128, 512], F32)
        self.ps_ss = self.ps("ps_ss", [128, 512], F32)
        self.pab = [(self.ps("pa%d" % i, [128, 512], F32), self.ps("pb%d" % i, [128, 512], F32)) for i in range(2)]
        self.pab_i = 0
        self.py = [self.ps("py%d" % i, [128, 512], F32) for i in range(2)]
        self.py_i = 0
        self.wst = [self.sb("wst%d" % i, [128, 16, 128], F32) for i in range(4)]
        self.wbf = [self.sb("wbf%d" % i, [128, 16, 128], BF16) for i in range(4)]
        self.w_i = 0
        self.dmaq_i = 0
        self.outs = []
        self.fpass = 11
        self.gT = self.sb("gT", [128, self.fpass, ntok], BF16)
        self.sg = [self.sb("sg%d" % i, [128, 512], F32) for i in range(2)]
        self.sgi = 0
        self.ost = [self.sb("ost%d" % i, [128, ntok], F32) for i in range(2)]
        self.ost_i = 0
        self.pTb = self.sb("pTb", [128, 2, ntok], BF16)

    def sb(self, name, shape, dt):
        return self.st.enter_context(self.nc.sbuf_tensor(uniq(name), shape, dt))

    def ps(self, name, shape, dt):
        return self.st.enter_context(self.nc.psum_tensor(uniq(name), shape, dt))

    def dram_in(self, name, shape, dt=F32):
        return dram_reg(self.nc, name, shape, dt, "ExternalInput")

    def dram_out(self, name, shape, dt=F32):
        return dram_reg(self.nc, name, shape, dt, "ExternalOutput")

    def dq(self):
        self.dmaq_i += 1
        return ("sp", "pool")[self.dmaq_i % 2]

    def load_x(self, x_ap):
        P = self.P
        for kc in range(self.kc):
            P.dma("sp", "x%d" % (kc % 4), self.xs[:, kc, :], x_ap[kc * 128:(kc + 1) * 128, :], w=[("x", kc, tc) for tc in range(self.ntc)])
        for kc in range(self.kc):
            sk = "dma_x%d" % (kc % 4)
            for tc in range(self.ntc):
                P.last_w[("x", kc, tc)] = (sk, P.cnt[sk])

    def store_x(self, out_ap):
        P = self.P
        for kc in range(self.kc):
            ev = P.dma("sp", "xo%d" % (kc % 4), out_ap[kc * 128:(kc + 1) * 128, :], self.xs[:, kc, :], r=[("x", kc, tc) for tc in range(self.ntc)])
            self.outs.append(ev)

    def load_vec(self, name, ap_pk):
        t = self.sb(name, [128, ap_pk.shape[1]], F32)
        self.P.dma("sp", "vec_" + name, t[:], ap_pk, w=[name])
        return t

    def rmsnorm(self, gname, gt, out_ap=None):
        P = self.P
        KC = self.kc
        for tc in range(self.ntc):
            ts = slice(tc * 512, (tc + 1) * 512)
            for kc in range(KC):
                i = self.sq_i = (self.sq_i + 1) % 2
                sq = self.sq[i]
                P.op("act", lambda e, sq=sq, kc=kc, ts=ts: e.activation(out=sq[:], in_=self.xs[:, kc, ts], func=AF.Square),
                     r=[("x", kc, tc)], w=[("sq", i)])
                P.op("pe", lambda e, sq=sq, kc=kc: e.matmul(self.ps_ss[:], lhsT=self.ones_r[:], rhs=sq[:], start=(kc == 0), stop=(kc == KC - 1)),
                     r=[("sq", i), "ones_r"], w=["ps_ss"])
            P.op("act", lambda e: e.activation(out=self.rstd[:], in_=self.ps_ss[:], func=AF.Sqrt, bias=self.epsT[:, 0:1], scale=1.0 / self.d),
                 r=["ps_ss", "eps"], w=["rstd"])
            P.op("dve", lambda e: e.reciprocal(out=self.rstd[:], in_=self.rstd[:]), r=["rstd"], w=["rstd"])
            for kc in range(KC):
                if out_ap is None:
                    P.op("dve", lambda e, kc=kc, ts=ts: e.scalar_tensor_tensor(out=self.hT[:, kc, ts], in0=self.xs[:, kc, ts], scalar=gt[:, kc:kc + 1], in1=self.rstd[:], op0=ALU.mult, op1=ALU.mult),
                         r=[("x", kc, tc), "rstd", gname], w=[("h", kc, tc)])
                else:
                    P.op("dve", lambda e, kc=kc, ts=ts: e.scalar_tensor_tensor(out=self.xs[:, kc, ts], in0=self.xs[:, kc, ts], scalar=gt[:, kc:kc + 1], in1=self.rstd[:], op0=ALU.mult, op1=ALU.mult),
                         r=[("x", kc, tc), "rstd", gname], w=[("x", kc, tc)])
        if out_ap is not None:
            self.store_x(out_ap)

    def load_w(self, src, nk, ncols, q=None):
        P = self.P
        i = self.w_i = (self.w_i + 1) % 4
        P.dma(q or self.dq(), "w%d" % i, self.wst[i][:, 0:nk, 0:ncols], src, w=[("wst", i)])
        if i % 2 == 0:
            P.op("dve", lambda e, i=i, nk=nk, ncols=ncols: e.tensor_copy(out=self.wbf[i][:, 0:nk, 0:ncols], in_=self.wst[i][:, 0:nk, 0:ncols]),
                 r=[("wst", i)], w=[("wbf", i)])
        else:
            P.op("act", lambda e, i=i, nk=nk, ncols=ncols: e.activation(out=self.wbf[i][:, 0:nk, 0:ncols], in_=self.wst[i][:, 0:nk, 0:ncols], func=AF.Identity),
                 r=[("wst", i)], w=[("wbf", i)])
        return self.wbf[i], ("wbf", i)

    def ffn(self, w1, w3, w2, f=FFN_HIDDEN, fpass=11):
        P = self.P
        KC = self.kc
        nfc = f // 128
        npass = nfc // fpass
        assert npass * fpass == nfc
        gT = self.gT
        sg = self.sg
        sgi = self.sgi
        for q in range(npass):
            for fc in range(fpass):
                j = q * fpass + fc
                w1t, w1k = self.load_w(w1[j], KC, 128)
                w3t, w3k = self.load_w(w3[j], KC, 128)
                for tc in range(self.ntc):
                    ts = slice(tc * 512, (tc + 1) * 512)
                    pi = self.pab_i = (self.pab_i + 1) % 2
                    pa, pb = self.pab[pi]
                    P.op("pe", [lambda e, kc=kc, pa=pa, w1t=w1t, ts=ts: e.matmul(pa[:], lhsT=w1t[:, kc, :], rhs=self.hT[:, kc, ts], start=(kc == 0), stop=(kc == KC - 1)) for kc in range(KC)],
                         r=[w1k] + [("h", kc, tc) for kc in range(KC)], w=[("pa", pi)])
                    P.op("pe", [lambda e, kc=kc, pb=pb, w3t=w3t, ts=ts: e.matmul(pb[:], lhsT=w3t[:, kc, :], rhs=self.hT[:, kc, ts], start=(kc == 0), stop=(kc == KC - 1)) for kc in range(KC)],
                         r=[w3k] + [("h", kc, tc) for kc in range(KC)], w=[("pb", pi)])
                    sgi = self.sgi = (self.sgi + 1) % 2
                    s = sg[sgi]
                    P.op("act", lambda e, s=s, pa=pa: e.activation(out=s[:], in_=pa[:], func=AF.Silu), r=[("pa", pi)], w=[("sg", sgi)])
                    P.op("dve", lambda e, s=s, pb=pb, fc=fc, ts=ts: e.tensor_tensor(out=gT[:, fc, ts], in0=s[:], in1=pb[:], op=ALU.mult),
                         r=[("sg", sgi), ("pb", pi)], w=[("g", fc, tc)])
            for dc in range(KC):
                w2t, w2k = self.load_w(w2[q, dc], fpass, 128)
                for tc in range(self.ntc):
                    ts = slice(tc * 512, (tc + 1) * 512)
                    yi = self.py_i = (self.py_i + 1) % 2
                    py = self.py[yi]
                    P.op("pe", [lambda e, fc=fc, py=py, w2t=w2t, ts=ts: e.matmul(py[:], lhsT=w2t[:, fc, :], rhs=gT[:, fc, ts], start=(fc == 0), stop=(fc == fpass - 1)) for fc in range(fpass)],
                         r=[w2k] + [("g", fc, tc) for fc in range(fpass)], w=[("py", yi)])
                    P.op("dve", lambda e, py=py, dc=dc, ts=ts: e.scalar_tensor_tensor(out=self.xs[:, dc, ts], in0=py[:], scalar=0.5, in1=self.xs[:, dc, ts], op0=ALU.mult, op1=ALU.add),
                         r=[("py", yi), ("x", dc, tc)], w=[("x", dc, tc)])

    def load_actT(self, ap, dst, nk, keyf):
        P = self.P
        for kc in range(nk):
            i = self.ost_i = (self.ost_i + 1) % 2
            P.dma(self.dq(), "ost%d" % i, self.ost[i][:], ap[kc * 128:(kc + 1) * 128, :], w=[("ost", i)])
            P.op("pool", lambda e, i=i, kc=kc: e.tensor_copy(out=dst[:, kc, :], in_=self.ost[i][:]),
                 r=[("ost", i)], w=[keyf(kc, tc) for tc in range(self.ntc)])

    def proj_out(self, w_ap, z_ap, n, cc=None):
        P = self.P
        KC = self.kc
        c0 = 0
        while c0 < n:
            ncols = min(128, n - c0)
            wt, wk = self.load_w(w_ap[c0 // 128], KC, ncols, q=("sp" if cc is not None else None))
            for tc in range(self.ntc):
                ts = slice(tc * 512, (tc + 1) * 512)
                yi = self.py_i = (self.py_i + 1) % 2
                py = self.py[yi]
                P.op("pe", [lambda e, kc=kc, py=py, wt=wt, ts=ts, ncols=ncols: e.matmul(py[0:ncols, :], lhsT=wt[:, kc, 0:ncols], rhs=self.hT[:, kc, ts], start=(kc == 0), stop=(kc == KC - 1)) for kc in range(KC)],
                     r=[wk] + [("h", kc, tc) for kc in range(KC)], w=[("py", yi)])
                si = self.sgi = (self.sgi + 1) % 2
                sgt = self.sg[si]
                P.op("act", lambda e, sgt=sgt, py=py, ncols=ncols: e.activation(out=sgt[0:ncols, :], in_=py[0:ncols, :], func=AF.Identity),
                     r=[("py", yi)], w=[("sg", si)])
                ev = P.dma("sp", "zo%d" % si, z_ap[c0:c0 + ncols, ts], sgt[0:ncols, :], r=[("sg", si)])
                self.outs.append(ev)
            c0 += ncols
            if cc is not None and (c0 % cc[1] == 0 or c0 >= n):
                r0 = ((c0 - 1) // cc[1]) * cc[1]
                P.cc_chunk("AllGather", PAIRS, z_ap, cc[0], r0, c0 - r0, cc[1], ["dma_zo0", "dma_zo1"])

    def addmm(self, w_ap, oT_ap, k):
        self.load_actT(oT_ap, self.hT, k // 128, lambda kc, tc: ("h", kc, tc))
        self.addmm_noload(w_ap, k)

    def addmm_noload(self, w_ap, k):
        P = self.P
        nk = k // 128
        for dc in range(self.kc):
            wt, wk = self.load_w(w_ap[dc], nk, 128)
            for tc in range(self.ntc):
                ts = slice(tc * 512, (tc + 1) * 512)
                yi = self.py_i = (self.py_i + 1) % 2
                py = self.py[yi]
                P.op("pe", [lambda e, kc=kc, py=py, wt=wt, ts=ts: e.matmul(py[:], lhsT=wt[:, kc, :], rhs=self.hT[:, kc, ts], start=(kc == 0), stop=(kc == nk - 1)) for kc in range(nk)],
                     r=[wk] + [("h", kc, tc) for kc in range(nk)], w=[("py", yi)])
                P.op("dve", lambda e, py=py, dc=dc, ts=ts: e.tensor_tensor(out=self.xs[:, dc, ts], in0=py[:], in1=self.xs[:, dc, ts], op=ALU.add),
                     r=[("py", yi), ("x", dc, tc)], w=[("x", dc, tc)])

    def ple(self, wg_ap, wp_ap, pT_ap):
        P = self.P
        KC = self.kc
        self.load_actT(pT_ap, self.pTb, 2, lambda kc, tc: ("pT", kc, tc))
        for dc in range(KC):
            wgt, wgk = self.load_w(wg_ap[dc], KC, 128)
            wpt, wpk = self.load_w(wp_ap[dc], 2, 128)
            for tc in range(self.ntc):
                ts = slice(tc * 512, (tc + 1) * 512)
                pi = self.pab_i = (self.pab_i + 1) % 2
                pa, pb = self.pab[pi]
                P.op("pe", [lambda e, kc=kc, pa=pa, wgt=wgt, ts=ts: e.matmul(pa[:], lhsT=wgt[:, kc, :], rhs=self.hT[:, kc, ts], start=(kc == 0), stop=(kc == KC - 1)) for kc in range(KC)],
                     r=[wgk] + [("h", kc, tc) for kc in range(KC)], w=[("pa", pi)])
                P.op("pe", [lambda e, kc=kc, pb=pb, wpt=wpt, ts=ts: e.matmul(pb[:], lhsT=wpt[:, kc, :], rhs=self.pTb[:, kc, ts], start=(kc == 0), stop=(kc == 1)) for kc in range(2)],
                     r=[wpk] + [("pT", kc, tc) for kc in range(2)], w=[("pb", pi)])
                si = self.sgi = (self.sgi + 1) % 2
                sgt = self.sg[si]
                P.op("act", lambda e, sgt=sgt, pa=pa: e.activation(out=sgt[:], in_=pa[:], func=AF.Sigmoid), r=[("pa", pi)], w=[("sg", si)])
                P.op("dve", lambda e, sgt=sgt, pb=pb: e.tensor_tensor(out=sgt[:], in0=sgt[:], in1=pb[:], op=ALU.mult),
                     r=[("sg", si), ("pb", pi)], w=[("sg", si)])
                P.op("dve", lambda e, sgt=sgt, dc=dc, ts=ts: e.tensor_tensor(out=self.xs[:, dc, ts], in0=sgt[:], in1=self.xs[:, dc, ts], op=ALU.add),
                     r=[("sg", si), ("x", dc, tc)], w=[("x", dc, tc)])

    def done(self):
        if self.shared:
            self.P.barrier()
        else:
            self.P.finish(self.outs)
        self.P.emit()
        self.st.close()
        return self.nc


SEQ = 2048
HD = 64


class MixKernel:
    def __init__(self, nc=None, P=None):
        import contextlib
        self.shared = nc is not None
        self.nc = nc if self.shared else bass.Bass("TRN2", target_bir_lowering=False)
        self.P = P if self.shared else Prog(self.nc)
        self.st = contextlib.ExitStack()
        self.outs = []
        self.dmaq_i = 0
        self.rot = {}

    def sb(self, name, shape, dt):
        return self.st.enter_context(self.nc.sbuf_tensor(uniq(name), shape, dt))

    def ps(self, name, shape, dt):
        return self.st.enter_context(self.nc.psum_tensor(uniq(name), shape, dt))

    def dram_in(self, name, shape, dt=F32):
        return dram_reg(self.nc, name, shape, dt, "ExternalInput")

    def dram_out(self, name, shape, dt=F32):
        return dram_reg(self.nc, name, shape, dt, "ExternalOutput")

    def dq(self):
        return "sp"

    def nxt(self, name, n):
        i = self.rot[name] = (self.rot.get(name, -1) + 1) % n
        return i

    def record(self, fn):
        P = self.P
        rec = []
        P.op = lambda *a, **k: rec.append((Prog.op, a, k))
        P.dma = lambda *a, **k: rec.append((Prog.dma, a, k))
        try:
            fn()
        finally:
            del P.op
            del P.dma
        return rec

    def set_drip(self, rec, ntiles):
        self.pending = rec
        self.drip_n = max(1, -(-len(rec) // max(1, ntiles)))

    def drip(self):
        for _ in range(getattr(self, "drip_n", 0)):
            if getattr(self, "pending", None):
                m, a, k = self.pending.pop(0)
                m(self.P, *a, **k)

    def flush(self):
        while getattr(self, "pending", None):
            m, a, k = self.pending.pop(0)
            m(self.P, *a, **k)

    def attn_setup(self, cos_ap, sin_ap, ident_ap=None):
        P = self.P
        T = SEQ
        if ident_ap is not None:
            self.ident = self.sb("identA", [128, 128], F32)
            P.dma("sp", "cst", self.ident[:], ident_ap, w=["ident"])
            self.vT = [self.sb("vT%d" % i, [64, T], F32) for i in range(2)]
            self.stg = [self.sb("stgA%d" % i, [128, 512], F32) for i in range(2)]
        self._cst_fix = True
        self.cosT = self.sb("cosT", [64, T], F32)
        self.sinT = self.sb("sinT", [64, T], F32)
        P.dma("sp", "cst", self.cosT[:], cos_ap, w=["cos"])
        P.dma("sp", "cst", self.sinT[:], sin_ap, w=["sin"])
        for kname in ("cos", "sin", "ident"):
            if kname in P.last_w:
                P.last_w[kname] = ("dma_cst", P.cnt["dma_cst"])
        self.ones65 = self.sb("ones65", [128, 65], F32)
        P.op("pool", lambda e: e.memset(self.ones65[:], 1.0), w=["ones65"])
        P.op("dve", lambda e: e.tensor_copy(out=self.ones65r[:], in_=self.ones65[:, 0:1]), r=["ones65"], w=["ones65r"])
        self.ld = [[self.sb("ld%d_%d" % (i, j), [64, T], F32) for j in range(2)] for i in range(2)]
        F32R = mybir.dt.float32r
        self.Qa = [self.sb("Qa%d" % i, [65, T], F32R) for i in range(2)]
        self.Ka = [self.sb("Ka%d" % i, [65, T], F32R) for i in range(2)]
        self.ones65r = self.sb("ones65r", [128, 1], F32R)
        self.crow = [self.sb("crow%d" % i, [65, T], F32) for i in range(2)]
        self._ones65r_pending = True
        self.tmp = self.sb("ropetmp", [64, T], F32)
        self.nrm = self.sb("nrm", [65, T], F32)
        self.kmax = self.sb("kmax", [65, 1], F32)
        self.vld = [self.sb("vld%d" % i, [128, 16, 64], F32) for i in range(2)]
        self.Vb = [self.sb("Vb%d" % i, [128, 16, 65], BF16) for i in range(2)]
        self.Vb2 = None
        self.pt = [self.sb("pt%d" % i, [128, 512], BF16) for i in range(4)]
        self.pS = [self.ps("pS%d" % i, [128, 512], F32) for i in range(3)]
        self.pO = [self.ps("pO%d" % i, [128, 4, 128], F32) for i in range(2)]
        self.pN = self.ps("pN", [128, 512], F32)
        self.rec = [self.sb("rec%d" % i, [128, 4], F32) for i in range(2)]

    def rope_aug(self, dst, dkey, raw_ap, rot_ap, is_k):
        P = self.P
        i = self.nxt("ld", 2)
        a, b = self.ld[i]
        P.dma(self.dq(), "ld%da" % i, a[:], raw_ap, w=[("ld", i, 0)])
        P.dma(self.dq(), "ld%db" % i, b[:], rot_ap, w=[("ld", i, 1)])
        P.op("dve", lambda e: e.tensor_tensor(out=dst[0:64, :], in0=a[:], in1=self.cosT[:], op=ALU.mult), r=[("ld", i, 0), "cos"], w=[dkey])
        P.op("pool", lambda e: e.tensor_tensor(out=self.tmp[:], in0=b[:], in1=self.sinT[:], op=ALU.mult), r=[("ld", i, 1), "sin"], w=["ropetmp"])
        P.op("dve", lambda e: e.tensor_tensor(out=dst[0:64, :], in0=dst[0:64, :], in1=self.tmp[:], op=ALU.add), r=[dkey, "ropetmp"], w=[dkey])

    def rope_aug2(self, dst, dkey, zm, row0):
        P = self.P
        T = SEQ
        i = self.nxt("ld", 2)
        a, b = self.ld[i]
        P.dma(self.dq(), "ld%da" % i, a[:], zm[row0:row0 + 64, 1:T + 1], w=[("ld", i, 0)])
        P.dma(self.dq(), "ld%db" % i, b[0:32, :], zm[row0 + 32:row0 + 64, 1:T + 1], w=[("ld", i, 1)])
        P.dma(self.dq(), "ld%dc" % i, b[32:64, :], zm[row0:row0 + 32, 1:T + 1], w=[("ld", i, 2)])
        P.op("dve", lambda e: e.tensor_tensor(out=dst[0:64, :], in0=a[:], in1=self.cosT[:], op=ALU.mult), r=[("ld", i, 0), "cos"], w=[dkey])
        P.op("pool", lambda e: e.tensor_tensor(out=self.tmp[:], in0=b[:], in1=self.sinT[:], op=ALU.mult), r=[("ld", i, 1), ("ld", i, 2), "sin"], w=["ropetmp"])
        P.op("dve", lambda e: e.tensor_tensor(out=dst[0:64, :], in0=dst[0:64, :], in1=self.tmp[:], op=ALU.add), r=[dkey, "ropetmp"], w=[dkey])

    def load_v_T(self, zm, row0, Vt, vkey, shift=0, nblk=16):
        P = self.P
        T = SEQ
        i = self.nxt("vT", 2)
        vT = self.vT[i]
        P.dma(self.dq(), "vT%d" % i, vT[:], zm[row0:row0 + 64, 1:T + 1], w=[("vT", i)])
        P.op("pool", lambda e: e.memset(Vt[:, :, 64:65], 1.0), w=[vkey])
        for g in range(0, nblk, 8):
            n = min(8, nblk - g)
            P.op("pe", [(lambda e, b=b, g=g: e.matmul(self.pN[:, b * 64:(b + 1) * 64], lhsT=vT[:, shift + (g + b) * 128:shift + (g + b + 1) * 128], rhs=self.ident[0:64, 0:64], start=True, stop=True)) for b in range(n)],
                 r=[("vT", i), "ident"], w=["pN"])
            P.op("act", lambda e, g=g, n=n: e.activation(out=Vt[:, g:g + n, 0:64], in_=self.pN[:, 0:n * 64].rearrange("p (b d) -> p b d", d=64), func=AF.Identity), r=[], w=[vkey, "pN"])

    def out_T(self, ost, okey, ncols, oex, row0):
        P = self.P
        for cc in range(ncols // 128):
            for g in range(4):
                P.op("pe", [(lambda e, b=b, g=g, cc=cc: e.matmul(self.pN[:, b * 128:(b + 1) * 128], lhsT=ost[:, g * 4 + b, cc * 128:(cc + 1) * 128], rhs=self.ident[:, :], start=True, stop=True)) for b in range(4)],
                     r=[okey, "ident"], w=["pN"])
                si = self.nxt("stg", 2)
                st = self.stg[si]
                P.op("act", lambda e, st=st: e.activation(out=st[:], in_=self.pN[:], func=AF.Identity), r=[], w=[("stg", si), "pN"])
                P.dma("sp", "oT%d" % si, oex[row0 + cc * 128:row0 + (cc + 1) * 128, g * 512:(g + 1) * 512], st[:], r=[("stg", si)])

    def plain_aug(self, dst, dkey, raw_ap):
        P = self.P
        P.dma(self.dq(), "pl" + str(dkey), dst[0:64, :], raw_ap, w=[dkey])

    def norms(self, src, skey):
        P = self.P
        T = SEQ
        P.op("act", lambda e: e.activation(out=self.tmp[:], in_=src[0:64, :], func=AF.Square), r=[skey], w=["ropetmp"])
        for c in range(T // 512):
            ts = slice(c * 512, (c + 1) * 512)
            P.op("pe", lambda e, ts=ts: e.matmul(self.pN[0:65, :], lhsT=self.ones65[0:64, :], rhs=self.tmp[:, ts], start=True, stop=True),
                 r=["ropetmp", "ones65"], w=["pN"])
            P.op("act", lambda e, ts=ts: e.activation(out=self.nrm[64:65, ts], in_=self.pN[64:65, :], func=AF.Sqrt), r=["pN"], w=["nrm"])

    def prep_k(self, Ka, kkey):
        P = self.P
        self.norms(Ka, kkey)
        P.op("dve", lambda e: e.tensor_reduce(out=self.kmax[64:65, 0:1], in_=self.nrm[64:65, :], axis=AX.X, op=ALU.max), r=["nrm"], w=["kmax"])
        P.op("dve", lambda e: e.tensor_scalar(out=Ka[64:65, :], in0=self.nrm[64:65, :], scalar1=0.0, scalar2=1.0, op0=ALU.mult, op1=ALU.add), r=["nrm"], w=[kkey])

    def prep_q(self, Qa, qkey):
        P = self.P
        self.norms(Qa, qkey)
        P.op("dve", lambda e: e.tensor_scalar(out=Qa[64:65, :], in0=self.nrm[64:65, :], scalar1=self.kmax[64:65, 0:1], scalar2=-1.0, op0=ALU.mult, op1=ALU.mult),
             r=["nrm", "kmax"], w=[qkey])
        cr = self.crow[qkey[1] % 2]
        P.op("act", lambda e: e.activation(out=cr[64:65, :], in_=Qa[64:65, :], func=AF.Identity), r=[qkey], w=[("crow", qkey[1] % 2)])

    def load_v(self, v_ap, shift=False):
        P = self.P
        i = self.nxt("v", 2)
        P.dma(self.dq(), "v%d" % i, self.vld[i][:], v_ap.rearrange("(b p) d -> p b d", p=128), w=[("vld", i)])
        P.op("pool", lambda e: e.memset(self.Vb[i][:, :, 64:65], 1.0), w=[("Vb", i)])
        P.op("pool", lambda e: e.tensor_copy(out=self.Vb[i][:, :, 0:64], in_=self.vld[i][:]), r=[("vld", i)], w=[("Vb", i)])
        return self.Vb[i], ("Vb", i)

    def score_group(self, segs, W_ap, wkeys, rkeys):
        P = self.P
        self.drip()
        si = self.nxt("pS", 3)
        pS = self.pS[si]
        off = 0
        fns = []
        offs = []
        for (ka, qa, nq) in segs:
            fns.append(lambda e, ka=ka, qa=qa, off=off, nq=nq: e.matmul(pS[:, off:off + nq], lhsT=ka, rhs=qa, start=True, stop=True))
            offs.append(off)
            off += nq
        P.op("pe", fns, r=rkeys, w=[("pS", si)])
        pi = self.nxt("pt", 4)
        pt = self.pt[pi]
        P.op("act", lambda e, off=off: e.activation(out=pt[:, 0:off], in_=pS[:, 0:off], func=AF.Exp, scale=0.125), r=[("pS", si)], w=[("pt", pi)])
        eng = "dve"
        P.op(eng, lambda e, off=off: e.tensor_tensor(out=pt[:, 0:off], in0=pt[:, 0:off], in1=W_ap, op=ALU.mult), r=[("pt", pi)] + wkeys, w=[("pt", pi)])
        return pt, ("pt", pi), offs

    def attn_toeplitz(self, Qa, qkey, Ka, kkey, Vb, vkey, mask, mkey, mask_off, reach, ost, okey, ocol, sink_col=None):
        P = self.P
        T = SEQ
        for qc in range(T // 512):
            q0 = qc * 512
            kbs = [kb for kb in range(T // 128) if not (kb * 128 > q0 + 511 + reach or kb * 128 + 127 < q0 - reach)]
            oi = self.nxt("pO", 2)
            pO = self.pO[oi]
            pend = []

            def emit_pv(item, pO=pO, oi=oi, kbs=kbs):
                pt, ptk, kb, n = item
                P.op("pe", [lambda e, j=j, pt=pt, kb=kb, n=n, pO=pO, kbs=kbs: e.matmul(pO[:, j, 0:65], lhsT=pt[:, j * 128:(j + 1) * 128], rhs=Vb[:, kb, :], start=(n == 0 and j == 0), stop=(n == len(kbs) - 1), skip_group_check=True) for j in range(4)],
                     r=[ptk, vkey], w=[("pO", oi)])
            for n, kb in enumerate(kbs):
                u0 = q0 - kb * 128 + mask_off
                pt, ptk, _ = self.score_group([(Ka[0:65, kb * 128:(kb + 1) * 128], Qa[0:65, q0:q0 + 512], 512)],
                                              mask[:, u0:u0 + 512], [mkey], [qkey, kkey])
                pend.append((pt, ptk, kb, n))
                if len(pend) > 2:
                    emit_pv(pend.pop(0))
            while pend:
                emit_pv(pend.pop(0))
            ri = self.nxt("rec", 2)
            rec = self.rec[ri]
            if sink_col is not None:
                P.op("pe", [lambda e, j=j, pO=pO, q0=q0: e.matmul(pO[:, j, 65:66], lhsT=self.crow[qkey[1] % 2][64:65, q0 + j * 128:q0 + (j + 1) * 128], rhs=self.ones65[64:65, 0:1], start=True, stop=True) for j in range(4)],
                     r=[("crow", qkey[1] % 2), "ones65"], w=[("pO", oi)])
                P.op("act", lambda e, rec=rec, pO=pO: e.activation(out=rec[:], in_=pO[:, :, 65], func=AF.Exp, bias=sink_col, scale=0.125), r=[("pO", oi), "sink"], w=[("rec", ri)])
                P.op("dve", lambda e, rec=rec, pO=pO: e.tensor_tensor(out=rec[:], in0=rec[:], in1=pO[:, :, 64], op=ALU.add), r=[("pO", oi), ("rec", ri)], w=[("rec", ri)])
                P.op("dve", lambda e, rec=rec: e.reciprocal(out=rec[:], in_=rec[:]), r=[("rec", ri)], w=[("rec", ri)])
            else:
                P.op("dve", lambda e, rec=rec, pO=pO: e.reciprocal(out=rec[:], in_=pO[:, :, 64]), r=[("pO", oi)], w=[("rec", ri)])
            for j in range(4):
                P.op("dve", lambda e, j=j, rec=rec, pO=pO, qc=qc: e.tensor_scalar(out=ost[:, qc * 4 + j, ocol:ocol + 64], in0=pO[:, j, 0:64], scalar1=rec[:, j:j + 1], scalar2=None, op0=ALU.mult),
                     r=[("pO", oi), ("rec", ri)], w=[okey])

    def done(self):
        if self.shared:
            self.P.barrier()
        else:
            self.P.finish(self.outs)
        self.P.emit()
        self.st.close()
        return self.nc


def rope_tables():
    half = HD // 2
    inv = 10000.0 ** (-np.arange(half, dtype=np.float64) / half)
    ang = np.arange(SEQ, dtype=np.float64)[None, :] * inv[:, None]
    cos = np.concatenate([np.cos(ang), np.cos(ang)], 0).astype(np.float32)
    sin = np.concatenate([-np.sin(ang), np.sin(ang)], 0).astype(np.float32)
    return cos, sin


def rot_half_rows(zT):
    return np.concatenate([zT[..., 32:, :], zT[..., :32, :]], axis=-2)


def toeplitz_mask(f, width=SEQ):
    off = width - 128
    u = np.arange(2 * width - 128)[None, :]
    p = np.arange(128)[:, None]
    return f(u - off - p).astype(ml_dtypes.bfloat16), off


def f_dil(d):
    a = np.abs(d)
    return ((a <= 64).astype(np.float32) + ((a % 4 == 0) & (a <= 256)) + ((a % 16 == 0) & (a <= 1024)))


def f_win(d):
    return (np.abs(d) <= 128).astype(np.float32)


def build_attn_A(nh):
    K = MixKernel()
    T = SEQ
    q = K.dram_in("q", [nh, 64, T]); qr = K.dram_in("qr", [nh, 64, T])
    k = K.dram_in("k", [nh, 64, T]); kr = K.dram_in("kr", [nh, 64, T])
    v = K.dram_in("v", [nh, T, 64])
    cos = K.dram_in("cos", [64, T]); sin = K.dram_in("sin", [64, T])
    mk = K.dram_in("mask", [128, 2 * T - 128], BF16)
    o = K.dram_out("o", [T, nh * 64])
    K.attn_setup(cos, sin)
    mask = K.sb("maskA", [128, 2 * T - 128], BF16)
    K.P.dma("sp", "mask", mask[:], mk, w=["mask"])
    ost = K.sb("ost", [128, 16, nh * 64], F32)
    for h in range(nh):
        i = h % 2
        Qa, Ka = K.Qa[i], K.Ka[i]
        K.rope_aug(Ka, ("Ka", i), k[h], kr[h], True)
        K.prep_k(Ka, ("Ka", i))
        K.rope_aug(Qa, ("Qa", i), q[h], qr[h], False)
        K.prep_q(Qa, ("Qa", i))
        Vb, vkey = K.load_v(v[h])
        K.attn_toeplitz(Qa, ("Qa", i), Ka, ("Ka", i), Vb, vkey, mask, "mask", T - 128, 1024, ost, "ost", h * 64)
    ev = K.P.dma("sp", "out", o.rearrange("(b p) c -> p b c", p=128), ost[:], r=["ost"])
    K.outs.append(ev)
    return K.done()


def build_attn_C(nq=8):
    K = MixKernel()
    T = SEQ
    q = K.dram_in("q", [nq, 64, T]); qr = K.dram_in("qr", [nq, 64, T])
    k = K.dram_in("k", [64, T]); kr = K.dram_in("kr", [64, T])
    v = K.dram_in("v", [T, 64])
    sk = K.dram_in("sink", [128, nq])
    cos = K.dram_in("cos", [64, T]); sin = K.dram_in("sin", [64, T])
    mk = K.dram_in("mask", [128, 2 * T - 128], BF16)
    o = K.dram_out("o", [T, nq * 64])
    K.attn_setup(cos, sin)
    mask = K.sb("maskC", [128, 2 * T - 128], BF16)
    K.P.dma("sp", "mask", mask[:], mk, w=["mask"])
    sink = K.sb("sinkS", [128, nq], F32)
    K.P.dma("sp", "sink", sink[:], sk, w=["sink"])
    ost = K.sb("ost", [128, 16, nq * 64], F32)
    Ka = K.Ka[0]
    K.rope_aug(Ka, ("Ka", 0), k, kr, True)
    K.prep_k(Ka, ("Ka", 0))
    Vb, vkey = K.load_v(v)
    for h in range(nq):
        i = h % 2
        Qa = K.Qa[i]
        K.rope_aug(Qa, ("Qa", i), q[h], qr[h], False)
        K.prep_q(Qa, ("Qa", i))
        K.attn_toeplitz(Qa, ("Qa", i), Ka, ("Ka", 0), Vb, vkey, mask, "mask", T - 128, 128, ost, "ost", h * 64, sink_col=sink[:, h:h + 1])
    ev = K.P.dma("sp", "out", o.rearrange("(b p) c -> p b c", p=128), ost[:], r=["ost"])
    K.outs.append(ev)
    return K.done()


GRID_W = 64
NA_ROWS = SEQ // GRID_W


def build_attn_D(nh=8):
    K = MixKernel()
    P = K.P
    T = SEQ
    q = K.dram_in("q", [nh, 64, T]); k = K.dram_in("k", [nh, 64, T])
    v = K.dram_in("v", [nh, T, 64])
    wl = K.dram_in("wlog", [nh, 128, 2048])
    cos = K.dram_in("cos", [64, T]); sin = K.dram_in("sin", [64, T])
    o = K.dram_out("o", [T, nh * 64])
    K.attn_setup(cos, sin)
    wst = [K.sb("wlst%d" % i, [128, 2048], F32) for i in range(2)]
    Wt = [K.sb("Wt%d" % i, [128, 2048], BF16) for i in range(2)]
    vld2 = [K.sb("vld2_%d" % i, [128, 15, 64], F32) for i in range(2)]
    Vb2 = [K.sb("Vb2_%d" % i, [128, 15, 65], BF16) for i in range(2)]
    ostl = [K.sb("ostD%d" % i, [64, NA_ROWS, 64], F32) for i in range(2)]
    for h in range(nh):
        i = h % 2
        Qa, Ka = K.Qa[i], K.Ka[i]
        P.dma(K.dq(), "ka%d" % i, Ka[0:64, :], k[h], w=[("Ka", i)])
        K.prep_k(Ka, ("Ka", i))
        P.dma(K.dq(), "qa%d" % i, Qa[0:64, :], q[h], w=[("Qa", i)])
        K.prep_q(Qa, ("Qa", i))
        Vb, vkey = K.load_v(v[h])
        P.dma(K.dq(), "v2_%d" % i, vld2[i][:], v[h][64:64 + 15 * 128, :].rearrange("(b p) d -> p b d", p=128), w=[("vld2", i)])
        P.op("pool", lambda e, i=i: e.memset(Vb2[i][:, :, 64:65], 1.0), w=[("Vb2", i)])
        P.op("pool", lambda e, i=i: e.tensor_copy(out=Vb2[i][:, :, 0:64], in_=vld2[i][:]), r=[("vld2", i)], w=[("Vb2", i)])
        P.dma(K.dq(), "wl%d" % i, wst[i][:], wl[h], w=[("wlst", i)])
        P.op("act", lambda e, i=i: e.activation(out=Wt[i][:], in_=wst[i][:], func=AF.Exp), r=[("wlst", i)], w=[("Wt", i)])
        for r0 in range(0, NA_ROWS, 4):
            oi = K.nxt("pO", 2)
            pO = K.pO[oi]
            for j in range(4):
                r = r0 + j
                ws = min(max(r - 4, 0), NA_ROWS - 8)
                var = ws - r + 7
                segs = [(Ka[0:65, ws * 64 + b * 128: ws * 64 + (b + 1) * 128], Qa[0:65, r * 64:(r + 1) * 64], 64) for b in range(4)]
                pt, ptk, _ = K.score_group(segs, Wt[i][:, var * 256:(var + 1) * 256], [("Wt", i)], [("Qa", i), ("Ka", i)])
                if ws % 2 == 0:
                    vt, vk, b0 = Vb, vkey, ws // 2
                else:
                    vt, vk, b0 = Vb2[i], ("Vb2", i), (ws - 1) // 2
                P.op("pe", [lambda e, b=b, pt=pt, pO=pO, j=j, vt=vt, b0=b0: e.matmul(pO[0:64, j, 0:65], lhsT=pt[:, b * 64:(b + 1) * 64], rhs=vt[:, b0 + b, :], start=(b == 0 and j == 0), stop=(b == 3), skip_group_check=True) for b in range(4)],
                     r=[ptk, vk], w=[("pO", oi)])
            ri = K.nxt("rec", 2)
            rec = K.rec[ri]
            P.op("dve", lambda e, rec=rec, pO=pO: e.reciprocal(out=rec[0:64, :], in_=pO[0:64, :, 64]), r=[("pO", oi)], w=[("rec", ri)])
            for j in range(4):
                P.op("dve", lambda e, j=j, rec=rec, pO=pO, r0=r0, h=h: e.tensor_scalar(out=ostl[h % 2][:, r0 + j, :], in0=pO[0:64, j, 0:64], scalar1=rec[0:64, j:j + 1], scalar2=None, op0=ALU.mult),
                     r=[("pO", oi), ("rec", ri)], w=[("ostD", i)])
        K.outs.append(P.dma("sp", "out%d" % i, o.rearrange("(r c) n -> c r n", c=64)[:, :, h * 64:(h + 1) * 64], ostl[i][:], r=[("ostD", i)]))
    return K.done()


def gather_rpb(rpb):
    nh = rpb.shape[0]
    dkr = np.arange(2)[:, None, None, None, None]
    kc = np.arange(64)[None, :, None, None, None]
    var = np.arange(8)[None, None, :, None, None]
    blk = np.arange(4)[None, None, None, :, None]
    c = np.arange(64)[None, None, None, None, :]
    ri = (var - 7) + 2 * blk + dkr + 7
    qc0 = np.clip(c - 8, 0, GRID_W - 16)
    inwin = (kc >= qc0) & (kc < qc0 + 16)
    ci = np.clip(kc - c, -15, 15) + 15
    ri_b, ci_b, in_b = np.broadcast_arrays(ri, ci, inwin)
    g = rpb[:, ri_b, ci_b]
    g = np.where(in_b[None], g, np.float32(-30000.0)).astype(np.float32)
    return np.ascontiguousarray(g.reshape(nh, 128, 8 * 4 * 64))


RW_H = 20
RW_SKIP_CHUNKS = False
RW_LN_A = 0.6065306597126334


def build_rwkv(ntq_lim=None, ng_lim=None, nch_lim=None, nlvl=5, stage=9):
    K = MixKernel()
    P = K.P
    T = SEQ
    H, HG, TQ = RW_H, 4, 256
    NG, NTQ, NCH = H // HG, T // TQ, TQ // 64
    C = H * 64
    zr = K.dram_in("zr", [64, H, T + 2]); zk = K.dram_in("zk", [64, H, T + 2]); zv = K.dram_in("zv", [64, H, T + 2])
    zwl = K.dram_in("zwl", [64, T + 2]); zal = K.dram_in("zal", [64, T + 2]); zgl = K.dram_in("zgl", [64, 3, T + 2])
    mu_d = K.dram_in("mu", [64, 2, 65])
    w0_d = K.dram_in("w0", [64, H]); w2_d = K.dram_in("w2", [64, C]); a0_d = K.dram_in("a0", [64, H]); a2_d = K.dram_in("a2", [64, C])
    g2_d = K.dram_in("g2", [64, 3, C]); kk_d = K.dram_in("kk", [64, H]); ka_d = K.dram_in("ka", [64, H]); rk_d = K.dram_in("rk", [64, H])
    id_d = K.dram_in("ident", [64, 4, 64]); msi_d = K.dram_in("maskSI", [64, 4, 128]); mlt_d = K.dram_in("maskLT", [64, 4, 64])
    rs_d = K.dram_in("resetm", [64, 4 * TQ])
    y_o = K.dram_out("y", [T, C]); bo_o = K.dram_out("bonus", [64, H, T]); g_o = K.dram_out("g", [64, H, T])

    def const(name, ap, shape):
        t = K.sb(name, shape, F32)
        P.dma(K.dq(), "c_" + name, t[:], ap, w=[name])
        return t
    mu = const("mu_s", mu_d, [64, 2, 65]); w0 = const("w0_s", w0_d, [64, H]); w2 = const("w2_s", w2_d, [64, C])
    a0 = const("a0_s", a0_d, [64, H]); a2 = const("a2_s", a2_d, [64, C]); g2 = const("g2_s", g2_d, [64, 3, C])
    kkp = const("kk_s", kk_d, [64, H]); ka = const("ka_s", ka_d, [64, H]); rk = const("rk_s", rk_d, [64, H])
    ident = const("ident_s", id_d, [64, 4, 64]); msi = const("msi_s", msi_d, [64, 4, 128]); mlt = const("mlt_s", mlt_d, [64, 4, 64])
    resetm = const("resetm_s", rs_d, [64, 4 * TQ])
    ones = K.sb("ones64", [64, 64], F32)
    P.op("pool", lambda e: e.memset(ones[:], 1.0), w=["ones64"])
    c0 = K.sb("c0", [64, 65], F32)
    P.op("dve", lambda e: e.tensor_tensor(out=c0[:], in0=mu[:, 0, :], in1=mu[:, 1, :], op=ALU.add), r=["mu_s"], w=["c0"])
    P.op("dve", lambda e: e.tensor_scalar(out=c0[:], in0=c0[:], scalar1=-1.0, scalar2=1.0, op0=ALU.mult, op1=ALU.add), r=["c0"], w=["c0"])
    omka = K.sb("omka", [64, H], F32)
    P.op("dve", lambda e: e.tensor_scalar(out=omka[:], in0=ka[:], scalar1=-1.0, scalar2=1.0, op0=ALU.mult, op1=ALU.add), r=["ka_s"], w=["omka"])
    ST = K.sb("ST", [64, H, 64], F32)
    P.op("pool", lambda e: e.memset(ST[:], 0.0), w=[("ST", g) for g in range(NG)])

    A3 = [64, HG, TQ]
    def arr(name):
        return K.sb(name, A3, F32)
    zl = [[K.sb("zl%d_%d" % (i, j), [64, HG, TQ + 2], F32) for j in range(3)] for i in range(2)]
    rs, ks, vs = arr("rs"), arr("ks"), arr("vs")
    t1, t2 = arr("t1"), arr("t2")
    lgs, ag, cin, cex, E3 = arr("lgs"), arr("ag"), arr("cin"), arr("cex"), arr("E3")
    kkn, rn, kd = arr("kkn"), arr("rn"), arr("kd")
    AR = K.sb("AR", [64, HG, NCH, 2, 64], F32)
    Bt, Kt = arr("Bt"), arr("Kt")
    gst = [arr("gst%d" % i) for i in range(2)]
    bst = [arr("bst%d" % i) for i in range(2)]
    lw = [K.sb("lw%d" % i, [64, TQ + 2], F32) for i in range(2)]
    lg3 = K.sb("lg3", [64, 3, TQ + 2], F32)
    twl = K.sb("twl", [64, TQ], F32); als = K.sb("als", [64, TQ], F32); sgl = K.sb("sgl", [64, 3, TQ], F32)
    Yst = [K.sb("Yst%d" % i, [64, NCH, HG * 64], F32) for i in range(2)]
    B = [K.ps("B%d" % i, [64, 512], F32) for i in range(8)]

    def sm(name, shape):
        return K.sb(name, shape, F32)
    TS = []
    for par in range(2):
        sfx = "_%d" % par
        TS.append(dict(BKT=sm("BKT" + sfx, [64, 2, HG, 64]), VTs=sm("VTs" + sfx, [64, HG, 64]), MabT=sm("MabT" + sfx, [64, HG, 64]),
                       M1=sm("M1" + sfx, [64, HG, 128]), M2=sm("M2" + sfx, [64, HG, 128]),
                       Nb=[sm("Nb%d%s" % (i, sfx), [64, HG, 64]) for i in range(2)], NTb=[sm("NTb%d%s" % (i, sfx), [64, HG, 64]) for i in range(2)],
                       Q=sm("Qs" + sfx, [64, HG, 64])))
    XT = sm("XTs", [64, HG, 64]); UT = sm("UTs", [64, HG, 64]); tS = sm("tS", [64, HG, 64])

    def tt(eng, out, a, b, op, r, w):
        P.op(eng, lambda e: e.tensor_tensor(out=out, in0=a, in1=b, op=op), r=r, w=w)

    def act(out, in_, func, r, w, **kw):
        P.op("act", lambda e: e.activation(out=out, in_=in_, func=func, **kw), r=r, w=w)

    def bc(t, col0, n=HG, width=TQ):
        return t[:, col0:col0 + n, None].to_broadcast([64, n, width])

    def shift1(dst, src, col, r, w):
        P.op("dve", lambda e: e.tensor_scalar(out=dst, in0=src[:, 1:TQ + 1], scalar1=c0[:, col:col + 1], scalar2=None, op0=ALU.mult), r=r + ["c0"], w=w)
        P.op("dve", lambda e: e.scalar_tensor_tensor(out=dst, in0=src[:, 0:TQ], scalar=mu[:, 0, col:col + 1], in1=dst, op0=ALU.mult, op1=ALU.add), r=r + w + ["mu_s"], w=w)
        P.op("dve", lambda e: e.scalar_tensor_tensor(out=dst, in0=src[:, 2:TQ + 2], scalar=mu[:, 1, col:col + 1], in1=dst, op0=ALU.mult, op1=ALU.add), r=r + w + ["mu_s"], w=w)

    def mm(out, lhsT, rhs, start=True, stop=True):
        return lambda e: e.matmul(out, lhsT=lhsT, rhs=rhs, start=start, stop=stop, skip_group_check=True)

    pl_i = [0]

    def next_pl():
        pl_i[0] = (pl_i[0] + 1) % 2
        return B[pl_i[0]], ("B", pl_i[0])

    for tq in range(NTQ if ntq_lim is None else ntq_lim):
        t0 = tq * TQ
        i = tq % 2
        P.dma(K.dq(), "lw%d" % i, lw[i][:], zwl[:, t0:t0 + TQ + 2], w=[("lw", i)])
        shift1(twl[:], lw[i], 60, [("lw", i)], ["twl"])
        act(twl[:], twl[:], AF.Tanh, ["twl"], ["twl"])
        j = 1 - i
        P.dma(K.dq(), "lwb%d" % i, lw[j][:], zal[:, t0:t0 + TQ + 2], w=[("lw", j)])
        shift1(als[:], lw[j], 61, [("lw", j)], ["als"])
        P.dma(K.dq(), "lg3", lg3[:], zgl[:, :, t0:t0 + TQ + 2], w=["lg3"])
        for jj in range(3):
            shift1(sgl[:, jj, :], lg3[:, jj, :], 62 + jj, ["lg3"], [("sgl", jj)])
            act(sgl[:, jj, :], sgl[:, jj, :], AF.Sigmoid, [("sgl", jj)], [("sgl", jj)])
        for hg in range(NG if ng_lim is None else ng_lim):
            h0 = hg * HG
            zi = K.nxt("zl", 2)
            srcs = (zr, zk, zv)
            dsts = (rs, ks, vs)
            for a in range(3):
                P.dma(K.dq(), "zl%d_%d" % (zi, a), zl[zi][a][:], srcs[a][:, h0:h0 + HG, t0:t0 + TQ + 2], w=[("zl", zi, a)])
                z = zl[zi][a]
                col = a * H + h0
                nm = ("rs", "ks", "vs")[a]
                tt("dve", t1[:], z[:, :, 1:TQ + 1], bc(c0, col), ALU.mult, [("zl", zi, a), "c0"], ["t1"])
                tt("pool", t2[:], z[:, :, 0:TQ], bc(mu[:, 0, :], col), ALU.mult, [("zl", zi, a), "mu_s"], ["t2"])
                tt("dve", t1[:], t1[:], t2[:], ALU.add, ["t1", "t2"], ["t1"])
                tt("pool", t2[:], z[:, :, 2:TQ + 2], bc(mu[:, 1, :], col), ALU.mult, [("zl", zi, a), "mu_s"], ["t2"])
                tt("dve", dsts[a][:], t1[:], t2[:], ALU.add, ["t1", "t2"], [nm])
            for pr in range(HG // 2):
                for (wt, wk, src, skey, bias, bkey, dst, dkey) in ((w2, "w2_s", twl, "twl", w0, "w0_s", lgs, "lgs"), (a2, "a2_s", als, "als", a0, "a0_s", ag, "ag")):
                    pl, plk = next_pl()
                    P.op("pe", [mm(pl[:, q * TQ:(q + 1) * TQ], wt[:, (h0 + pr * 2 + q) * 64:(h0 + pr * 2 + q + 1) * 64], src[:]) for q in range(2)], r=[wk, skey], w=[plk])
                    for q in range(2):
                        hh = pr * 2 + q
                        act(dst[:, hh, :], pl[:, q * TQ:(q + 1) * TQ], AF.Sigmoid, [plk, bkey], [dkey], bias=bias[:, h0 + hh:h0 + hh + 1])
                pl, plk = next_pl()
                fns = []
                for q in range(2):
                    h = h0 + pr * 2 + q
                    for jj in range(3):
                        fns.append(mm(pl[:, q * TQ:(q + 1) * TQ], g2[:, jj, h * 64:(h + 1) * 64], sgl[:, jj, :], start=(jj == 0 and q == 0), stop=(jj == 2)))
                P.op("pe", fns, r=["g2_s"] + [("sgl", jj) for jj in range(3)], w=[plk])
                gi = K.nxt("gst", 2) if pr == 0 else K.rot["gst"]
                act(gst[gi][:, pr * 2:pr * 2 + 2, :], pl[:].rearrange("p (q t) -> p q t", q=2), AF.Identity, [plk], [("gst", gi)])
            K.outs.append(P.dma("sp", "go%d" % gi, g_o[:, h0:h0 + HG, t0:t0 + TQ], gst[gi][:], r=[("gst", gi)]))
            f2 = lambda t: t[:].rearrange("p h t -> p (h t)")
            P.op("dve", lambda e: e.tensor_tensor_scan(out=f2(cin), data0=resetm[:], data1=f2(lgs), initial=0.0, op0=ALU.mult, op1=ALU.add), r=["lgs", "resetm_s"], w=["cin"])
            tt("pool", cex[:], cin[:], lgs[:], ALU.subtract, ["cin", "lgs"], ["cex"])
            act(E3[:], cin[:], AF.Exp, ["cin"], ["E3"], scale=RW_LN_A)
            act(cin[:], cin[:], AF.Exp, ["cin"], ["cin"], scale=-RW_LN_A)
            act(cex[:], cex[:], AF.Exp, ["cex"], ["cex"], scale=-RW_LN_A)
            E2, E1 = cin, cex
            tt("pool", kkn[:], ks[:], bc(kkp, h0), ALU.mult, ["ks", "kk_s"], ["kkn"])
            act(t1[:], kkn[:], AF.Square, ["kkn"], ["t1"])
            for pr in range(HG // 2):
                pl, plk = next_pl()
                P.op("pe", [mm(pl[:, q * TQ:(q + 1) * TQ], ones[:], t1[:, pr * 2 + q, :]) for q in range(2)], r=["ones64", "t1"], w=[plk])
                act(rn[:, pr * 2:pr * 2 + 2, :], pl[:].rearrange("p (q t) -> p q t", q=2), AF.Sqrt, [plk], ["rn"])
            P.op("dve", lambda e: e.tensor_scalar(out=rn[:], in0=rn[:], scalar1=1e-12, scalar2=None, op0=ALU.max), r=["rn"], w=["rn"])
            P.op("dve", lambda e: e.reciprocal(out=rn[:], in_=rn[:]), r=["rn"], w=["rn"])
            tt("dve", kkn[:], kkn[:], rn[:], ALU.mult, ["kkn", "rn"], ["kkn"])
            v4 = lambda t: t[:].rearrange("p h (c t) -> p h c t", t=64)
            P.op("dve", lambda e: e.scalar_tensor_tensor(out=AR[:, :, :, 0, :], in0=v4(kkn), scalar=-1.0, in1=v4(E1), op0=ALU.mult, op1=ALU.mult), r=["kkn", "cex"], w=["AR"])
            tt("pool", AR[:, :, :, 1, :], v4(rs), v4(E2), ALU.mult, ["rs", "cin"], ["AR"])
            tt("pool", t2[:], kkn[:], ag[:], ALU.mult, ["kkn", "ag"], ["t2"])
            tt("dve", Bt[:], t2[:], E3[:], ALU.mult, ["t2", "E3"], ["Bt"])
            tt("pool", t2[:], ag[:], bc(ka, h0), ALU.mult, ["ag", "ka_s"], ["t2"])
            tt("pool", t2[:], t2[:], bc(omka, h0), ALU.add, ["t2", "omka"], ["t2"])
            tt("dve", kd[:], ks[:], t2[:], ALU.mult, ["ks", "t2"], ["kd"])
            tt("pool", Kt[:], kd[:], E3[:], ALU.mult, ["kd", "E3"], ["Kt"])
            tt("dve", t1[:], rs[:], kd[:], ALU.mult, ["rs", "kd"], ["t1"])
            tt("pool", t1[:], t1[:], bc(rk, h0), ALU.mult, ["t1", "rk_s"], ["t1"])
            bi = K.nxt("bst", 2)
            for pr in range(HG // 2):
                pl, plk = next_pl()
                P.op("pe", [mm(pl[:, q * TQ:(q + 1) * TQ], ones[:], t1[:, pr * 2 + q, :]) for q in range(2)], r=["ones64", "t1"], w=[plk])
                tt("dve", bst[bi][:, pr * 2:pr * 2 + 2, :], pl[:].rearrange("p (q t) -> p q t", q=2), vs[:, pr * 2:pr * 2 + 2, :], ALU.mult, [plk, "vs"], [("bst", bi)])
            K.outs.append(P.dma("sp", "bo%d" % bi, bo_o[:, h0:h0 + HG, t0:t0 + TQ], bst[bi][:], r=[("bst", bi)]))
            yi = K.nxt("Yst", 2)
            skey = ("ST", hg)

            def precompute(c, par):
                cs = slice(c * 64, (c + 1) * 64)
                t = TS[par]
                BKT, VTs, MabT, M1, M2, Q = t["BKT"], t["VTs"], t["MabT"], t["M1"], t["M2"], t["Q"]
                k = lambda n: (n, par)
                ba, bb = 2 + 2 * par, 3 + 2 * par
                bT = B[ba][:].rearrange("p (a h k) -> p a h k", a=2, h=HG)
                bV = B[bb][:].rearrange("p (a h k) -> p a h k", a=2, h=HG)
                P.op("pe", [mm(bT[:, 0, hh, :], Bt[:, hh, cs], ident[:, 0, :]) for hh in range(HG)] + [mm(bT[:, 1, hh, :], Kt[:, hh, cs], ident[:, 0, :]) for hh in range(HG)],
                     r=["Bt", "Kt", "ident_s"], w=[("B", ba)])
                act(BKT[:], bT, AF.Identity, [], [k("BKT"), ("B", ba)])
                P.op("pe", [mm(bV[:, 0, hh, :], vs[:, hh, cs], ident[:, 0, :]) for hh in range(HG)] + [mm(bV[:, 1, hh, :], AR[:, hh, c, 0, :], Bt[:, hh, cs]) for hh in range(HG)],
                     r=["vs", "AR", "Bt", "ident_s"], w=[("B", bb)])
                yield
                act(VTs[:], bV[:, 0, :, :], AF.Identity, [], [k("VTs"), ("B", bb)])
                tt("dve", MabT[:], bV[:, 1, :, :], mlt[:], ALU.mult, ["mlt_s"], [k("MabT"), ("B", bb)])
                g1 = B[ba][:].rearrange("p (h x) -> p h x", h=HG)
                g2p = B[bb][:].rearrange("p (h x) -> p h x", h=HG)
                P.op("pe", [mm(g1[:, hh, :], Bt[:, hh, cs], AR[:, hh, c, :, :].rearrange("p a t -> p (a t)")) for hh in range(HG)], r=["Bt", "AR"], w=[("B", ba)])
                P.op("pe", [mm(g2p[:, hh, :], Kt[:, hh, cs], AR[:, hh, c, :, :].rearrange("p a t -> p (a t)")) for hh in range(HG)], r=["Kt", "AR"], w=[("B", bb)])
                yield
                tt("dve", M1[:], g1, msi[:], ALU.mult, ["msi_s"], [k("M1"), ("B", ba)])
                tt("dve", M2[:], g2p, msi[:], ALU.mult, ["msi_s"], [k("M2"), ("B", bb)])
                tt("pool", Q[:], M1[:, :, 0:64], ident[:], ALU.add, [k("M1"), "ident_s"], [k("Q")])
                N_ap, N_key, NT_ap, NT_key = M1[:, :, 0:64], k("M1"), MabT[:], k("MabT")
                bI = B[ba][:].rearrange("p (a h k) -> p a h k", a=2, h=HG)
                bQ = B[bb][:, 0:HG * 64].rearrange("p (h k) -> p h k", h=HG)
                for lvl in range(nlvl):
                    nb, ntb = t["Nb"][lvl % 2], t["NTb"][lvl % 2]
                    P.op("pe", [mm(bI[:, 0, hh, :], NT_ap[:, hh, :], N_ap[:, hh, :]) for hh in range(HG)] + [mm(bI[:, 1, hh, :], N_ap[:, hh, :], NT_ap[:, hh, :]) for hh in range(HG)],
                         r=[N_key, NT_key], w=[("B", ba)])
                    yield
                    act(nb[:], bI[:, 0, :, :], AF.Identity, [], [k(("Nb", lvl % 2)), ("B", ba)])
                    P.op("act", lambda e, ntb=ntb, bI=bI: e.activation(out=ntb[:], in_=bI[:, 1, :, :], func=AF.Identity), r=[], w=[k(("NTb", lvl % 2)), ("B", ba)])
                    N_ap, N_key, NT_ap, NT_key = nb[:], k(("Nb", lvl % 2)), ntb[:], k(("NTb", lvl % 2))
                    P.op("pe", [mm(bQ[:, hh, :], NT_ap[:, hh, :], Q[:, hh, :]) for hh in range(HG)], r=[NT_key, k("Q")], w=[("B", bb)])
                    yield
                    tt("dve", Q[:], Q[:], bQ, ALU.add, [], [k("Q"), ("B", bb)])

            def sequential(c, par):
                t = TS[par]
                BKT, VTs, M1, M2, Q = t["BKT"], t["VTs"], t["M1"], t["M2"], t["Q"]
                k = lambda n: (n, par)
                s1 = B[6][:].rearrange("p (a h k) -> p a h k", a=2, h=HG)
                s2 = B[7][:].rearrange("p (a h k) -> p a h k", a=2, h=HG)
                fns = []
                for hh in range(HG):
                    fns.append(mm(s1[:, 0, hh, :], AR[:, hh, c, 0, :], ST[:, h0 + hh, :], start=(hh == 0), stop=False))
                    fns.append(mm(s1[:, 0, hh, :], M2[:, hh, 0:64], VTs[:, hh, :], start=False, stop=True))
                P.op("pe", fns, r=["AR", skey, k("M2"), k("VTs")], w=[("B", 6)])
                act(XT[:], s1[:, 0, :, :], AF.Identity, [], ["XT", ("B", 6)])
                P.op("pe", [mm(s1[:, 1, hh, :], Q[:, hh, :], XT[:, hh, :], start=False, stop=True) for hh in range(HG)], r=[k("Q"), "XT"], w=[("B", 6)])
                P.op("act", lambda e: e.activation(out=UT[:], in_=s1[:, 1, :, :], func=AF.Identity), r=[], w=["UT", ("B", 6)])
                fns = []
                for hh in range(HG):
                    fns.append(mm(s2[:, 0, hh, :], AR[:, hh, c, 1, :], ST[:, h0 + hh, :], start=(hh == 0), stop=False))
                    fns.append(mm(s2[:, 0, hh, :], M1[:, hh, 64:128], UT[:, hh, :], start=False, stop=False))
                    fns.append(mm(s2[:, 0, hh, :], M2[:, hh, 64:128], VTs[:, hh, :], start=False, stop=True))
                for hh in range(HG):
                    fns.append(mm(s2[:, 1, hh, :], BKT[:, 0, hh, :], UT[:, hh, :], start=False, stop=False))
                    fns.append(mm(s2[:, 1, hh, :], BKT[:, 1, hh, :], VTs[:, hh, :], start=False, stop=True))
                P.op("pe", fns, r=["AR", skey, k("M1"), k("M2"), "UT", k("VTs"), k("BKT")], w=[("B", 7)])
                act(Yst[yi][:, c, :].rearrange("p (h v) -> p h v", h=HG), s2[:, 0, :, :], AF.Identity, [], [("Yst", yi), ("B", 7)])
                tt("dve", tS[:], s2[:, 1, :, :], ST[:, h0:h0 + HG, :], ALU.add, [skey], ["tS", ("B", 7)])
                tt("dve", ST[:, h0:h0 + HG, :], tS[:], E2[:, :, c * 64 + 63:c * 64 + 64].to_broadcast([64, HG, 64]), ALU.mult, ["tS", "cin"], [skey])

            nch = NCH if nch_lim is None else nch_lim
            for cp in range(0, nch, 2):
                cl = [cc for cc in (cp, cp + 1) if cc < nch]
                gens = [precompute(cc, cc % 2) for cc in cl]
                while gens:
                    for gnr in list(gens):
                        try:
                            next(gnr)
                        except StopIteration:
                            gens.remove(gnr)
                for cc in cl:
                    sequential(cc, cc % 2)
            K.outs.append(P.dma("sp", "yo%d" % yi, y_o[t0:t0 + TQ, h0 * 64:(h0 + HG) * 64].rearrange("(c t) n -> t c n", t=64), Yst[yi][:], r=[("Yst", yi)]))
    return K.done()


A_W, B_W = 768, 1280
RW_OFF = dict(r=0, k=1280, v=2560, wl=3840, al=3968, gl=4096)


def rw_consts():
    TQ = 256
    j = np.arange(64)[:, None]
    t = np.arange(64)[None, :]
    su = (j < t).astype(np.float32)
    iu = (j <= t).astype(np.float32)
    msi = np.concatenate([su, iu], 1)
    ident = np.eye(64, dtype=np.float32)
    rep = lambda m: np.ascontiguousarray(np.broadcast_to(m[:, None, :], (64, 4, m.shape[1])))
    resetm = np.ones((64, 4 * TQ), np.float32)
    resetm[:, ::64] = 0.0
    return {"ident": rep(ident), "maskSI": rep(msi), "maskLT": rep((j > t).astype(np.float32)), "resetm": resetm}


def rw_host_inputs(zbT, prm, d):
    T = zbT.shape[1]
    z = zbT[:, ::-1] if d else zbT
    zp = np.pad(z, ((0, 0), (1, 1)))
    hv = lambda a: np.ascontiguousarray(a.reshape(RW_H, 64, -1).transpose(1, 0, 2))
    pc = lambda v: v.reshape(-1, 64).T
    mup, mun = (prm["rw_mu_next"], prm["rw_mu_prev"]) if d else (prm["rw_mu_prev"], prm["rw_mu_next"])
    mu = np.zeros((64, 2, 65), np.float32)
    for i, m in enumerate((mup, mun)):
        mu[:, i, 0:20] = pc(m[0:1280]); mu[:, i, 20:40] = pc(m[1280:2560]); mu[:, i, 40:60] = pc(m[2560:3840])
        mu[:, i, 60] = m[3840 + 64 * d:3840 + 64 * (d + 1)]; mu[:, i, 61] = m[3968 + 64 * d:3968 + 64 * (d + 1)]
        mu[:, i, 62:65] = pc(m[4096:4288])
    ins = {
        "zr": hv(zp[0:1280]), "zk": hv(zp[1280:2560]), "zv": hv(zp[2560:3840]),
        "zwl": np.ascontiguousarray(zp[3840 + 64 * d:3840 + 64 * (d + 1)]), "zal": np.ascontiguousarray(zp[3968 + 64 * d:3968 + 64 * (d + 1)]),
        "zgl": np.ascontiguousarray(zp[4096:4288].reshape(3, 64, T + 2).transpose(1, 0, 2)),
        "mu": mu, "w0": np.ascontiguousarray(pc(prm["rw_w0"][d])), "w2": np.ascontiguousarray(prm["rw_w2"][d]),
        "a0": np.ascontiguousarray(pc(prm["rw_a0"][d])), "a2": np.ascontiguousarray(prm["rw_a2"][d]),
        "g2": np.ascontiguousarray(prm["rw_g2"].reshape(3, 64, 1280).transpose(1, 0, 2)),
        "kk": np.ascontiguousarray(pc(prm["rw_k_k"])), "ka": np.ascontiguousarray(pc(prm["rw_k_a"])), "rk": np.ascontiguousarray(prm["rw_r_k"].T),
    }
    ins.update(rw_consts())
    return ins


RW_GN_EPS = 64e-5


def tok_rw_post(K, y0, y1, b0, b1, g, lnw, lnb):
    P = K.P
    bo = K.sb("blockones", [128, 128], F32)
    P.op("pool", lambda e: e.memset(bo[:], 0.0), w=["bo"])
    P.op("pool", lambda e: e.memset(bo[0:64, 0:64], 1.0 / 64), w=["bo"])
    P.op("pool", lambda e: e.memset(bo[64:128, 64:128], 1.0 / 64), w=["bo"])
    epsg = K.sb("epsg", [128, 1], F32)
    P.op("pool", lambda e: e.memset(epsg[:], RW_GN_EPS), w=["epsg"])
    st = [K.sb("rwp%d" % i, [128, 512], F32) for i in range(5)]
    srcs = (y0, y1, b0, b1, g)
    for cc in range(10):
        for tc in range(K.ntc):
            ts = slice(tc * 512, (tc + 1) * 512)
            for i in range(5):
                P.dma(K.dq(), "rwp%d" % i, st[i][:], srcs[i][cc * 128:(cc + 1) * 128, ts], w=[("rwp", i)])
            A, Bq, Cq, Dq, G = st
            s0, s1 = K.sq

            def tt(eng, out, a, b, op, r, w):
                P.op(eng, lambda e: e.tensor_tensor(out=out, in0=a, in1=b, op=op), r=r, w=w)
            tt("dve", A[:], A[:], Bq[:], ALU.add, [("rwp", 0), ("rwp", 1)], [("rwp", 0)])
            tt("pool", Cq[:], Cq[:], Dq[:], ALU.add, [("rwp", 2), ("rwp", 3)], [("rwp", 2)])
            P.op("pe", lambda e, A=A: e.matmul(K.ps_ss[:], lhsT=bo[:], rhs=A[:], start=True, stop=True), r=["bo", ("rwp", 0)], w=["ps_ss"])
            tt("dve", A[:], A[:], K.ps_ss[:], ALU.subtract, [("rwp", 0)], [("rwp", 0), "ps_ss"])
            P.op("act", lambda e, A=A, s0=s0: e.activation(out=s0[:], in_=A[:], func=AF.Square), r=[("rwp", 0)], w=[("sq", 0)])
            P.op("pe", lambda e, s0=s0: e.matmul(K.ps_ss[:], lhsT=bo[:], rhs=s0[:], start=True, stop=True), r=["bo", ("sq", 0)], w=["ps_ss"])
            P.op("act", lambda e, s1=s1: e.activation(out=s1[:], in_=K.ps_ss[:], func=AF.Sqrt, bias=epsg[:, 0:1], scale=1.0), r=["epsg"], w=[("sq", 1), "ps_ss"])
            P.op("dve", lambda e, s1=s1: e.reciprocal(out=s1[:], in_=s1[:]), r=[("sq", 1)], w=[("sq", 1)])
            tt("dve", A[:], A[:], s1[:], ALU.mult, [("rwp", 0), ("sq", 1)], [("rwp", 0)])
            P.op("dve", lambda e, A=A, cc=cc: e.tensor_scalar(out=A[:], in0=A[:], scalar1=lnw[:, cc:cc + 1], scalar2=lnb[:, cc:cc + 1], op0=ALU.mult, op1=ALU.add),
                 r=[("rwp", 0), "lnw", "lnb"], w=[("rwp", 0)])
            tt("dve", A[:], A[:], Cq[:], ALU.add, [("rwp", 0), ("rwp", 2)], [("rwp", 0)])
            tt("dve", K.hT[:, 6 + cc, ts], A[:], G[:], ALU.mult, [("rwp", 0), ("rwp", 4)], [("h", 6 + cc, tc)])


def _run(nc, in_maps):
    res = run_bass_kernel_spmd(nc, in_maps, core_ids=list(range(NCORES)))
    return res.results


def _pk(v):
    return np.ascontiguousarray(np.asarray(v, np.float32).reshape(-1, 128).T)


PAIRS = [[0, 1], [2, 3], [4, 5], [6, 7]]


def stage_attn_A(nc, P, zm, oex, cos, sin, ident_d, mask_d, nh=6):
    K = MixKernel(nc, P)
    T = SEQ
    K.attn_setup(cos, sin, ident_d)
    mask = K.sb("maskA", [128, 2 * T - 128], BF16)
    P.dma("sp", "mask", mask[:], mask_d, w=["mask"])
    ost = K.sb("ostA", [128, 16, nh * 64], F32)
    def prep(h):
        i = h % 2
        Qa, Ka = K.Qa[i], K.Ka[i]
        K.rope_aug2(Ka, ("Ka", i), zm, nh * 64 + h * 64)
        K.prep_k(Ka, ("Ka", i))
        K.rope_aug2(Qa, ("Qa", i), zm, h * 64)
        K.prep_q(Qa, ("Qa", i))
        K.load_v_T(zm, 2 * nh * 64 + h * 64, K.Vb[i], ("Vb", i))
    prep(0)
    for h in range(nh):
        i = h % 2
        Qa, Ka = K.Qa[i], K.Ka[i]
        if h + 1 < nh:
            K.set_drip(K.record(lambda: prep(h + 1)), 50)
        K.attn_toeplitz(Qa, ("Qa", i), Ka, ("Ka", i), K.Vb[i], ("Vb", i), mask, "mask", T - 128, 1024, ost, "ost", h * 64)
        K.flush()
    K.out_T(ost, "ost", nh * 64, oex, 0)
    K.done()


def stage_attn_C(nc, P, zm, oex, cos, sin, ident_d, mask_d, sink_d, nq=8):
    K = MixKernel(nc, P)
    T = SEQ
    K.attn_setup(cos, sin, ident_d)
    mask = K.sb("maskC", [128, 2 * T - 128], BF16)
    P.dma("sp", "mask", mask[:], mask_d, w=["mask"])
    sink = K.sb("sinkS", [128, nq], F32)
    P.dma("sp", "sink", sink[:], sink_d, w=["sink"])
    ost = K.sb("ostC", [128, 16, nq * 64], F32)
    Ka = K.Ka[0]
    K.rope_aug2(Ka, ("Ka", 0), zm, nq * 64)
    K.prep_k(Ka, ("Ka", 0))
    K.load_v_T(zm, nq * 64 + 64, K.Vb[0], ("Vb", 0))
    def prep(h):
        i = h % 2
        K.rope_aug2(K.Qa[i], ("Qa", i), zm, h * 64)
        K.prep_q(K.Qa[i], ("Qa", i))
    prep(0)
    for h in range(nq):
        i = h % 2
        Qa = K.Qa[i]
        if h + 1 < nq:
            K.set_drip(K.record(lambda: prep(h + 1)), 20)
        K.attn_toeplitz(Qa, ("Qa", i), Ka, ("Ka", 0), K.Vb[0], ("Vb", 0), mask, "mask", T - 128, 128, ost, "ost", h * 64, sink_col=sink[:, h:h + 1])
        K.flush()
    K.out_T(ost, "ost", nq * 64, oex, 0)
    K.done()


def stage_attn_D(nc, P, zm, oex, cos, sin, ident_d, wl, nh=8, zrow0=640, orow0=512):
    K = MixKernel(nc, P)
    T = SEQ
    K.attn_setup(cos, sin, ident_d)
    wst = [K.sb("wlst%d" % i, [128, 2048], F32) for i in range(2)]
    Wt = [K.sb("Wt%d" % i, [128, 2048], BF16) for i in range(2)]
    Vb2 = [K.sb("Vb2_%d" % i, [128, 15, 65], BF16) for i in range(2)]
    ostl = [K.sb("ostD%d" % i, [64, NA_ROWS, 64], F32) for i in range(2)]
    def prep(h):
        i = h % 2
        Qa, Ka = K.Qa[i], K.Ka[i]
        sa, sb_ = K.ld[i]
        P.dma(K.dq(), "ka%d" % i, sa[:], zm[zrow0 + 512 + h * 64:zrow0 + 512 + (h + 1) * 64, 1:T + 1], w=[("ld", i, 0)])
        P.op("act", lambda e, sa=sa, Ka=Ka: e.activation(out=Ka[0:64, :], in_=sa[:], func=AF.Identity), r=[("ld", i, 0)], w=[("Ka", i)])
        K.prep_k(Ka, ("Ka", i))
        P.dma(K.dq(), "qa%d" % i, sb_[:], zm[zrow0 + h * 64:zrow0 + (h + 1) * 64, 1:T + 1], w=[("ld", i, 1)])
        P.op("dve", lambda e, sb_=sb_, Qa=Qa: e.tensor_copy(out=Qa[0:64, :], in_=sb_[:]), r=[("ld", i, 1)], w=[("Qa", i)])
        K.prep_q(Qa, ("Qa", i))
        K.load_v_T(zm, zrow0 + 1024 + h * 64, K.Vb[i], ("Vb", i))
        K.load_v_T(zm, zrow0 + 1024 + h * 64, Vb2[i], ("Vb2", i), shift=64, nblk=15)
        P.dma(K.dq(), "wl%d" % i, wst[i][:], wl[h], w=[("wlst", i)])
        P.op("act", lambda e, i=i: e.activation(out=Wt[i][:], in_=wst[i][:], func=AF.Exp), r=[("wlst", i)], w=[("Wt", i)])
    prep(0)
    for h in range(nh):
        i = h % 2
        Qa, Ka = K.Qa[i], K.Ka[i]
        Vb, vkey = K.Vb[i], ("Vb", i)
        if h + 1 < nh:
            K.set_drip(K.record(lambda: prep(h + 1)), 30)
        pend = []

        def emit_pv(item, i=i):
            pt, ptk, vt, vk, b0, j, pO, oi, r0 = item
            P.op("pe", [lambda e, b=b, pt=pt, pO=pO, j=j, vt=vt, b0=b0: e.matmul(pO[0:64, j, 0:65], lhsT=pt[:, b * 64:(b + 1) * 64], rhs=vt[:, b0 + b, :], start=(b == 0 and j == 0), stop=(b == 3), skip_group_check=True) for b in range(4)],
                 r=[ptk, vk], w=[("pO", oi)])
            if j == 3:
                ri = K.nxt("rec", 2)
                rec = K.rec[ri]
                P.op("dve", lambda e, rec=rec, pO=pO: e.reciprocal(out=rec[0:64, :], in_=pO[0:64, :, 64]), r=[("pO", oi)], w=[("rec", ri)])
                for jj in range(4):
                    P.op("dve", lambda e, jj=jj, rec=rec, pO=pO, r0=r0, i=i: e.tensor_scalar(out=ostl[i][:, r0 + jj, :], in0=pO[0:64, jj, 0:64], scalar1=rec[0:64, jj:jj + 1], scalar2=None, op0=ALU.mult),
                         r=[("pO", oi), ("rec", ri)], w=[("ostD", i)])
        for r0 in range(0, NA_ROWS, 4):
            oi = K.nxt("pO", 2)
            pO = K.pO[oi]
            for j in range(4):
                r = r0 + j
                ws = min(max(r - 4, 0), NA_ROWS - 8)
                var = ws - r + 7
                segs = [(Ka[0:65, ws * 64 + b * 128: ws * 64 + (b + 1) * 128], Qa[0:65, r * 64:(r + 1) * 64], 64) for b in range(4)]
                pt, ptk, _ = K.score_group(segs, Wt[i][:, var * 256:(var + 1) * 256], [("Wt", i)], [("Qa", i), ("Ka", i)])
                if ws % 2 == 0:
                    vt, vk, b0 = Vb, vkey, ws // 2
                else:
                    vt, vk, b0 = Vb2[i], ("Vb2", i), (ws - 1) // 2
                pend.append((pt, ptk, vt, vk, b0, j, pO, oi, r0))
                if len(pend) > 2:
                    emit_pv(pend.pop(0))
        while pend:
            emit_pv(pend.pop(0))
        K.flush()
        for g in range(4):
            P.op("pe", [(lambda e, b=b, g=g, i=i: e.matmul(K.pN[0:64, b * 64:(b + 1) * 64], lhsT=ostl[i][:, g * 8 + b, :], rhs=K.ident[0:64, 0:64], start=True, stop=True)) for b in range(8)],
                 r=[("ostD", i), "ident"], w=["pN"])
            si = K.nxt("stg", 2)
            st = K.stg[si]
            P.op("act", lambda e, st=st: e.activation(out=st[0:64, :], in_=K.pN[0:64, :], func=AF.Identity), r=[], w=[("stg", si), "pN"])
            P.dma("sp", "oT%d" % si, oex[orow0 + h * 64:orow0 + (h + 1) * 64, g * 512:(g + 1) * 512], st[0:64, :], r=[("stg", si)])
    K.done()


def stage_rwkv(nc, P, zm, oex, prm, z_row0=1152, o_row0=384):
    K = MixKernel(nc, P)
    T = SEQ
    H, TQ = 10, 256
    NTQ, NCH = T // TQ, TQ // 64
    C = H * 64
    GROUPS = [(0, 4), (4, 4), (8, 2)]
    HGM = 4
    hv = lambda lo: zm[z_row0 + lo:z_row0 + lo + 640, :].rearrange("(h p) t -> p h t", p=64)
    zr, zk, zv = hv(0), hv(640), hv(1280)
    zwl = [zm[z_row0 + 1920 + 64 * d:z_row0 + 1984 + 64 * d, :] for d in range(2)]
    zal = [zm[z_row0 + 2048 + 64 * d:z_row0 + 2112 + 64 * d, :] for d in range(2)]
    zgl = zm[z_row0 + 2176:z_row0 + 2368, :].rearrange("(j p) t -> p j t", p=64)
    yd = [dram_reg(nc, "rw_y%d" % d, [64, H, T], F32, "Internal") for d in range(2)]
    bd = [dram_reg(nc, "rw_b%d" % d, [64, H, T], F32, "Internal") for d in range(2)]
    gd = dram_reg(nc, "rw_g", [64, H, T], F32, "Internal")
    oexv = oex[o_row0:o_row0 + C, :].rearrange("(h p) t -> p h t", p=64)

    def const(name, ap, shape):
        t = K.sb(name, shape, F32)
        P.dma(K.dq(), "cst", t[:], ap, w=[name])
        return t
    NMU = 37
    mu = const("mu_s", prm["mu"], [64, 2, NMU]); w0 = const("w0_s", prm["w0"], [64, 2, H]); w2 = const("w2_s", prm["w2"], [64, 2, C])
    a0 = const("a0_s", prm["a0"], [64, 2, H]); a2 = const("a2_s", prm["a2"], [64, 2, C]); g2 = const("g2_s", prm["g2"], [64, 3, C])
    kkp = const("kk_s", prm["kk"], [64, H]); ka = const("ka_s", prm["ka"], [64, H]); rk = const("rk_s", prm["rk"], [64, H])
    lnw = const("lnw_s", prm["lnw"], [64, H]); lnb = const("lnb_s", prm["lnb"], [64, H])
    ident = const("ident_s", prm["ident4"], [64, 4, 64])
    msi = [const("msi%d_s" % d, prm["maskSI"][d], [64, 4, 128]) for d in range(2)]
    mlt = [const("mlt%d_s" % d, prm["maskLT"][d], [64, 4, 64]) for d in range(2)]
    resetm = const("resetm_s", prm["resetm"], [64, 4 * TQ])
    CK = ["mu_s", "w0_s", "w2_s", "a0_s", "a2_s", "g2_s", "kk_s", "ka_s", "rk_s", "lnw_s", "lnb_s", "ident_s", "msi0_s", "msi1_s", "mlt0_s", "mlt1_s", "resetm_s"]
    ones = K.sb("ones64", [64, 64], F32)
    P.op("pool", lambda e: e.memset(ones[:], 1.0), r=CK, w=["ones64"])
    onesm = K.sb("onesm64", [64, 64], F32)
    P.op("pool", lambda e: e.memset(onesm[:], 1.0 / 64), w=["onesm"])
    epsg = K.sb("epsg", [64, 1], F32)
    P.op("pool", lambda e: e.memset(epsg[:], RW_GN_EPS), w=["epsg"])
    c0 = K.sb("c0", [64, NMU], F32)
    P.op("dve", lambda e: e.tensor_tensor(out=c0[:], in0=mu[:, 0, :], in1=mu[:, 1, :], op=ALU.add), r=CK, w=["c0"])
    P.op("dve", lambda e: e.tensor_scalar(out=c0[:], in0=c0[:], scalar1=-1.0, scalar2=1.0, op0=ALU.mult, op1=ALU.add), r=["c0"], w=["c0"])
    omka = K.sb("omka", [64, H], F32)
    P.op("dve", lambda e: e.tensor_scalar(out=omka[:], in0=ka[:], scalar1=-1.0, scalar2=1.0, op0=ALU.mult, op1=ALU.add), r=CK, w=["omka"])
    P.op("act", lambda e: e.activation(out=epsg[:], in_=epsg[:], func=AF.Identity), r=CK + ["epsg"], w=["epsg"])
    ST = K.sb("ST", [64, H, 64], F32)

    A3 = [64, HGM, TQ]
    arr = lambda name: K.sb(name, A3, F32)
    zl = [K.sb("zl%d" % j, [64, HGM, TQ + 2], F32) for j in range(3)]
    F32R = mybir.dt.float32r
    arr_r = lambda name: K.sb(name, A3, F32R)
    rs, ks, vs = arr("rs"), arr("ks"), arr_r("vs")
    ep_b0 = arr("ep_b0")
    t1, t2 = arr("t1"), arr("t2")
    lgs, ag, cin, cex, E3 = arr("lgs"), arr("ag"), arr("cin"), arr("cex"), arr("E3")
    kkn, rn, kd = arr("kkn"), arr("rn"), arr("kd")
    AR = K.sb("AR", [64, HGM, NCH, 2, 64], F32R)
    Bt, Kt = arr_r("Bt"), arr_r("Kt")
    gst, bst, Yst = arr("gst"), arr("bst"), arr("Yst")
    lw = [K.sb("lw%d" % i, [64, TQ + 2], F32) for i in range(2)]
    lg3 = K.sb("lg3", [64, 3, TQ + 2], F32)
    twl = K.sb("twl", [64, TQ], F32); als = K.sb("als", [64, TQ], F32); sgl = K.sb("sgl", [64, 3, TQ], F32)
    B = [K.ps("B%d" % i, [64, 512], F32) for i in range(8)]
    sm = lambda name, shape: K.sb(name, shape, F32R)
    identr = K.sb("identr", [64, 4, 64], F32R)
    P.op("act", lambda e: e.activation(out=identr[:], in_=ident[:], func=AF.Identity), r=CK, w=["identr"])
    STr = K.sb("STr", [64, H, 64], F32R)
    TS = []
    for par in range(2):
        sfx = "_%d" % par
        TS.append(dict(BKT=sm("BKT" + sfx, [64, 2, HGM, 64]), VTs=sm("VTs" + sfx, [64, HGM, 64]), MabT=sm("MabT" + sfx, [64, HGM, 64]),
                       M1=sm("M1" + sfx, [64, HGM, 128]), M2=sm("M2" + sfx, [64, HGM, 128]),
                       NN=[sm("NN%d%s" % (i, sfx), [64, 2, HGM, 64]) for i in range(2)],
                       Q=sm("Qs" + sfx, [64, HGM, 64])))
    XT = sm("XTs", [64, HGM, 64]); UT = sm("UTs", [64, HGM, 64]); tS = K.sb("tS", [64, HGM, 64], F32)

    def tt(eng, out, a, b, op, r, w):
        P.op(eng, lambda e: e.tensor_tensor(out=out, in0=a, in1=b, op=op), r=r, w=w)

    def act(out, in_, func, r, w, **kw):
        P.op("act", lambda e: e.activation(out=out, in_=in_, func=func, **kw), r=r, w=w)

    def bc(t, col0, n, width=TQ):
        return t[:, col0:col0 + n, None].to_broadcast([64, n, width])

    def shift1(dst, src, col, r, w):
        P.op("dve", lambda e: e.tensor_scalar(out=dst, in0=src[:, 1:TQ + 1], scalar1=c0[:, col:col + 1], scalar2=None, op0=ALU.mult), r=r + ["c0"], w=w)
        P.op("dve", lambda e: e.scalar_tensor_tensor(out=dst, in0=src[:, 0:TQ], scalar=mu[:, 0, col:col + 1], in1=dst, op0=ALU.mult, op1=ALU.add), r=r + w, w=w)
        P.op("dve", lambda e: e.scalar_tensor_tensor(out=dst, in0=src[:, 2:TQ + 2], scalar=mu[:, 1, col:col + 1], in1=dst, op0=ALU.mult, op1=ALU.add), r=r + w, w=w)

    def mm(out, lhsT, rhs, start=True, stop=True):
        return lambda e: e.matmul(out, lhsT=lhsT, rhs=rhs, start=start, stop=stop, skip_group_check=True)

    pl_i = [0]

    def next_pl():
        pl_i[0] = (pl_i[0] + 1) % 2
        return B[pl_i[0]], ("B", pl_i[0])

    vs2 = [vs, arr_r("vsB")]
    cin2 = [cin, arr("cinB")]
    AR2 = [AR, K.sb("ARB", [64, HGM, NCH, 2, 64], F32R)]
    Bt2 = [Bt, arr_r("BtB")]
    Kt2 = [Kt, arr_r("KtB")]
    tq_order_of = lambda d: list(range(NTQ)) if d == 0 else list(range(NTQ - 1, -1, -1))
    rwc = [dram_reg(nc, "rw_c%d" % i, [64, H, T], F32, "Internal") for i in range(4)]

    def lora_prep(d, tq):
        t0 = tq * TQ
        P.dma(K.dq(), "lw0", lw[0][:], zwl[d][:, t0:t0 + TQ + 2], w=[("lw", 0)])
        shift1(twl[:], lw[0], 30 + d, [("lw", 0)], ["twl"])
        act(twl[:], twl[:], AF.Tanh, ["twl"], ["twl"])
        P.dma(K.dq(), "lw1", lw[1][:], zal[d][:, t0:t0 + TQ + 2], w=[("lw", 1)])
        shift1(als[:], lw[1], 32 + d, [("lw", 1)], ["als"])
        if d == 0:
            P.dma(K.dq(), "lg3", lg3[:], zgl[:, :, t0:t0 + TQ + 2], w=["lg3"])
            for jj in range(3):
                shift1(sgl[:, jj, :], lg3[:, jj, :], 34 + jj, ["lg3"], [("sgl", jj)])
                act(sgl[:, jj, :], sgl[:, jj, :], AF.Sigmoid, [("sgl", jj)], [("sgl", jj)])

    def prep_visit(d, tq, hg, par, with_lora):
        if with_lora:
            lora_prep(d, tq)
        t0 = tq * TQ
        h0, n = GROUPS[hg]
        vs, cin, AR, Bt, Kt = vs2[par], cin2[par], AR2[par], Bt2[par], Kt2[par]
        if d == 0:
            srcs = (zr, zk, zv)
            dsts = (rs, ks, vs)
            for a in range(3):
                P.dma(K.dq(), "zl%d" % a, zl[a][:, 0:n, :], srcs[a][:, h0:h0 + n, t0:t0 + TQ + 2], w=[("zl", a)])
                z = zl[a]
                col = a * H + h0
                nm = ("rs", "ks", ("vs", par))[a]
                dd = dsts[a]
                tt("dve", dd[:, 0:n, :], z[:, 0:n, 1:TQ + 1], bc(c0, col, n), ALU.mult, [("zl", a), "c0"], [nm])
                tt("dve", t2[:, 0:n, :], z[:, 0:n, 0:TQ], bc(mu[:, 0, :], col, n), ALU.mult, [("zl", a)], ["t2"])
                tt("dve", dd[:, 0:n, :], dd[:, 0:n, :], t2[:, 0:n, :], ALU.add, [nm, "t2"], [nm])
                tt("dve", t1[:, 0:n, :], z[:, 0:n, 2:TQ + 2], bc(mu[:, 1, :], col, n), ALU.mult, [("zl", a)], ["t1"])
                tt("dve", dd[:, 0:n, :], dd[:, 0:n, :], t1[:, 0:n, :], ALU.add, [nm, "t1"], [nm])
            for a_, (arr_, nm_) in enumerate(((rs, "rs"), (ks, "ks"), (vs, ("vs", par)))):
                src_ = arr_[:, 0:n, :].bitcast(F32) if a_ == 2 else arr_[:, 0:n, :]
                P.dma("sp", "cs%d" % a_, rwc[a_][:, h0:h0 + n, t0:t0 + TQ], src_, r=[nm_], w=[("rwc", a_, tq, hg)])
        else:
            P.dma("sp", "cl0", rs[:, 0:n, :], rwc[0][:, h0:h0 + n, t0:t0 + TQ], r=[("rwc", 0, tq, hg)], w=["rs"])
            P.dma("sp", "cl1", ks[:, 0:n, :], rwc[1][:, h0:h0 + n, t0:t0 + TQ], r=[("rwc", 1, tq, hg)], w=["ks"])
            P.dma("sp", "cl2", t1[:, 0:n, :], rwc[2][:, h0:h0 + n, t0:t0 + TQ], r=[("rwc", 2, tq, hg)], w=["t1"])
            P.op("dve", lambda e, n=n, vs=vs: e.tensor_copy(out=vs[:, 0:n, :], in_=t1[:, 0:n, :]), r=["t1"], w=[("vs", par)])
        for pr in range(n // 2):
            for (wt, src, skey, bias, dst, dkey) in ((w2, twl, "twl", w0, lgs, "lgs"), (a2, als, "als", a0, ag, "ag")):
                pl, plk = next_pl()
                P.op("pe", [mm(pl[:, q * TQ:(q + 1) * TQ], wt[:, d, (h0 + pr * 2 + q) * 64:(h0 + pr * 2 + q + 1) * 64], src[:]) for q in range(2)], r=[skey], w=[plk])
                for q in range(2):
                    hh = pr * 2 + q
                    act(dst[:, hh, :], pl[:, q * TQ:(q + 1) * TQ], AF.Sigmoid, [], [dkey, plk], bias=bias[:, d, h0 + hh:h0 + hh + 1])
            if d == 0:
                pl, plk = next_pl()
                fns = []
                for q in range(2):
                    h = h0 + pr * 2 + q
                    for jj in range(3):
                        fns.append(mm(pl[:, q * TQ:(q + 1) * TQ], g2[:, jj, h * 64:(h + 1) * 64], sgl[:, jj, :], start=(jj == 0 and q == 0), stop=(jj == 2)))
                P.op("pe", fns, r=[("sgl", jj) for jj in range(3)], w=[plk])
                act(gst[:, pr * 2:pr * 2 + 2, :], pl[:].rearrange("p (q t) -> p q t", q=2), AF.Identity, [], ["gst", plk])
        if d == 0:
            P.dma("sp", "go", gd[:, h0:h0 + n, t0:t0 + TQ], gst[:, 0:n, :], r=["gst"])
        f2 = lambda t, n=n: t[:, 0:n, :].rearrange("p h t -> p (h t)")
        v4 = lambda t, n=n: t[:, 0:n, :].rearrange("p h (c t) -> p h c t", t=64)
        P.op("dve", lambda e, n=n, f2=f2: e.tensor_tensor_scan(out=f2(cin), data0=resetm[:, 0:n * TQ], data1=f2(lgs), initial=0.0, op0=ALU.mult, op1=ALU.add), r=["lgs"], w=[("cin", par)])
        if d == 0:
            tt("dve", cex[:, 0:n, :], cin[:, 0:n, :], lgs[:, 0:n, :], ALU.subtract, [("cin", par), "lgs"], ["cex"])
        else:
            tot = v4(cin)[:, :, :, 63:64].to_broadcast([64, n, NCH, 64])
            tt("dve", v4(cex), tot, v4(cin), ALU.subtract, [("cin", par)], ["cex"])
            tt("dve", cin[:, 0:n, :], cex[:, 0:n, :], lgs[:, 0:n, :], ALU.add, ["cex", "lgs"], [("cin", par)])
        act(E3[:, 0:n, :], cin[:, 0:n, :], AF.Exp, [("cin", par)], ["E3"], scale=RW_LN_A)
        act(cin[:, 0:n, :], cin[:, 0:n, :], AF.Exp, [("cin", par)], [("cin", par)], scale=-RW_LN_A)
        act(cex[:, 0:n, :], cex[:, 0:n, :], AF.Exp, ["cex"], ["cex"], scale=-RW_LN_A)
        E2, E1 = cin, cex
        gcol = 63 if d == 0 else 0
        if d == 0:
            tt("dve", kkn[:, 0:n, :], ks[:, 0:n, :], bc(kkp, h0, n), ALU.mult, ["ks"], ["kkn"])
            act(t1[:, 0:n, :], kkn[:, 0:n, :], AF.Square, ["kkn"], ["t1"])
            for pr in range(n // 2):
                pl, plk = next_pl()
                P.op("pe", [mm(pl[:, q * TQ:(q + 1) * TQ], ones[:], t1[:, pr * 2 + q, :]) for q in range(2)], r=["ones64", "t1"], w=[plk])
                act(rn[:, pr * 2:pr * 2 + 2, :], pl[:].rearrange("p (q t) -> p q t", q=2), AF.Sqrt, [], ["rn", plk])
            P.op("dve", lambda e, n=n: e.tensor_scalar(out=rn[:, 0:n, :], in0=rn[:, 0:n, :], scalar1=1e-12, scalar2=None, op0=ALU.max), r=["rn"], w=["rn"])
            P.op("dve", lambda e, n=n: e.reciprocal(out=rn[:, 0:n, :], in_=rn[:, 0:n, :]), r=["rn"], w=["rn"])
            tt("dve", kkn[:, 0:n, :], kkn[:, 0:n, :], rn[:, 0:n, :], ALU.mult, ["kkn", "rn"], ["kkn"])
            P.dma("sp", "cs3", rwc[3][:, h0:h0 + n, t0:t0 + TQ], kkn[:, 0:n, :], r=["kkn"], w=[("rwc", 3, tq, hg)])
        else:
            P.dma("sp", "cl3", kkn[:, 0:n, :], rwc[3][:, h0:h0 + n, t0:t0 + TQ], r=[("rwc", 3, tq, hg)], w=["kkn"])
        P.op("dve", lambda e, n=n, v4=v4, E1=E1: e.scalar_tensor_tensor(out=AR[:, 0:n, :, 0, :], in0=v4(kkn), scalar=-1.0, in1=v4(E1), op0=ALU.mult, op1=ALU.mult), r=["kkn", "cex"], w=[("AR", par)])
        tt("dve", AR[:, 0:n, :, 1, :], v4(rs), v4(E2), ALU.mult, ["rs", ("cin", par)], [("AR", par)])
        tt("dve", t2[:, 0:n, :], kkn[:, 0:n, :], ag[:, 0:n, :], ALU.mult, ["kkn", "ag"], ["t2"])
        tt("dve", Bt[:, 0:n, :], t2[:, 0:n, :], E3[:, 0:n, :], ALU.mult, ["t2", "E3"], [("Bt", par)])
        tt("dve", t2[:, 0:n, :], ag[:, 0:n, :], bc(ka, h0, n), ALU.mult, ["ag"], ["t2"])
        tt("dve", t2[:, 0:n, :], t2[:, 0:n, :], bc(omka, h0, n), ALU.add, ["t2", "omka"], ["t2"])
        tt("dve", kd[:, 0:n, :], ks[:, 0:n, :], t2[:, 0:n, :], ALU.mult, ["ks", "t2"], ["kd"])
        tt("dve", Kt[:, 0:n, :], kd[:, 0:n, :], E3[:, 0:n, :], ALU.mult, ["kd", "E3"], [("Kt", par)])
        tt("dve", t1[:, 0:n, :], rs[:, 0:n, :], kd[:, 0:n, :], ALU.mult, ["rs", "kd"], ["t1"])
        tt("dve", t1[:, 0:n, :], t1[:, 0:n, :], bc(rk, h0, n), ALU.mult, ["t1"], ["t1"])
        for pr in range(n // 2):
            pl, plk = next_pl()
            P.op("pe", [mm(pl[:, q * TQ:(q + 1) * TQ], ones[:], t1[:, pr * 2 + q, :]) for q in range(2)], r=["ones64", "t1"], w=[plk])
            tt("dve", bst[:, pr * 2:pr * 2 + 2, :], pl[:].rearrange("p (q t) -> p q t", q=2), vs[:, pr * 2:pr * 2 + 2, :], ALU.mult, [("vs", par)], ["bst", plk])
        P.dma("sp", "bo", bd[d][:, h0:h0 + n, t0:t0 + TQ], bst[:, 0:n, :], r=["bst"])

    def chunks_visit(d, tq, hg, par, first):
        vpar = par
        t0 = tq * TQ
        h0, n = GROUPS[hg]
        vs, cin, AR, Bt, Kt = vs2[par], cin2[par], AR2[par], Bt2[par], Kt2[par]
        E2 = cin
        gcol = 63 if d == 0 else 0
        if first:
            P.op("pool", lambda e: e.memset(ST[:, h0:h0 + n, :], 0.0), r=[("ST", hg)], w=[("ST", hg)])
            act(STr[:, h0:h0 + n, :], ST[:, h0:h0 + n, :], AF.Identity, [("ST", hg)], [("STr", ("ST", hg))])
        skey = ("ST", hg)

        def precompute(c, par, n=n):
            cs = slice(c * 64, (c + 1) * 64)
            t = TS[c % 2]
            BKT, VTs, MabT, M1, M2, Q = t["BKT"], t["VTs"], t["MabT"], t["M1"], t["M2"], t["Q"]
            k = lambda nm: (nm, c % 2)
            ba, bb = 2 + 2 * par, 3 + 2 * par
            bT = B[ba][:].rearrange("p (a h k) -> p a h k", a=2, h=HGM)
            bV = B[bb][:].rearrange("p (a h k) -> p a h k", a=2, h=HGM)
            P.op("pe", [mm(bT[:, 0, hh, :], Bt[:, hh, cs], identr[:, 0, :]) for hh in range(n)] + [mm(bT[:, 1, hh, :], Kt[:, hh, cs], identr[:, 0, :]) for hh in range(n)],
                 r=[("Bt", vpar), ("Kt", vpar)], w=[("B", ba)])
            act(BKT[:, :, 0:n, :], bT[:, :, 0:n, :], AF.Identity, [], [k("BKT"), ("B", ba)])
            P.op("pe", [mm(bV[:, 0, hh, :], vs[:, hh, cs], identr[:, 0, :]) for hh in range(n)] + [mm(bV[:, 1, hh, :], AR[:, hh, c, 0, :], Bt[:, hh, cs]) for hh in range(n)],
                 r=[("vs", vpar), ("AR", vpar), ("Bt", vpar)], w=[("B", bb)])
            yield
            act(VTs[:, 0:n, :], bV[:, 0, 0:n, :], AF.Identity, [], [k("VTs"), ("B", bb)])
            tt("dve", MabT[:, 0:n, :], bV[:, 1, 0:n, :], mlt[d][:, 0:n, :], ALU.mult, [], [k("MabT"), ("B", bb)])
            g1 = B[ba][:].rearrange("p (h x) -> p h x", h=HGM)
            g2p = B[bb][:].rearrange("p (h x) -> p h x", h=HGM)
            P.op("pe", [mm(g1[:, hh, :], Bt[:, hh, cs], AR[:, hh, c, :, :].rearrange("p a t -> p (a t)")) for hh in range(n)], r=[("Bt", vpar), ("AR", vpar)], w=[("B", ba)])
            P.op("pe", [mm(g2p[:, hh, :], Kt[:, hh, cs], AR[:, hh, c, :, :].rearrange("p a t -> p (a t)")) for hh in range(n)], r=[("Kt", vpar), ("AR", vpar)], w=[("B", bb)])
            yield
            tt("dve", M1[:, 0:n, :], g1[:, 0:n, :], msi[d][:, 0:n, :], ALU.mult, [], [k("M1"), ("B", ba)])
            tt("dve", M2[:, 0:n, :], g2p[:, 0:n, :], msi[d][:, 0:n, :], ALU.mult, [], [k("M2"), ("B", bb)])
            tt("pool", Q[:, 0:n, :], M1[:, 0:n, 0:64], ident[:, 0:n, :], ALU.add, [k("M1")], [k("Q")])
            N_ap, N_key, NT_ap, NT_key = M1[:, :, 0:64], k("M1"), MabT[:], k("MabT")
            bI = B[ba][:].rearrange("p (a h k) -> p a h k", a=2, h=HGM)
            bQ = B[bb][:, 0:HGM * 64].rearrange("p (h k) -> p h k", h=HGM)
            for lvl in range(5):
                nn = t["NN"][lvl % 2]
                fns = [mm(bI[:, 0, hh, :], NT_ap[:, hh, :], N_ap[:, hh, :]) for hh in range(n)] + [mm(bI[:, 1, hh, :], N_ap[:, hh, :], NT_ap[:, hh, :]) for hh in range(n)]
                wk = [("B", ba)]
                rk_ = [N_key, NT_key]
                if lvl >= 1:
                    fns += [mm(bQ[:, hh, :], NT_ap[:, hh, :], Q[:, hh, :]) for hh in range(n)]
                    wk.append(("B", bb))
                    rk_.append(k("Q"))
                P.op("pe", fns, r=rk_, w=wk)
                yield
                act(nn[:, :, 0:n, :], bI[:, :, 0:n, :], AF.Identity, [], [k(("NN", lvl % 2)), ("B", ba)])
                if lvl >= 1:
                    tt("dve", Q[:, 0:n, :], Q[:, 0:n, :], bQ[:, 0:n, :], ALU.add, [], [k("Q"), ("B", bb)])
                N_ap, N_key, NT_ap, NT_key = nn[:, 0, :, :], k(("NN", lvl % 2)), nn[:, 1, :, :], k(("NN", lvl % 2))
                yield
            P.op("pe", [mm(bQ[:, hh, :], NT_ap[:, hh, :], Q[:, hh, :]) for hh in range(n)], r=[NT_key, k("Q")], w=[("B", bb)])
            yield
            tt("dve", Q[:, 0:n, :], Q[:, 0:n, :], bQ[:, 0:n, :], ALU.add, [], [k("Q"), ("B", bb)])

        def sequential(c, par, n=n, h0=h0, skey=skey):
            t = TS[c % 2]
            BKT, VTs, M1, M2, Q = t["BKT"], t["VTs"], t["M1"], t["M2"], t["Q"]
            k = lambda nm: (nm, c % 2)
            s1 = B[6][:].rearrange("p (a h k) -> p a h k", a=2, h=HGM)
            s2 = B[7][:].rearrange("p (a h k) -> p a h k", a=2, h=HGM)
            fns = []
            for hh in range(n):
                fns.append(mm(s1[:, 0, hh, :], AR[:, hh, c, 0, :], STr[:, h0 + hh, :], start=(hh == 0), stop=False))
                fns.append(mm(s1[:, 0, hh, :], M2[:, hh, 0:64], VTs[:, hh, :], start=False, stop=True))
            P.op("pe", fns, r=[("AR", vpar), ("STr", skey), k("M2"), k("VTs")], w=[("B", 6)])
            yield
            act(XT[:, 0:n, :], s1[:, 0, 0:n, :], AF.Identity, [], ["XT", ("B", 6)])
            P.op("pe", [mm(s1[:, 1, hh, :], Q[:, hh, :], XT[:, hh, :], start=False, stop=True) for hh in range(n)], r=[k("Q"), "XT"], w=[("B", 6)])
            yield
            act(UT[:, 0:n, :], s1[:, 1, 0:n, :], AF.Identity, [], ["UT", ("B", 6)])
            fns = []
            for hh in range(n):
                fns.append(mm(s2[:, 0, hh, :], STr[:, h0 + hh, :], AR[:, hh, c, 1, :], start=(hh == 0), stop=False))
                fns.append(mm(s2[:, 0, hh, :], UT[:, hh, :], M1[:, hh, 64:128], start=False, stop=False))
                fns.append(mm(s2[:, 0, hh, :], VTs[:, hh, :], M2[:, hh, 64:128], start=False, stop=True))
            for hh in range(n):
                fns.append(mm(s2[:, 1, hh, :], identr[:, 0, :], STr[:, h0 + hh, :], start=False, stop=False))
                fns.append(mm(s2[:, 1, hh, :], BKT[:, 0, hh, :], UT[:, hh, :], start=False, stop=False))
                fns.append(mm(s2[:, 1, hh, :], BKT[:, 1, hh, :], VTs[:, hh, :], start=False, stop=True))
            P.op("pe", fns, r=[("AR", vpar), ("STr", skey), k("M1"), k("M2"), "UT", k("VTs"), k("BKT"), "identr"], w=[("B", 7)])
            yield
            tt("dve", STr[:, h0:h0 + n, :], s2[:, 1, 0:n, :], E2[:, 0:n, c * 64 + gcol:c * 64 + gcol + 1].to_broadcast([64, n, 64]), ALU.mult, [("cin", vpar)], [("STr", skey), ("B", 7)])
            act(Yst[:, 0:n, c * 64:(c + 1) * 64], s2[:, 0, 0:n, :], AF.Identity, [], ["Yst", ("B", 7)])
            yield

        corder = [0, 1, 2, 3] if d == 0 else [3, 2, 1, 0]

        def rr(gens):
            while gens:
                for gnr in list(gens):
                    try:
                        next(gnr)
                    except StopIteration:
                        gens.remove(gnr)
                    K.drip()

        def seq_pair(cl):
            for cc in cl:
                for _ in sequential(cc, 0):
                    yield
        pairs = [corder[0:2], corder[2:4]]
        if RW_SKIP_CHUNKS:
            return
        rr([precompute(cc, j) for j, cc in enumerate(pairs[0])])
        rr([seq_pair(pairs[0])])
        rr([precompute(cc, j) for j, cc in enumerate(pairs[1])])
        rr([seq_pair(pairs[1])])
        P.dma("sp", "yo", yd[d][:, h0:h0 + n, t0:t0 + TQ], Yst[:, 0:n, :], r=["Yst"])

    visits = [(d, tq, hg) for d in range(2) for tq in tq_order_of(d) for hg in range(len(GROUPS))]
    prep_visit(*visits[0], 0, True)
    for vi, v in enumerate(visits):
        if vi + 1 < len(visits):
            nv = visits[vi + 1]
            K.set_drip(K.record(lambda: prep_visit(nv[0], nv[1], nv[2], (vi + 1) % 2, nv[2] == 0)), 100)
        chunks_visit(v[0], v[1], v[2], vi % 2, v[1] == tq_order_of(v[0])[0])
        K.flush()
    P.barrier()
    EPS = [dict(Y=rs, Y1=ks, B0=ep_b0, B1=t1, G=t2, yc=kkn, sq=rn, sd=kd, out=Yst),
           dict(Y=lgs, Y1=ag, B0=cex, B1=E3, G=gst, yc=bst, sq=cin2[0], sd=cin2[1], out=zl[0])]

    def ep_gen(tq, hg, S):
        t0 = tq * TQ
        h0, n = GROUPS[hg]
        b_ = EPS[S]
        k = lambda nm: ("ep", nm, S)
        Y, Y1, B0, B1, G, yc, sq, sd, out = (b_[x] for x in ("Y", "Y1", "B0", "B1", "G", "yc", "sq", "sd", "out"))
        srcs5 = (yd[0], yd[1], bd[0], bd[1], gd)
        for i5, (dst5, nm5) in enumerate(((Y, "Y"), (Y1, "Y1"), (B0, "B0"), (B1, "B1"), (G, "G"))):
            P.dma("sp", "ep%d_%d" % (i5, S), dst5[:, 0:n, 0:TQ], srcs5[i5][:, h0:h0 + n, t0:t0 + TQ], w=[k(nm5)])
        yield
        tt("dve", Y[:, 0:n, 0:TQ], Y[:, 0:n, 0:TQ], Y1[:, 0:n, 0:TQ], ALU.add, [k("Y"), k("Y1")], [k("Y")])
        tt("dve", B0[:, 0:n, 0:TQ], B0[:, 0:n, 0:TQ], B1[:, 0:n, 0:TQ], ALU.add, [k("B0"), k("B1")], [k("B0")])
        for pr in range(n // 2):
            sl = slice(pr * 2, pr * 2 + 2)
            pl, plk = next_pl()
            P.op("pe", [mm(pl[:, q * TQ:(q + 1) * TQ], onesm[:], Y[:, pr * 2 + q, 0:TQ]) for q in range(2)], r=["onesm", k("Y")], w=[plk])
            yield
            tt("dve", yc[:, sl, 0:TQ], Y[:, sl, 0:TQ], pl[:].rearrange("p (q t) -> p q t", q=2), ALU.subtract, [k("Y")], [k("yc"), plk])
        act(sq[:, 0:n, 0:TQ], yc[:, 0:n, 0:TQ], AF.Square, [k("yc")], [k("sq")])
        yield
        for pr in range(n // 2):
            sl = slice(pr * 2, pr * 2 + 2)
            pl, plk = next_pl()
            P.op("pe", [mm(pl[:, q * TQ:(q + 1) * TQ], onesm[:], sq[:, pr * 2 + q, 0:TQ]) for q in range(2)], r=["onesm", k("sq")], w=[plk])
            yield
            act(sd[:, sl, 0:TQ], pl[:].rearrange("p (q t) -> p q t", q=2), AF.Sqrt, ["epsg"], [k("sd"), plk], bias=epsg[:, 0:1])
        yield
        P.op("dve", lambda e, n=n, sd=sd: e.reciprocal(out=sd[:, 0:n, 0:TQ], in_=sd[:, 0:n, 0:TQ]), r=[k("sd")], w=[k("sd")])
        tt("dve", yc[:, 0:n, 0:TQ], yc[:, 0:n, 0:TQ], sd[:, 0:n, 0:TQ], ALU.mult, [k("yc"), k("sd")], [k("yc")])
        yield
        tt("dve", yc[:, 0:n, 0:TQ], yc[:, 0:n, 0:TQ], bc(lnw, h0, n), ALU.mult, [k("yc")], [k("yc")])
        tt("dve", yc[:, 0:n, 0:TQ], yc[:, 0:n, 0:TQ], bc(lnb, h0, n), ALU.add, [k("yc")], [k("yc")])
        yield
        tt("dve", yc[:, 0:n, 0:TQ], yc[:, 0:n, 0:TQ], B0[:, 0:n, 0:TQ], ALU.add, [k("yc"), k("B0")], [k("yc")])
        tt("dve", out[:, 0:n, 0:TQ], yc[:, 0:n, 0:TQ], G[:, 0:n, 0:TQ], ALU.mult, [k("yc"), k("G")], [k("out")])
        P.dma("sp", "oo%d" % S, oexv[:, h0:h0 + n, t0:t0 + TQ], out[:, 0:n, 0:TQ], r=[k("out")])

    ep_visits = [(tq, hg) for tq in range(NTQ) for hg in range(len(GROUPS))]
    for i0 in range(0, len(ep_visits), 2):
        gens = [ep_gen(tq, hg, S) for S, (tq, hg) in enumerate(ep_visits[i0:i0 + 2])]
        while gens:
            for gnr in list(gens):
                try:
                    next(gnr)
                except StopIteration:
                    gens.remove(gnr)
    K.done()


def rw_stage_consts():
    TQ = 256
    j = np.arange(64)[:, None]
    t = np.arange(64)[None, :]
    rep = lambda m: np.ascontiguousarray(np.broadcast_to(m[:, None, :].astype(np.float32), (64, 4, m.shape[1])))
    msi = np.stack([rep(np.concatenate([j < t, j <= t], 1)), rep(np.concatenate([j > t, j >= t], 1))])
    mlt = np.stack([rep(j > t), rep(j < t)])
    resetm = np.ones((64, 4 * TQ), np.float32)
    resetm[:, ::64] = 0.0
    return {"rw_ident4": rep(np.eye(64)), "rw_maskSI": np.ascontiguousarray(msi), "rw_maskLT": np.ascontiguousarray(mlt), "rw_resetm": resetm}


def rw_stage_params(inp_rw, hf):
    H = 10
    hs = slice(hf * H * 64, (hf + 1) * H * 64)
    pc = lambda v: np.ascontiguousarray(np.asarray(v, np.float32).reshape(-1, 64).T)
    mu = np.zeros((64, 2, 37), np.float32)
    for i, m in enumerate((inp_rw["rw_mu_prev"], inp_rw["rw_mu_next"])):
        mu[:, i, 0:10] = pc(m[0:1280][hs]); mu[:, i, 10:20] = pc(m[1280:2560][hs]); mu[:, i, 20:30] = pc(m[2560:3840][hs])
        mu[:, i, 30:32] = pc(m[3840:3968]); mu[:, i, 32:34] = pc(m[3968:4096]); mu[:, i, 34:37] = pc(m[4096:4288])
    two = lambda a: np.ascontiguousarray(np.stack([pc(a[0][hs]), pc(a[1][hs])], 1))
    return {
        "rw_mu": mu, "rw_w0": two(inp_rw["rw_w0"]), "rw_a0": two(inp_rw["rw_a0"]),
        "rw_w2": np.ascontiguousarray(np.asarray(inp_rw["rw_w2"], np.float32)[:, :, hs].transpose(1, 0, 2)),
        "rw_a2": np.ascontiguousarray(np.asarray(inp_rw["rw_a2"], np.float32)[:, :, hs].transpose(1, 0, 2)),
        "rw_g2": np.ascontiguousarray(np.asarray(inp_rw["rw_g2"], np.float32)[:, hs].reshape(3, 64, 640).transpose(1, 0, 2)),
        "rw_kk": pc(inp_rw["rw_k_k"][hs]), "rw_ka": pc(inp_rw["rw_k_a"][hs]), "rw_rk": np.ascontiguousarray(np.asarray(inp_rw["rw_r_k"], np.float32)[hf * H:(hf + 1) * H].T),
        "rw_lnw": pc(inp_rw["rw_ln_w"][hs]), "rw_lnb": pc(inp_rw["rw_ln_b"][hs]),
    }


def rw_stage_dram(nc):
    sh = {"rw_mu": [64, 2, 37], "rw_w0": [64, 2, 10], "rw_a0": [64, 2, 10], "rw_w2": [64, 2, 640], "rw_a2": [64, 2, 640], "rw_g2": [64, 3, 640],
          "rw_kk": [64, 10], "rw_ka": [64, 10], "rw_rk": [64, 10], "rw_lnw": [64, 10], "rw_lnb": [64, 10],
          "rw_ident4": [64, 4, 64], "rw_maskSI": [2, 64, 4, 128], "rw_maskLT": [2, 64, 4, 64], "rw_resetm": [64, 1024]}
    aps = {k: dram_reg(nc, k, v, F32, "ExternalInput") for k, v in sh.items()}
    prm = {k[3:]: v for k, v in aps.items()}
    prm["maskSI"] = [aps["rw_maskSI"][d] for d in range(2)]
    prm["maskLT"] = [aps["rw_maskLT"][d] for d in range(2)]
    return prm


ZCR = 512
OCR = 256


def gathered_pieces(rank, r0, n, R, CR):
    out = []
    r = r0
    while r < r0 + n:
        c = r // CR
        nc_rows = min(CR, R - c * CR)
        cnt = min(r0 + n, c * CR + nc_rows) - r
        out.append((c * 2 * CR + rank * nc_rows + (r - c * CR), cnt, r - r0))
        r += cnt
    return out


def stage_repack(nc, P, zg, zm, SH, sel_d):
    K = MixKernel(nc, P)
    T = SEQ
    R = 2 * SH
    sel = K.sb("selR", [128, 2], F32)
    P.dma("sp", "cst", sel[:], sel_d, w=["sel"])
    zero = K.sb("zeroR", [128, 2], F32)
    P.op("pool", lambda e: e.memset(zero[:], 0.0), w=["zero"])
    ta = [K.sb("rpa%d" % i, [128, 1024], F32) for i in range(3)]
    tb = [K.sb("rpb%d" % i, [128, 1024], F32) for i in range(3)]
    for c0 in range(0, SH, 128):
        n = min(128, SH - c0)
        P.dma("sp", "pad0", zm[c0:c0 + n, 0:1], zero[0:n, 0:1], r=["zero"], slow=True)
        P.dma("act", "pad1", zm[c0:c0 + n, T + 1:T + 2], zero[0:n, 1:2], r=["zero"], slow=True)
    for rank in range(2):
        for c0 in range(0, SH, 128):
            n = min(128, SH - c0)
            i = K.nxt("rp", 3)
            a, b = ta[i], tb[i]
            for pi, (dr, cnt, off) in enumerate(gathered_pieces(rank, c0, n, R, ZCR)):
                P.dma("sp", "rpa%d_%d" % (i, pi), a[off:off + cnt, :], zg[dr:dr + cnt, :], w=[("rpa", i, pi)])
            for pi, (dr, cnt, off) in enumerate(gathered_pieces(rank, SH + c0, n, R, ZCR)):
                P.dma("act", "rpb%d_%d" % (i, pi), b[off:off + cnt, :], zg[dr:dr + cnt, :], w=[("rpb", i, pi)])
            rk = [("rpa", i, 0), ("rpa", i, 1), ("rpb", i, 0), ("rpb", i, 1)]
            P.op("dve", lambda e, a=a, n=n: e.tensor_scalar(out=a[0:n, :], in0=a[0:n, :], scalar1=sel[0:n, 0:1], scalar2=None, op0=ALU.mult), r=["sel"], w=rk[0:2])
            P.op("dve", lambda e, a=a, b=b, n=n: e.scalar_tensor_tensor(out=a[0:n, :], in0=b[0:n, :], scalar=sel[0:n, 1:2], in1=a[0:n, :], op0=ALU.mult, op1=ALU.add),
                 r=["sel"], w=rk)
            P.dma("sp", "rpo%d" % i, zm[c0:c0 + n, 1 + rank * 1024:1 + (rank + 1) * 1024], a[0:n, :], r=[], w=rk)
    K.done()


def tok_load_sel(K, og, sel_d):
    P = K.P
    sel = K.sb("selT", [128, 2], F32)
    P.dma("sp", "vec_sel", sel[:], sel_d, w=["selT"])
    ostb = [K.sb("ostb%d" % i, [128, K.ntok], F32) for i in range(2)]
    for kc in range(K.kc):
        i = K.ost_i = (K.ost_i + 1) % 2
        a, b = K.ost[i], ostb[i]
        (dr, cnt, off), = gathered_pieces(kc // 8, (kc % 8) * 128, 128, 1024, OCR)
        P.dma("sp", "osa%d" % i, a[:], og[dr:dr + 128, 0:K.ntok], w=[("ost", i)])
        P.dma("act", "osb%d" % i, b[:], og[dr:dr + 128, K.ntok:2 * K.ntok], w=[("ostb", i)])
        P.op("act", lambda e, a=a: e.activation(out=a[:], in_=a[:], func=AF.Identity, scale=sel[:, 0:1]), r=[("ost", i), "selT"], w=[("ost", i)])
        P.op("dve", lambda e, a=a, b=b, kc=kc: e.scalar_tensor_tensor(out=K.hT[:, kc, :], in0=b[:], scalar=sel[:, 1:2], in1=a[:], op0=ALU.mult, op1=ALU.add),
             r=[("ost", i), ("ostb", i), "selT"], w=[("h", kc, tc) for tc in range(K.ntc)])


SH1, SH2 = 3520, 2176


def build_mega(upto=99):
    nc = bass.Bass("TRN2", target_bir_lowering=False)
    P = Prog(nc)
    D, F, NT, T = D_MODEL, FFN_HIDDEN, 1024, SEQ
    I = lambda name, shape: dram_reg(nc, name, shape, F32, "Internal")
    E = lambda name, shape, dt=F32: dram_reg(nc, name, shape, dt, "ExternalInput")
    z1, zg1, zm1 = I("z1", [2 * SH1, NT]), I("zg1", [-(-2 * SH1 // ZCR) * 2 * ZCR, NT]), I("zm1", [SH1, T + 2])
    z2, zg2, zm2 = I("z2", [2 * SH2, NT]), I("zg2", [-(-2 * SH2 // ZCR) * 2 * ZCR, NT]), I("zm2", [SH2, T + 2])
    oex1, og1, oex2, og2 = I("oex1", [1024, T]), I("og1", [2048, T]), I("oex2", [1024, T]), I("og2", [2048, T])
    xs_d = I("xs_d", [D, NT])
    sel = E("sel", [128, 2]); cos = E("cos", [64, T]); sin = E("sin", [64, T]); ident = E("ident", [128, 128])
    maskA = E("maskA", [128, 2 * T - 128], BF16); maskC = E("maskC", [128, 2 * T - 128], BF16)
    sink = E("sink", [128, 8]); wlog = E("wlog", [8, 128, 2048])

    def ffn_block(K, pfx):
        g = K.load_vec(pfx + "g_s", E(pfx + "g", [128, 16]))
        K.rmsnorm(pfx + "g_s", g)
        K.ffn(E(pfx + "w1", [F // 128, 128, 16, 128]), E(pfx + "w3", [F // 128, 128, 16, 128]), E(pfx + "w2", [4, 16, 128, 11, 128]))

    def ple_block(K, l):
        g = K.load_vec("pg%d_s" % l, E("pg%d" % l, [128, 16]))
        K.rmsnorm("pg%d_s" % l, g)
        K.ple(E("pwg%d" % l, [16, 128, 16, 128]), E("pwp%d" % l, [16, 128, 2, 128]), E("pT%d" % l, [256, NT]))

    def proj_block(K, l, n, zout, zg):
        g = K.load_vec("mg%d_s" % l, E("mg%d" % l, [128, 16]))
        K.rmsnorm("mg%d_s" % l, g)
        K.proj_out(E("win%d" % l, [n // 128, 128, 16, 128]), zout, n, cc=(zg, ZCR))

    K = TokKernel(NT, nc=nc, P=P)
    K.load_x(E("xT", [D, NT]))
    ffn_block(K, "L0f1")
    K.store_x(xs_d)
    proj_block(K, 0, 2 * SH1, z1, zg1)
    K.done()
    if upto <= 0:
        K0 = MixKernel(nc, P)
        tdbg = K0.sb("dbg", [128, 1024], F32)
        P.dma("sp", "dbg", tdbg[:], z1[0:128, :], w=["dbg"])
        P.dma("sp", "dbg", dram_reg(nc, "out", [D, NT], F32, "ExternalOutput")[0:128, :], tdbg[:], r=["dbg"])
        K0.done()
        return nc
    if upto <= 1:
        K0 = MixKernel(nc, P)
        tdbg = K0.sb("dbg", [128, 1024], F32)
        P.dma("sp", "dbg", tdbg[:], zg1[0:128, :], w=["dbg"])
        P.dma("sp", "dbg", dram_reg(nc, "out", [D, NT], F32, "ExternalOutput")[0:128, :], tdbg[:], r=["dbg"])
        K0.done()
        return nc
    stage_repack(nc, P, zg1, zm1, SH1, sel)
    stage_attn_A(nc, P, zm1, oex1, cos, sin, ident, maskA, 6)
    stage_rwkv(nc, P, zm1, oex1, rw_stage_dram(nc))
    P.cc("AllGather", PAIRS, oex1, og1, 1024, OCR)
    K = TokKernel(NT, nc=nc, P=P)
    K.load_x(xs_d)
    tok_load_sel(K, og1, sel)
    K.addmm_noload(E("wout0", [16, 128, 16, 128]), D)
    ffn_block(K, "L0f2")
    ple_block(K, 0)
    ffn_block(K, "L1f1")
    K.store_x(xs_d)
    proj_block(K, 1, 2 * SH2, z2, zg2)
    K.done()
    stage_repack(nc, P, zg2, zm2, SH2, sel)
    stage_attn_C(nc, P, zm2, oex2, cos, sin, ident, maskC, sink, 8)
    stage_attn_D(nc, P, zm2, oex2, cos, sin, ident, wlog, 8, 640, 512)
    P.cc("AllGather", PAIRS, oex2, og2, 1024, OCR)
    K = TokKernel(NT, nc=nc, P=P)
    K.load_x(xs_d)
    tok_load_sel(K, og2, sel)
    K.addmm_noload(E("wout1", [16, 128, 16, 128]), D)
    ffn_block(K, "L1f2")
    ple_block(K, 1)
    gf = K.load_vec("fg_s", E("fg", [128, 16]))
    K.rmsnorm("fg_s", gf, out_ap=dram_reg(nc, "out", [D, NT], F32, "ExternalOutput"))
    K.done()
    P.barrier()
    P.emit()
    return nc


def ab_col_perm():
    idx = []
    for j in range(2):
        for base in (0, 768, 1536):
            idx += list(range(base + j * 384, base + (j + 1) * 384))
        for base in (2304, 2304 + 1280, 2304 + 2560):
            idx += list(range(base + j * 640, base + (j + 1) * 640))
        idx += list(range(2304 + 3840, 2304 + 4288))
    return np.array(idx)


def cd_col_perm():
    idx = []
    for j in range(2):
        idx += list(range(j * 512, (j + 1) * 512))
        idx += list(range(1024 + j * 64, 1024 + (j + 1) * 64))
        idx += list(range(1152 + j * 64, 1152 + (j + 1) * 64))
        for base in (1280, 2304, 3328):
            idx += list(range(base + j * 512, base + (j + 1) * 512))
    return np.array(idx)


def out_row_perm(layer):
    idx = []
    for rank in range(2):
        if layer == 0:
            idx += list(range(rank * 384, (rank + 1) * 384)) + list(range(768 + rank * 640, 768 + (rank + 1) * 640))
        else:
            idx += list(range(rank * 512, (rank + 1) * 512)) + list(range(1024 + rank * 512, 1024 + (rank + 1) * 512))
    return np.array(idx)


def tile_w(w):
    K_, N_ = w.shape
    return np.ascontiguousarray(np.asarray(w, np.float32).reshape(K_ // 128, 128, N_ // 128, 128).transpose(2, 1, 0, 3))


def tile_w2(w, npass=4, fpass=11):
    F_, D_ = w.shape
    return np.ascontiguousarray(np.asarray(w, np.float32).reshape(npass, fpass, 128, D_ // 128, 128).transpose(0, 3, 2, 1, 4))


def kernel(**inp):
    inp = {k: np.asarray(v) for k, v in inp.items()}
    x = inp["x"].astype(np.float32)
    Bn, S, D = x.shape
    NTK = Bn * S
    per = NTK // NCORES
    c = lambda a: np.ascontiguousarray(a, dtype=np.float32)
    shards = lambda aT: [c(aT[:, i * per:(i + 1) * per]) for i in range(NCORES)]
    cos, sin = rope_tables()
    maskA, _ = toeplitz_mask(f_dil)
    maskC, _ = toeplitz_mask(f_win)
    com = {"cos": cos, "sin": sin, "ident": np.eye(128, dtype=np.float32), "maskA": maskA, "maskC": maskC}
    for l in range(2):
        for nm, key in (("f1", "ffn1"), ("f2", "ffn2")):
            pfx = "L%d%s" % (l, nm)
            com[pfx + "g"] = _pk(inp[key + "_norm"][l])
            com[pfx + "w1"] = tile_w(inp[key + "_w1"][l]); com[pfx + "w3"] = tile_w(inp[key + "_w3"][l]); com[pfx + "w2"] = tile_w2(inp[key + "_w2"][l])
        com["pg%d" % l] = _pk(inp["ple_norm"][l]); com["pwg%d" % l] = tile_w(inp["ple_w_gate"][l]); com["pwp%d" % l] = tile_w(inp["ple_w_proj"][l])
        com["mg%d" % l] = _pk(inp["mix_norm"][l])
    com["win0"] = tile_w(inp["ab_w_in"][0][:, ab_col_perm()])
    com["win1"] = tile_w(inp["cd_w_in"][0][:, cd_col_perm()])
    com["wout0"] = tile_w(inp["ab_w_out"][0][out_row_perm(0), :])
    com["wout1"] = tile_w(inp["cd_w_out"][0][out_row_perm(1), :])
    com["fg"] = _pk(inp["final_norm"])
    com.update(rw_stage_consts())
    rw = {k: inp[k][0] for k in inp if k.startswith("rw_")}
    xs = shards(x.reshape(NTK, D).T)
    p0 = shards(inp["p"][0].reshape(NTK, -1).T)
    p1 = shards(inp["p"][1].reshape(NTK, -1).T)
    rwp = [rw_stage_params(rw, hf) for hf in range(2)]
    ins = []
    for i in range(NCORES):
        j = i % 2
        d = dict(com, xT=xs[i], pT0=p0[i], pT1=p1[i])
        d["sel"] = c(np.broadcast_to(np.array([1.0 - j, float(j)], np.float32)[None, :], (128, 2)))
        d["sink"] = c(np.broadcast_to(inp["c_sink"][0][j][None, :], (128, 8)))
        d["wlog"] = gather_rpb(inp["d_rpb"][0][j * 8:(j + 1) * 8].astype(np.float32))
        d.update(rwp[j])
        ins.append(d)
    res = run_bass_kernel_spmd(build_mega(), ins, core_ids=list(range(NCORES))).results
    outT = np.concatenate([r["out"] for r in res], axis=1)
    return np.ascontiguousarray(outT.T).reshape(Bn, S, D).astype(np.float32)
```

```python
import numpy as np
import ml_dtypes
import concourse.bass as bass
import concourse.mybir as mybir
from concourse.bass_utils import run_bass_kernel_spmd

F32 = mybir.dt.float32
BF16 = mybir.dt.bfloat16
AF = mybir.ActivationFunctionType
ALU = mybir.AluOpType
AX = mybir.AxisListType

D_MODEL = 2048
FFN_HIDDEN = 5632
NORM_EPS = 1e-6
NCORES = 8


_DRAM_REG = {}
_UNIQ = [0]


def uniq(name):
    _UNIQ[0] += 1
    return "%s_u%d" % (name, _UNIQ[0])


def dram_reg(nc, name, shape, dt, kind):
    key = (id(nc), name)
    if key not in _DRAM_REG:
        if kind == "Internal":
            _DRAM_REG[key] = nc.dram_tensor(name, list(shape), dt).ap()
        else:
            _DRAM_REG[key] = nc.dram_tensor(name, list(shape), dt, kind=kind).ap()
    return _DRAM_REG[key]


class Prog:
    ENGS = ("pe", "act", "dve", "pool", "sp")

    def __init__(self, nc):
        self.nc = nc
        self.streams = {e: [] for e in self.ENGS}
        self.cnt = {}
        self.waited = {}
        self.last_w = {}
        self.readers = {}
        self.dma_slots = []

    def _deps(self, eng, r, w):
        evs = []
        for k in list(r) + list(w):
            ev = self.last_w.get(k)
            if ev is not None:
                evs.append(ev)
        for k in w:
            evs.extend(self.readers.get(k, ()))
        need = {}
        for (sk, c) in evs:
            if sk == "pe" and eng == "pe":
                continue
            if c > need.get(sk, 0):
                need[sk] = c
        for sk, c in need.items():
            if self.waited.get((eng, sk), 0) < c:
                self.waited[(eng, sk)] = c
                self.streams[eng].append(("wait", sk, c))

    def _commit(self, ev, r, w):
        for k in r:
            self.readers.setdefault(k, []).append(ev)
        for k in w:
            self.last_w[k] = ev
            self.readers[k] = []

    def op(self, eng, fns, r=(), w=()):
        if not isinstance(fns, (list, tuple)):
            fns = [fns]
        self._deps(eng, r, w)
        self.cnt[eng] = self.cnt.get(eng, 0) + 1
        ev = (eng, self.cnt[eng])
        self.streams[eng].append(("op", list(fns), eng, 1))
        self._commit(ev, r, w)
        return ev

    def dma(self, q, slot, out, in_, r=(), w=(), slow=False):
        sk = "dma_" + slot
        if sk not in self.cnt:
            self.cnt[sk] = 0
            self.dma_slots.append(sk)
        self._deps(q, r, w)
        self.cnt[sk] += 16
        ev = (sk, self.cnt[sk])
        if slow:
            self.streams[q].append(("op", [lambda e, out=out, in_=in_: e.dma_start(out=out, in_=in_, allow_slow_non_contiguous=True)], sk, 16))
        else:
            self.streams[q].append(("op", [lambda e, out=out, in_=in_: e.dma_start(out=out, in_=in_)], sk, 16))
        self._commit(ev, r, w)
        return ev

    def finish(self, out_events, eng="sp"):
        for (sk, c) in out_events:
            if self.waited.get((eng, sk), 0) < c:
                self.waited[(eng, sk)] = c
                self.streams[eng].append(("wait", sk, c))

    def barrier(self):
        for eng in self.ENGS:
            for sk, c in self.cnt.items():
                if c > 0 and sk != eng and self.waited.get((eng, sk), 0) < c:
                    self.waited[(eng, sk)] = c
                    self.streams[eng].append(("wait", sk, c))

    def cc(self, kind, groups, src, dst, R, CR, chunks=None):
        self.barrier()
        for c0 in range(0, R, CR):
            if chunks is not None and (c0 // CR) not in chunks:
                continue
            n = min(CR, R - c0)
            d0 = (c0 // CR) * 2 * CR
            self.cnt["cc"] = self.cnt.get("cc", 0) + 1
            k = self.cnt["cc"]
            self.streams["pool"].append(("op", [lambda e, c0=c0, n=n, d0=d0: e.collective_compute(kind, ALU.bypass, replica_groups=groups, ins=[src[c0:c0 + n, :]], outs=[dst[d0:d0 + 2 * n, :]])], "cc", 1))
            self.streams["pool"].append(("wait", "cc", k))
            self.waited[("pool", "cc")] = k
        self.barrier()

    def cc_chunk(self, kind, groups, src, dst, c0, n, CR, wait_slots):
        for sk in wait_slots:
            c = self.cnt.get(sk, 0)
            if c > self.waited.get(("pool", sk), 0):
                self.waited[("pool", sk)] = c
                self.streams["pool"].append(("wait", sk, c))
        d0 = (c0 // CR) * 2 * CR
        self.cnt["cc"] = self.cnt.get("cc", 0) + 1
        k = self.cnt["cc"]
        self.streams["pool"].append(("op", [lambda e: e.collective_compute(kind, ALU.bypass, replica_groups=groups, ins=[src[c0:c0 + n, :]], outs=[dst[d0:d0 + 2 * n, :]])], "cc", 1))
        self.streams["pool"].append(("wait", "cc", k))
        self.waited[("pool", "cc")] = k

    def emit(self):
        nc = self.nc
        import contextlib
        if not hasattr(self, "semstack"):
            self.semstack = contextlib.ExitStack()
            self.sems = {}
        for sk in list(self.cnt.keys()):
            if sk not in self.sems:
                self.sems[sk] = self.semstack.enter_context(nc.semaphore("s_" + sk))
        sems = self.sems
        with nc.Block() as block:
            def run(stream):
                def f(e):
                    for it in stream:
                        if it[0] == "wait":
                            e.wait_ge(sems[it[1]], it[2])
                        else:
                            _, fns, sk, inc = it
                            ins = None
                            for fn in fns:
                                ins = fn(e)
                            ins.then_inc(sems[sk], inc)
                return f

            block.tensor(run(self.streams["pe"]))
            block.scalar(run(self.streams["act"]))
            block.vector(run(self.streams["dve"]))
            block.gpsimd(run(self.streams["pool"]))
            block.sync(run(self.streams["sp"]))
        self.streams = {e: [] for e in self.ENGS}


class TokKernel:
    def __init__(self, ntok=1024, d=D_MODEL, nc=None, P=None):
        import contextlib
        self.shared = nc is not None
        self.nc = nc if self.shared else bass.Bass("TRN2", target_bir_lowering=False)
        self.P = P if self.shared else Prog(self.nc)
        self.st = contextlib.ExitStack()
        self.ntok = ntok
        self.d = d
        self.kc = d // 128
        self.ntc = ntok // 512
        self.uid = 0
        nc = self.nc
        self.xs = self.sb("xs", [128, self.kc, ntok], F32)
        self.hT = self.sb("hT", [128, self.kc, ntok], BF16)
        self.ones = self.sb("ones", [128, 128], F32)
        self.P.op("pool", lambda e: e.memset(self.ones[:], 1.0), w=["ones"])
        self.ones_r = self.sb("ones_r", [128, 128], mybir.dt.float32r)
        self.P.op("dve", lambda e: e.tensor_copy(out=self.ones_r[:], in_=self.ones[:]), r=["ones"], w=["ones_r"])
        self.epsT = self.sb("epsT", [128, 1], F32)
        self.P.op("pool", lambda e: e.memset(self.epsT[:], NORM_EPS), w=["eps"])
        self.sq = [self.sb("sq%d" % i, [128, 512], mybir.dt.float32r) for i in range(2)]

        self.sq_i = 0
        self.rstd = self.sb("rstd", [128, 512], F32)
        self.ps_ss = self.ps("ps_ss", [128, 512], F32)
        self.pab = [(self.ps("pa%d" % i, [128, 512], F32), self.ps("pb%d" % i, [128, 512], F32)) for i in range(2)]
        self.pab_i = 0
        self.py = [self.ps("py%d" % i, [128, 512], F32) for i in range(2)]
        self.py_i = 0
        self.wst = [self.sb("wst%d" % i, [128, 16, 128], F32) for i in range(4)]
        self.wbf = [self.sb("wbf%d" % i, [128, 16, 128], BF16) for i in range(4)]
        self.w_i = 0
        self.dmaq_i = 0
        self.outs = []
        self.fpass = 11
        self.gT = self.sb("gT", [128, self.fpass, ntok], BF16)
        self.sg = [self.sb("sg%d" % i, [128, 512], F32) for i in range(2)]
        self.sgi = 0
        self.ost = [self.sb("ost%d" % i, [128, ntok], F32) for i in range(2)]
        self.ost_i = 0
        self.pTb = self.sb("pTb", [128, 2, ntok], BF16)

    def sb(self, name, shape, dt):
        return self.st.enter_context(self.nc.sbuf_tensor(uniq(name), shape, dt))

    def ps(self, name, shape, dt):
        return self.st.enter_context(self.nc.psum_tensor(uniq(name), shape, dt))

    def dram_in(self, name, shape, dt=F32):
        return dram_reg(self.nc, name, shape, dt, "ExternalInput")

    def dram_out(self, name, shape, dt=F32):
        return dram_reg(self.nc, name, shape, dt, "ExternalOutput")

    def dq(self):
        self.dmaq_i += 1
        return ("sp", "pool")[self.dmaq_i % 2]

    def load_x(self, x_ap):
        P = self.P
        for kc in range(self.kc):
            P.dma("sp", "x%d" % (kc % 4), self.xs[:, kc, :], x_ap[kc * 128:(kc + 1) * 128, :], w=[("x", kc, tc) for tc in range(self.ntc)])
        for kc in range(self.kc):
            sk = "dma_x%d" % (kc % 4)
            for tc in range(self.ntc):
                P.last_w[("x", kc, tc)] = (sk, P.cnt[sk])

    def store_x(self, out_ap):
        P = self.P
        for kc in range(self.kc):
            ev = P.dma("sp", "xo%d" % (kc % 4), out_ap[kc * 128:(kc + 1) * 128, :], self.xs[:, kc, :], r=[("x", kc, tc) for tc in range(self.ntc)])
            self.outs.append(ev)

    def load_vec(self, name, ap_pk):
        t = self.sb(name, [128, ap_pk.shape[1]], F32)
        self.P.dma("sp", "vec_" + name, t[:], ap_pk, w=[name])
        return t

    def rmsnorm(self, gname, gt, out_ap=None):
        P = self.P
        KC = self.kc
        for tc in range(self.ntc):
            ts = slice(tc * 512, (tc + 1) * 512)
            for kc in range(KC):
                i = self.sq_i = (self.sq_i + 1) % 2
                sq = self.sq[i]
                P.op("act", lambda e, sq=sq, kc=kc, ts=ts: e.activation(out=sq[:], in_=self.xs[:, kc, ts], func=AF.Square),
                     r=[("x", kc, tc)], w=[("sq", i)])
                P.op("pe", lambda e, sq=sq, kc=kc: e.matmul(self.ps_ss[:], lhsT=self.ones_r[:], rhs=sq[:], start=(kc == 0), stop=(kc == KC - 1)),
                     r=[("sq", i), "ones_r"], w=["ps_ss"])
            P.op("act", lambda e: e.activation(out=self.rstd[:], in_=self.ps_ss[:], func=AF.Sqrt, bias=self.epsT[:, 0:1], scale=1.0 / self.d),
                 r=["ps_ss", "eps"], w=["rstd"])
            P.op("dve", lambda e: e.reciprocal(out=self.rstd[:], in_=self.rstd[:]), r=["rstd"], w=["rstd"])
            for kc in range(KC):
                if out_ap is None:
                    P.op("dve", lambda e, kc=kc, ts=ts: e.scalar_tensor_tensor(out=self.hT[:, kc, ts], in0=self.xs[:, kc, ts], scalar=gt[:, kc:kc + 1], in1=self.rstd[:], op0=ALU.mult, op1=ALU.mult),
                         r=[("x", kc, tc), "rstd", gname], w=[("h", kc, tc)])
                else:
                    P.op("dve", lambda e, kc=kc, ts=ts: e.scalar_tensor_tensor(out=self.xs[:, kc, ts], in0=self.xs[:, kc, ts], scalar=gt[:, kc:kc + 1], in1=self.rstd[:], op0=ALU.mult, op1=ALU.mult),
                         r=[("x", kc, tc), "rstd", gname], w=[("x", kc, tc)])
        if out_ap is not None:
            self.store_x(out_ap)

    def load_w(self, src, nk, ncols, q=None):
        P = self.P
        i = self.w_i = (self.w_i + 1) % 4
        P.dma(q or self.dq(), "w%d" % i, self.wst[i][:, 0:nk, 0:ncols], src, w=[("wst", i)])
        if i % 2 == 0:
            P.op("dve", lambda e, i=i, nk=nk, ncols=ncols: e.tensor_copy(out=self.wbf[i][:, 0:nk, 0:ncols], in_=self.wst[i][:, 0:nk, 0:ncols]),
                 r=[("wst", i)], w=[("wbf", i)])
        else:
            P.op("act", lambda e, i=i, nk=nk, ncols=ncols: e.activation(out=self.wbf[i][:, 0:nk, 0:ncols], in_=self.wst[i][:, 0:nk, 0:ncols], func=AF.Identity),
                 r=[("wst", i)], w=[("wbf", i)])
        return self.wbf[i], ("wbf", i)

    def ffn(self, w1, w3, w2, f=FFN_HIDDEN, fpass=11):
        P = self.P
        KC = self.kc
        nfc = f // 128
        npass = nfc // fpass
        assert npass * fpass == nfc
        gT = self.gT
        sg = self.sg
        sgi = self.sgi
        for q in range(npass):
            for fc in range(fpass):
                j = q * fpass + fc
                w1t, w1k = self.load_w(w1[j], KC, 128)
                w3t, w3k = self.load_w(w3[j], KC, 128)
                for tc in range(self.ntc):
                    ts = slice(tc * 512, (tc + 1) * 512)
                    pi = self.pab_i = (self.pab_i + 1) % 2
                    pa, pb = self.pab[pi]
                    P.op("pe", [lambda e, kc=kc, pa=pa, w1t=w1t, ts=ts: e.matmul(pa[:], lhsT=w1t[:, kc, :], rhs=self.hT[:, kc, ts], start=(kc == 0), stop=(kc == KC - 1)) for kc in range(KC)],
                         r=[w1k] + [("h", kc, tc) for kc in range(KC)], w=[("pa", pi)])
                    P.op("pe", [lambda e, kc=kc, pb=pb, w3t=w3t, ts=ts: e.matmul(pb[:], lhsT=w3t[:, kc, :], rhs=self.hT[:, kc, ts], start=(kc == 0), stop=(kc == KC - 1)) for kc in range(KC)],
                         r=[w3k] + [("h", kc, tc) for kc in range(KC)], w=[("pb", pi)])
                    sgi = self.sgi = (self.sgi + 1) % 2
                    s = sg[sgi]
                    P.op("act", lambda e, s=s, pa=pa: e.activation(out=s[:], in_=pa[:], func=AF.Silu), r=[("pa", pi)], w=[("sg", sgi)])
                    P.op("dve", lambda e, s=s, pb=pb, fc=fc, ts=ts: e.tensor_tensor(out=gT[:, fc, ts], in0=s[:], in1=pb[:], op=ALU.mult),
                         r=[("sg", sgi), ("pb", pi)], w=[("g", fc, tc)])
            for dc in range(KC):
                w2t, w2k = self.load_w(w2[q, dc], fpass, 128)
                for tc in range(self.ntc):
                    ts = slice(tc * 512, (tc + 1) * 512)
                    yi = self.py_i = (self.py_i + 1) % 2
                    py = self.py[yi]
                    P.op("pe", [lambda e, fc=fc, py=py, w2t=w2t, ts=ts: e.matmul(py[:], lhsT=w2t[:, fc, :], rhs=gT[:, fc, ts], start=(fc == 0), stop=(fc == fpass - 1)) for fc in range(fpass)],
                         r=[w2k] + [("g", fc, tc) for fc in range(fpass)], w=[("py", yi)])
                    P.op("dve", lambda e, py=py, dc=dc, ts=ts: e.scalar_tensor_tensor(out=self.xs[:, dc, ts], in0=py[:], scalar=0.5, in1=self.xs[:, dc, ts], op0=ALU.mult, op1=ALU.add),
                         r=[("py", yi), ("x", dc, tc)], w=[("x", dc, tc)])

    def load_actT(self, ap, dst, nk, keyf):
        P = self.P
        for kc in range(nk):
            i = self.ost_i = (self.ost_i + 1) % 2
            P.dma(self.dq(), "ost%d" % i, self.ost[i][:], ap[kc * 128:(kc + 1) * 128, :], w=[("ost", i)])
            P.op("pool", lambda e, i=i, kc=kc: e.tensor_copy(out=dst[:, kc, :], in_=self.ost[i][:]),
                 r=[("ost", i)], w=[keyf(kc, tc) for tc in range(self.ntc)])

    def proj_out(self, w_ap, z_ap, n, cc=None):
        P = self.P
        KC = self.kc
        c0 = 0
        while c0 < n:
            ncols = min(128, n - c0)
            wt, wk = self.load_w(w_ap[c0 // 128], KC, ncols, q=("sp" if cc is not None else None))
            for tc in range(self.ntc):
                ts = slice(tc * 512, (tc + 1) * 512)
                yi = self.py_i = (self.py_i + 1) % 2
                py = self.py[yi]
                P.op("pe", [lambda e, kc=kc, py=py, wt=wt, ts=ts, ncols=ncols: e.matmul(py[0:ncols, :], lhsT=wt[:, kc, 0:ncols], rhs=self.hT[:, kc, ts], start=(kc == 0), stop=(kc == KC - 1)) for kc in range(KC)],
                     r=[wk] + [("h", kc, tc) for kc in range(KC)], w=[("py", yi)])
                si = self.sgi = (self.sgi + 1) % 2
                sgt = self.sg[si]
                P.op("act", lambda e, sgt=sgt, py=py, ncols=ncols: e.activation(out=sgt[0:ncols, :], in_=py[0:ncols, :], func=AF.Identity),
                     r=[("py", yi)], w=[("sg", si)])
                ev = P.dma("sp", "zo%d" % si, z_ap[c0:c0 + ncols, ts], sgt[0:ncols, :], r=[("sg", si)])
                self.outs.append(ev)
            c0 += ncols
            if cc is not None and (c0 % cc[1] == 0 or c0 >= n):
                r0 = ((c0 - 1) // cc[1]) * cc[1]
                P.cc_chunk("AllGather", PAIRS, z_ap, cc[0], r0, c0 - r0, cc[1], ["dma_zo0", "dma_zo1"])

    def addmm(self, w_ap, oT_ap, k):
        self.load_actT(oT_ap, self.hT, k // 128, lambda kc, tc: ("h", kc, tc))
        self.addmm_noload(w_ap, k)

    def addmm_noload(self, w_ap, k):
        P = self.P
        nk = k // 128
        for dc in range(self.kc):
            wt, wk = self.load_w(w_ap[dc], nk, 128)
            for tc in range(self.ntc):
                ts = slice(tc * 512, (tc + 1) * 512)
                yi = self.py_i = (self.py_i + 1) % 2
                py = self.py[yi]
                P.op("pe", [lambda e, kc=kc, py=py, wt=wt, ts=ts: e.matmul(py[:], lhsT=wt[:, kc, :], rhs=self.hT[:, kc, ts], start=(kc == 0), stop=(kc == nk - 1)) for kc in range(nk)],
                     r=[wk] + [("h", kc, tc) for kc in range(nk)], w=[("py", yi)])
                P.op("dve", lambda e, py=py, dc=dc, ts=ts: e.tensor_tensor(out=self.xs[:, dc, ts], in0=py[:], in1=self.xs[:, dc, ts], op=ALU.add),
                     r=[("py", yi), ("x", dc, tc)], w=[("x", dc, tc)])

    def ple(self, wg_ap, wp_ap, pT_ap):
        P = self.P
        KC = self.kc
        self.load_actT(pT_ap, self.pTb, 2, lambda kc, tc: ("pT", kc, tc))
        for dc in range(KC):
            wgt, wgk = self.load_w(wg_ap[dc], KC, 128)
            wpt, wpk = self.load_w(wp_ap[dc], 2, 128)
            for tc in range(self.ntc):
                ts = slice(tc * 512, (tc + 1) * 512)
                pi = self.pab_i = (self.pab_i + 1) % 2
                pa, pb = self.pab[pi]
                P.op("pe", [lambda e, kc=kc, pa=pa, wgt=wgt, ts=ts: e.matmul(pa[:], lhsT=wgt[:, kc, :], rhs=self.hT[:, kc, ts], start=(kc == 0), stop=(kc == KC - 1)) for kc in range(KC)],
                     r=[wgk] + [("h", kc, tc) for kc in range(KC)], w=[("pa", pi)])
                P.op("pe", [lambda e, kc=kc, pb=pb, wpt=wpt, ts=ts: e.matmul(pb[:], lhsT=wpt[:, kc, :], rhs=self.pTb[:, kc, ts], start=(kc == 0), stop=(kc == 1)) for kc in range(2)],
                     r=[wpk] + [("pT", kc, tc) for kc in range(2)], w=[("pb", pi)])
                si = self.sgi = (self.sgi + 1) % 2
                sgt = self.sg[si]
                P.op("act", lambda e, sgt=sgt, pa=pa: e.activation(out=sgt[:], in_=pa[:], func=AF.Sigmoid), r=[("pa", pi)], w=[("sg", si)])
                P.op("dve", lambda e, sgt=sgt, pb=pb: e.tensor_tensor(out=sgt[:], in0=sgt[:], in1=pb[:], op=ALU.mult),
                     r=[("sg", si), ("pb", pi)], w=[("sg", si)])
                P.op("dve", lambda e, sgt=sgt, dc=dc, ts=ts: e.tensor_tensor(out=self.xs[:, dc, ts], in0=sgt[:], in1=self.xs[:, dc, ts], op=ALU.add),
                     r=[("sg", si), ("x", dc, tc)], w=[("x", dc, tc)])

    def done(self):
        if self.shared:
            self.P.barrier()
        else:
            self.P.finish(self.outs)
        self.P.emit()
        self.st.close()
        return self.nc


SEQ = 2048
HD = 64


class MixKernel:
    def __init__(self, nc=None, P=None):
        import contextlib
        self.shared = nc is not None
        self.nc = nc if self.shared else bass.Bass("TRN2", target_bir_lowering=False)
        self.P = P if self.shared else Prog(self.nc)
        self.st = contextlib.ExitStack()
        self.outs = []
        self.dmaq_i = 0
        self.rot = {}

    def sb(self, name, shape, dt):
        return self.st.enter_context(self.nc.sbuf_tensor(uniq(name), shape, dt))

    def ps(self, name, shape, dt):
        return self.st.enter_context(self.nc.psum_tensor(uniq(name), shape, dt))

    def dram_in(self, name, shape, dt=F32):
        return dram_reg(self.nc, name, shape, dt, "ExternalInput")

    def dram_out(self, name, shape, dt=F32):
        return dram_reg(self.nc, name, shape, dt, "ExternalOutput")

    def dq(self):
        return "sp"

    def nxt(self, name, n):
        i = self.rot[name] = (self.rot.get(name, -1) + 1) % n
        return i

    def record(self, fn):
        P = self.P
        rec = []
        P.op = lambda *a, **k: rec.append((Prog.op, a, k))
        P.dma = lambda *a, **k: rec.append((Prog.dma, a, k))
        try:
            fn()
        finally:
            del P.op
            del P.dma
        return rec

    def set_drip(self, rec, ntiles):
        self.pending = rec
        self.drip_n = max(1, -(-len(rec) // max(1, ntiles)))

    def drip(self):
        for _ in range(getattr(self, "drip_n", 0)):
            if getattr(self, "pending", None):
                m, a, k = self.pending.pop(0)
                m(self.P, *a, **k)

    def flush(self):
        while getattr(self, "pending", None):
            m, a, k = self.pending.pop(0)
            m(self.P, *a, **k)

    def attn_setup(self, cos_ap, sin_ap, ident_ap=None):
        P = self.P
        T = SEQ
        if ident_ap is not None:
            self.ident = self.sb("identA", [128, 128], F32)
            P.dma("sp", "cst", self.ident[:], ident_ap, w=["ident"])
            self.vT = [self.sb("vT%d" % i, [64, T], F32) for i in range(2)]
            self.stg = [self.sb("stgA%d" % i, [128, 512], F32) for i in range(2)]
        self._cst_fix = True
        self.cosT = self.sb("cosT", [64, T], F32)
        self.sinT = self.sb("sinT", [64, T], F32)
        P.dma("sp", "cst", self.cosT[:], cos_ap, w=["cos"])
        P.dma("sp", "cst", self.sinT[:], sin_ap, w=["sin"])
        for kname in ("cos", "sin", "ident"):
            if kname in P.last_w:
                P.last_w[kname] = ("dma_cst", P.cnt["dma_cst"])
        self.ones65 = self.sb("ones65", [128, 65], F32)
        P.op("dve", lambda e: e.memset(self.ones65[:], 1.0), w=["ones65"])
        P.op("dve", lambda e: e.tensor_copy(out=self.ones65r[:], in_=self.ones65[:, 0:1]), r=["ones65"], w=["ones65r"])
        self.ld = [[self.sb("ld%d_%d" % (i, j), [64, T], F32) for j in range(2)] for i in range(2)]
        F32R = mybir.dt.float32r
        self.Qa = [self.sb("Qa%d" % i, [65, T], F32R) for i in range(2)]
        self.Ka = [self.sb("Ka%d" % i, [65, T], F32R) for i in range(2)]
        self.ones65r = self.sb("ones65r", [128, 1], F32R)
        self.crow = [self.sb("crow%d" % i, [65, T], F32) for i in range(2)]
        self._ones65r_pending = True
        self.tmp = self.sb("ropetmp", [64, T], F32)
        self.nrm = self.sb("nrm", [65, T], F32)
        self.kmax = self.sb("kmax", [65, 1], F32)
        self.vld = [self.sb("vld%d" % i, [128, 16, 64], F32) for i in range(2)]
        self.Vb = [self.sb("Vb%d" % i, [128, 16, 65], BF16) for i in range(2)]
        self.Vb2 = None
        self.pt = [self.sb("pt%d" % i, [128, 512], BF16) for i in range(4)]
        self.pS = [self.ps("pS%d" % i, [128, 512], F32) for i in range(3)]
        self.pO = [self.ps("pO%d" % i, [128, 4, 128], F32) for i in range(2)]
        self.pN = self.ps("pN", [128, 512], F32)
        self.rec = [self.sb("rec%d" % i, [128, 4], F32) for i in range(2)]

    def rope_aug(self, dst, dkey, raw_ap, rot_ap, is_k):
        P = self.P
        i = self.nxt("ld", 2)
        a, b = self.ld[i]
        P.dma(self.dq(), "ld%da" % i, a[:], raw_ap, w=[("ld", i, 0)])
        P.dma(self.dq(), "ld%db" % i, b[:], rot_ap, w=[("ld", i, 1)])
        P.op("dve", lambda e: e.tensor_tensor(out=dst[0:64, :], in0=a[:], in1=self.cosT[:], op=ALU.mult), r=[("ld", i, 0), "cos"], w=[dkey])
        P.op("pool", lambda e: e.tensor_tensor(out=self.tmp[:], in0=b[:], in1=self.sinT[:], op=ALU.mult), r=[("ld", i, 1), "sin"], w=["ropetmp"])
        P.op("dve", lambda e: e.tensor_tensor(out=dst[0:64, :], in0=dst[0:64, :], in1=self.tmp[:], op=ALU.add), r=[dkey, "ropetmp"], w=[dkey])

    def rope_aug2(self, dst, dkey, zm, row0):
        P = self.P
        T = SEQ
        i = self.nxt("ld", 2)
        a, b = self.ld[i]
        P.dma(self.dq(), "ld%da" % i, a[:], zm[row0:row0 + 64, 1:T + 1], w=[("ld", i, 0)])
        P.dma(self.dq(), "ld%db" % i, b[0:32, :], zm[row0 + 32:row0 + 64, 1:T + 1], w=[("ld", i, 1)])
        P.dma(self.dq(), "ld%dc" % i, b[32:64, :], zm[row0:row0 + 32, 1:T + 1], w=[("ld", i, 2)])
        P.op("dve", lambda e: e.tensor_tensor(out=dst[0:64, :], in0=a[:], in1=self.cosT[:], op=ALU.mult), r=[("ld", i, 0), "cos"], w=[dkey])
        P.op("pool", lambda e: e.tensor_tensor(out=self.tmp[:], in0=b[:], in1=self.sinT[:], op=ALU.mult), r=[("ld", i, 1), ("ld", i, 2), "sin"], w=["ropetmp"])
        P.op("dve", lambda e: e.tensor_tensor(out=dst[0:64, :], in0=dst[0:64, :], in1=self.tmp[:], op=ALU.add), r=[dkey, "ropetmp"], w=[dkey])

    def load_v_T(self, zm, row0, Vt, vkey, shift=0, nblk=16):
        P = self.P
        T = SEQ
        i = self.nxt("vT", 2)
        vT = self.vT[i]
        P.dma(self.dq(), "vT%d" % i, vT[:], zm[row0:row0 + 64, 1:T + 1], w=[("vT", i)])
        P.op("act", lambda e: e.activation(out=Vt[:, :, 64:65], in_=self.ones65[:, 0:nblk, None], func=AF.Identity), r=["ones65"], w=[vkey])
        for g in range(0, nblk, 8):
            n = min(8, nblk - g)
            P.op("pe", [(lambda e, b=b, g=g: e.matmul(self.pN[:, b * 64:(b + 1) * 64], lhsT=vT[:, shift + (g + b) * 128:shift + (g + b + 1) * 128], rhs=self.ident[0:64, 0:64], start=True, stop=True)) for b in range(n)],
                 r=[("vT", i), "ident"], w=["pN"])
            P.op("act", lambda e, g=g, n=n: e.activation(out=Vt[:, g:g + n, 0:64], in_=self.pN[:, 0:n * 64].rearrange("p (b d) -> p b d", d=64), func=AF.Identity), r=[], w=[vkey, "pN"])

    def out_T(self, ost, okey, ncols, oex, row0):
        P = self.P
        for cc in range(ncols // 128):
            for g in range(4):
                P.op("pe", [(lambda e, b=b, g=g, cc=cc: e.matmul(self.pN[:, b * 128:(b + 1) * 128], lhsT=ost[:, g * 4 + b, cc * 128:(cc + 1) * 128], rhs=self.ident[:, :], start=True, stop=True)) for b in range(4)],
                     r=[okey, "ident"], w=["pN"])
                si = self.nxt("stg", 2)
                st = self.stg[si]
                P.op("act", lambda e, st=st: e.activation(out=st[:], in_=self.pN[:], func=AF.Identity), r=[], w=[("stg", si), "pN"])
                P.dma("sp", "oT%d" % si, oex[row0 + cc * 128:row0 + (cc + 1) * 128, g * 512:(g + 1) * 512], st[:], r=[("stg", si)])

    def plain_aug(self, dst, dkey, raw_ap):
        P = self.P
        P.dma(self.dq(), "pl" + str(dkey), dst[0:64, :], raw_ap, w=[dkey])

    def norms(self, src, skey):
        P = self.P
        T = SEQ
        P.op("act", lambda e: e.activation(out=self.tmp[:], in_=src[0:64, :], func=AF.Square), r=[skey], w=["ropetmp"])
        for c in range(T // 512):
            ts = slice(c * 512, (c + 1) * 512)
            P.op("pe", lambda e, ts=ts: e.matmul(self.pN[0:65, :], lhsT=self.ones65[0:64, :], rhs=self.tmp[:, ts], start=True, stop=True),
                 r=["ropetmp", "ones65"], w=["pN"])
            P.op("act", lambda e, ts=ts: e.activation(out=self.nrm[64:65, ts], in_=self.pN[64:65, :], func=AF.Sqrt), r=["pN"], w=["nrm"])

    def prep_k(self, Ka, kkey):
        P = self.P
        self.norms(Ka, kkey)
        P.op("dve", lambda e: e.tensor_reduce(out=self.kmax[64:65, 0:1], in_=self.nrm[64:65, :], axis=AX.X, op=ALU.max), r=["nrm"], w=["kmax"])
        P.op("dve", lambda e: e.tensor_scalar(out=Ka[64:65, :], in0=self.nrm[64:65, :], scalar1=0.0, scalar2=1.0, op0=ALU.mult, op1=ALU.add), r=["nrm"], w=[kkey])

    def prep_q(self, Qa, qkey):
        P = self.P
        self.norms(Qa, qkey)
        P.op("dve", lambda e: e.tensor_scalar(out=Qa[64:65, :], in0=self.nrm[64:65, :], scalar1=self.kmax[64:65, 0:1], scalar2=-1.0, op0=ALU.mult, op1=ALU.mult),
             r=["nrm", "kmax"], w=[qkey])
        cr = self.crow[qkey[1] % 2]
        P.op("act", lambda e: e.activation(out=cr[64:65, :], in_=Qa[64:65, :], func=AF.Identity), r=[qkey], w=[("crow", qkey[1] % 2)])

    def load_v(self, v_ap, shift=False):
        P = self.P
        i = self.nxt("v", 2)
        P.dma(self.dq(), "v%d" % i, self.vld[i][:], v_ap.rearrange("(b p) d -> p b d", p=128), w=[("vld", i)])
        P.op("pool", lambda e: e.memset(self.Vb[i][:, :, 64:65], 1.0), w=[("Vb", i)])
        P.op("pool", lambda e: e.tensor_copy(out=self.Vb[i][:, :, 0:64], in_=self.vld[i][:]), r=[("vld", i)], w=[("Vb", i)])
        return self.Vb[i], ("Vb", i)

    def score_group(self, segs, W_ap, wkeys, rkeys):
        P = self.P
        self.drip()
        si = self.nxt("pS", 3)
        pS = self.pS[si]
        off = 0
        fns = []
        offs = []
        for (ka, qa, nq) in segs:
            fns.append(lambda e, ka=ka, qa=qa, off=off, nq=nq: e.matmul(pS[:, off:off + nq], lhsT=ka, rhs=qa, start=True, stop=True))
            offs.append(off)
            off += nq
        P.op("pe", fns, r=rkeys, w=[("pS", si)])
        pi = self.nxt("pt", 4)
        pt = self.pt[pi]
        P.op("act", lambda e, off=off: e.activation(out=pt[:, 0:off], in_=pS[:, 0:off], func=AF.Exp, scale=0.125), r=[("pS", si)], w=[("pt", pi)])
        eng = "dve"
        P.op(eng, lambda e, off=off: e.tensor_tensor(out=pt[:, 0:off], in0=pt[:, 0:off], in1=W_ap, op=ALU.mult), r=[("pt", pi)] + wkeys, w=[("pt", pi)])
        return pt, ("pt", pi), offs

    def attn_toeplitz(self, Qa, qkey, Ka, kkey, Vb, vkey, mask, mkey, mask_off, reach, ost, okey, ocol, sink_col=None):
        P = self.P
        T = SEQ
        for qc in range(T // 512):
            q0 = qc * 512
            kbs = [kb for kb in range(T // 128) if not (kb * 128 > q0 + 511 + reach or kb * 128 + 127 < q0 - reach)]
            oi = self.nxt("pO", 2)
            pO = self.pO[oi]
            pend = []

            def emit_pv(item, pO=pO, oi=oi, kbs=kbs):
                pt, ptk, kb, n = item
                P.op("pe", [lambda e, j=j, pt=pt, kb=kb, n=n, pO=pO, kbs=kbs: e.matmul(pO[:, j, 0:65], lhsT=pt[:, j * 128:(j + 1) * 128], rhs=Vb[:, kb, :], start=(n == 0 and j == 0), stop=(n == len(kbs) - 1), skip_group_check=True) for j in range(4)],
                     r=[ptk, vkey], w=[("pO", oi)])
            for n, kb in enumerate(kbs):
                u0 = q0 - kb * 128 + mask_off
                pt, ptk, _ = self.score_group([(Ka[0:65, kb * 128:(kb + 1) * 128], Qa[0:65, q0:q0 + 512], 512)],
                                              mask[:, u0:u0 + 512], [mkey], [qkey, kkey])
                pend.append((pt, ptk, kb, n))
                if len(pend) > 2:
                    emit_pv(pend.pop(0))
            while pend:
                emit_pv(pend.pop(0))
            ri = self.nxt("rec", 2)
            rec = self.rec[ri]
            if sink_col is not None:
                P.op("pe", [lambda e, j=j, pO=pO, q0=q0: e.matmul(pO[:, j, 65:66], lhsT=self.crow[qkey[1] % 2][64:65, q0 + j * 128:q0 + (j + 1) * 128], rhs=self.ones65[64:65, 0:1], start=True, stop=True) for j in range(4)],
                     r=[("crow", qkey[1] % 2), "ones65"], w=[("pO", oi)])
                P.op("act", lambda e, rec=rec, pO=pO: e.activation(out=rec[:], in_=pO[:, :, 65], func=AF.Exp, bias=sink_col, scale=0.125), r=[("pO", oi), "sink"], w=[("rec", ri)])
                P.op("dve", lambda e, rec=rec, pO=pO: e.tensor_tensor(out=rec[:], in0=rec[:], in1=pO[:, :, 64], op=ALU.add), r=[("pO", oi), ("rec", ri)], w=[("rec", ri)])
                P.op("dve", lambda e, rec=rec: e.reciprocal(out=rec[:], in_=rec[:]), r=[("rec", ri)], w=[("rec", ri)])
            else:
                P.op("dve", lambda e, rec=rec, pO=pO: e.reciprocal(out=rec[:], in_=pO[:, :, 64]), r=[("pO", oi)], w=[("rec", ri)])
            for j in range(4):
                P.op("dve", lambda e, j=j, rec=rec, pO=pO, qc=qc: e.tensor_scalar(out=ost[:, qc * 4 + j, ocol:ocol + 64], in0=pO[:, j, 0:64], scalar1=rec[:, j:j + 1], scalar2=None, op0=ALU.mult),
                     r=[("pO", oi), ("rec", ri)], w=[okey])

    def done(self):
        if self.shared:
            self.P.barrier()
        else:
            self.P.finish(self.outs)
        self.P.emit()
        self.st.close()
        return self.nc


def rope_tables():
    half = HD // 2
    inv = 10000.0 ** (-np.arange(half, dtype=np.float64) / half)
    ang = np.arange(SEQ, dtype=np.float64)[None, :] * inv[:, None]
    cos = np.concatenate([np.cos(ang), np.cos(ang)], 0).astype(np.float32)
    sin = np.concatenate([-np.sin(ang), np.sin(ang)], 0).astype(np.float32)
    return cos, sin


def rot_half_rows(zT):
    return np.concatenate([zT[..., 32:, :], zT[..., :32, :]], axis=-2)


def toeplitz_mask(f, width=SEQ):
    off = width - 128
    u = np.arange(2 * width - 128)[None, :]
    p = np.arange(128)[:, None]
    return f(u - off - p).astype(ml_dtypes.bfloat16), off


def f_dil(d):
    a = np.abs(d)
    return ((a <= 64).astype(np.float32) + ((a % 4 == 0) & (a <= 256)) + ((a % 16 == 0) & (a <= 1024)))


def f_win(d):
    return (np.abs(d) <= 128).astype(np.float32)


def build_attn_A(nh):
    K = MixKernel()
    T = SEQ
    q = K.dram_in("q", [nh, 64, T]); qr = K.dram_in("qr", [nh, 64, T])
    k = K.dram_in("k", [nh, 64, T]); kr = K.dram_in("kr", [nh, 64, T])
    v = K.dram_in("v", [nh, T, 64])
    cos = K.dram_in("cos", [64, T]); sin = K.dram_in("sin", [64, T])
    mk = K.dram_in("mask", [128, 2 * T - 128], BF16)
    o = K.dram_out("o", [T, nh * 64])
    K.attn_setup(cos, sin)
    mask = K.sb("maskA", [128, 2 * T - 128], BF16)
    K.P.dma("sp", "mask", mask[:], mk, w=["mask"])
    ost = K.sb("ost", [128, 16, nh * 64], F32)
    for h in range(nh):
        i = h % 2
        Qa, Ka = K.Qa[i], K.Ka[i]
        K.rope_aug(Ka, ("Ka", i), k[h], kr[h], True)
        K.prep_k(Ka, ("Ka", i))
        K.rope_aug(Qa, ("Qa", i), q[h], qr[h], False)
        K.prep_q(Qa, ("Qa", i))
        Vb, vkey = K.load_v(v[h])
        K.attn_toeplitz(Qa, ("Qa", i), Ka, ("Ka", i), Vb, vkey, mask, "mask", T - 128, 1024, ost, "ost", h * 64)
    ev = K.P.dma("sp", "out", o.rearrange("(b p) c -> p b c", p=128), ost[:], r=["ost"])
    K.outs.append(ev)
    return K.done()


def build_attn_C(nq=8):
    K = MixKernel()
    T = SEQ
    q = K.dram_in("q", [nq, 64, T]); qr = K.dram_in("qr", [nq, 64, T])
    k = K.dram_in("k", [64, T]); kr = K.dram_in("kr", [64, T])
    v = K.dram_in("v", [T, 64])
    sk = K.dram_in("sink", [128, nq])
    cos = K.dram_in("cos", [64, T]); sin = K.dram_in("sin", [64, T])
    mk = K.dram_in("mask", [128, 2 * T - 128], BF16)
    o = K.dram_out("o", [T, nq * 64])
    K.attn_setup(cos, sin)
    mask = K.sb("maskC", [128, 2 * T - 128], BF16)
    K.P.dma("sp", "mask", mask[:], mk, w=["mask"])
    sink = K.sb("sinkS", [128, nq], F32)
    K.P.dma("sp", "sink", sink[:], sk, w=["sink"])
    ost = K.sb("ost", [128, 16, nq * 64], F32)
    Ka = K.Ka[0]
    K.rope_aug(Ka, ("Ka", 0), k, kr, True)
    K.prep_k(Ka, ("Ka", 0))
    Vb, vkey = K.load_v(v)
    for h in range(nq):
        i = h % 2
        Qa = K.Qa[i]
        K.rope_aug(Qa, ("Qa", i), q[h], qr[h], False)
        K.prep_q(Qa, ("Qa", i))
        K.attn_toeplitz(Qa, ("Qa", i), Ka, ("Ka", 0), Vb, vkey, mask, "mask", T - 128, 128, ost, "ost", h * 64, sink_col=sink[:, h:h + 1])
    ev = K.P.dma("sp", "out", o.rearrange("(b p) c -> p b c", p=128), ost[:], r=["ost"])
    K.outs.append(ev)
    return K.done()


GRID_W = 64
NA_ROWS = SEQ // GRID_W


def build_attn_D(nh=8):
    K = MixKernel()
    P = K.P
    T = SEQ
    q = K.dram_in("q", [nh, 64, T]); k = K.dram_in("k", [nh, 64, T])
    v = K.dram_in("v", [nh, T, 64])
    wl = K.dram_in("wlog", [nh, 128, 2048])
    cos = K.dram_in("cos", [64, T]); sin = K.dram_in("sin", [64, T])
    o = K.dram_out("o", [T, nh * 64])
    K.attn_setup(cos, sin)
    wst = [K.sb("wlst%d" % i, [128, 2048], F32) for i in range(2)]
    Wt = [K.sb("Wt%d" % i, [128, 2048], BF16) for i in range(2)]
    vld2 = [K.sb("vld2_%d" % i, [128, 15, 64], F32) for i in range(2)]
    Vb2 = [K.sb("Vb2_%d" % i, [128, 15, 65], BF16) for i in range(2)]
    ostl = [K.sb("ostD%d" % i, [64, NA_ROWS, 64], F32) for i in range(2)]
    for h in range(nh):
        i = h % 2
        Qa, Ka = K.Qa[i], K.Ka[i]
        P.dma(K.dq(), "ka%d" % i, Ka[0:64, :], k[h], w=[("Ka", i)])
        K.prep_k(Ka, ("Ka", i))
        P.dma(K.dq(), "qa%d" % i, Qa[0:64, :], q[h], w=[("Qa", i)])
        K.prep_q(Qa, ("Qa", i))
        Vb, vkey = K.load_v(v[h])
        P.dma(K.dq(), "v2_%d" % i, vld2[i][:], v[h][64:64 + 15 * 128, :].rearrange("(b p) d -> p b d", p=128), w=[("vld2", i)])
        P.op("pool", lambda e, i=i: e.memset(Vb2[i][:, :, 64:65], 1.0), w=[("Vb2", i)])
        P.op("pool", lambda e, i=i: e.tensor_copy(out=Vb2[i][:, :, 0:64], in_=vld2[i][:]), r=[("vld2", i)], w=[("Vb2", i)])
        P.dma(K.dq(), "wl%d" % i, wst[i][:], wl[h], w=[("wlst", i)])
        P.op("act", lambda e, i=i: e.activation(out=Wt[i][:], in_=wst[i][:], func=AF.Exp), r=[("wlst", i)], w=[("Wt", i)])
        for r0 in range(0, NA_ROWS, 4):
            oi = K.nxt("pO", 2)
            pO = K.pO[oi]
            for j in range(4):
                r = r0 + j
                ws = min(max(r - 4, 0), NA_ROWS - 8)
                var = ws - r + 7
                segs = [(Ka[0:65, ws * 64 + b * 128: ws * 64 + (b + 1) * 128], Qa[0:65, r * 64:(r + 1) * 64], 64) for b in range(4)]
                pt, ptk, _ = K.score_group(segs, Wt[i][:, var * 256:(var + 1) * 256], [("Wt", i)], [("Qa", i), ("Ka", i)])
                if ws % 2 == 0:
                    vt, vk, b0 = Vb, vkey, ws // 2
                else:
                    vt, vk, b0 = Vb2[i], ("Vb2", i), (ws - 1) // 2
                P.op("pe", [lambda e, b=b, pt=pt, pO=pO, j=j, vt=vt, b0=b0: e.matmul(pO[0:64, j, 0:65], lhsT=pt[:, b * 64:(b + 1) * 64], rhs=vt[:, b0 + b, :], start=(b == 0 and j == 0), stop=(b == 3), skip_group_check=True) for b in range(4)],
                     r=[ptk, vk], w=[("pO", oi)])
            ri = K.nxt("rec", 2)
            rec = K.rec[ri]
            P.op("dve", lambda e, rec=rec, pO=pO: e.reciprocal(out=rec[0:64, :], in_=pO[0:64, :, 64]), r=[("pO", oi)], w=[("rec", ri)])
            for j in range(4):
                P.op("dve", lambda e, j=j, rec=rec, pO=pO, r0=r0, h=h: e.tensor_scalar(out=ostl[h % 2][:, r0 + j, :], in0=pO[0:64, j, 0:64], scalar1=rec[0:64, j:j + 1], scalar2=None, op0=ALU.mult),
                     r=[("pO", oi), ("rec", ri)], w=[("ostD", i)])
        K.outs.append(P.dma("sp", "out%d" % i, o.rearrange("(r c) n -> c r n", c=64)[:, :, h * 64:(h + 1) * 64], ostl[i][:], r=[("ostD", i)]))
    return K.done()


def gather_rpb(rpb):
    nh = rpb.shape[0]
    dkr = np.arange(2)[:, None, None, None, None]
    kc = np.arange(64)[None, :, None, None, None]
    var = np.arange(8)[None, None, :, None, None]
    blk = np.arange(4)[None, None, None, :, None]
    c = np.arange(64)[None, None, None, None, :]
    ri = (var - 7) + 2 * blk + dkr + 7
    qc0 = np.clip(c - 8, 0, GRID_W - 16)
    inwin = (kc >= qc0) & (kc < qc0 + 16)
    ci = np.clip(kc - c, -15, 15) + 15
    ri_b, ci_b, in_b = np.broadcast_arrays(ri, ci, inwin)
    g = rpb[:, ri_b, ci_b]
    g = np.where(in_b[None], g, np.float32(-30000.0)).astype(np.float32)
    return np.ascontiguousarray(g.reshape(nh, 128, 8 * 4 * 64))


RW_H = 20
RW_SKIP_CHUNKS = False
RW_LN_A = 0.6065306597126334


def build_rwkv(ntq_lim=None, ng_lim=None, nch_lim=None, nlvl=5, stage=9):
    K = MixKernel()
    P = K.P
    T = SEQ
    H, HG, TQ = RW_H, 4, 256
    NG, NTQ, NCH = H // HG, T // TQ, TQ // 64
    C = H * 64
    zr = K.dram_in("zr", [64, H, T + 2]); zk = K.dram_in("zk", [64, H, T + 2]); zv = K.dram_in("zv", [64, H, T + 2])
    zwl = K.dram_in("zwl", [64, T + 2]); zal = K.dram_in("zal", [64, T + 2]); zgl = K.dram_in("zgl", [64, 3, T + 2])
    mu_d = K.dram_in("mu", [64, 2, 65])
    w0_d = K.dram_in("w0", [64, H]); w2_d = K.dram_in("w2", [64, C]); a0_d = K.dram_in("a0", [64, H]); a2_d = K.dram_in("a2", [64, C])
    g2_d = K.dram_in("g2", [64, 3, C]); kk_d = K.dram_in("kk", [64, H]); ka_d = K.dram_in("ka", [64, H]); rk_d = K.dram_in("rk", [64, H])
    id_d = K.dram_in("ident", [64, 4, 64]); msi_d = K.dram_in("maskSI", [64, 4, 128]); mlt_d = K.dram_in("maskLT", [64, 4, 64])
    rs_d = K.dram_in("resetm", [64, 4 * TQ])
    y_o = K.dram_out("y", [T, C]); bo_o = K.dram_out("bonus", [64, H, T]); g_o = K.dram_out("g", [64, H, T])

    def const(name, ap, shape):
        t = K.sb(name, shape, F32)
        P.dma(K.dq(), "c_" + name, t[:], ap, w=[name])
        return t
    mu = const("mu_s", mu_d, [64, 2, 65]); w0 = const("w0_s", w0_d, [64, H]); w2 = const("w2_s", w2_d, [64, C])
    a0 = const("a0_s", a0_d, [64, H]); a2 = const("a2_s", a2_d, [64, C]); g2 = const("g2_s", g2_d, [64, 3, C])
    kkp = const("kk_s", kk_d, [64, H]); ka = const("ka_s", ka_d, [64, H]); rk = const("rk_s", rk_d, [64, H])
    ident = const("ident_s", id_d, [64, 4, 64]); msi = const("msi_s", msi_d, [64, 4, 128]); mlt = const("mlt_s", mlt_d, [64, 4, 64])
    resetm = const("resetm_s", rs_d, [64, 4 * TQ])
    ones = K.sb("ones64", [64, 64], F32)
    P.op("pool", lambda e: e.memset(ones[:], 1.0), w=["ones64"])
    c0 = K.sb("c0", [64, 65], F32)
    P.op("dve", lambda e: e.tensor_tensor(out=c0[:], in0=mu[:, 0, :], in1=mu[:, 1, :], op=ALU.add), r=["mu_s"], w=["c0"])
    P.op("dve", lambda e: e.tensor_scalar(out=c0[:], in0=c0[:], scalar1=-1.0, scalar2=1.0, op0=ALU.mult, op1=ALU.add), r=["c0"], w=["c0"])
    omka = K.sb("omka", [64, H], F32)
    P.op("dve", lambda e: e.tensor_scalar(out=omka[:], in0=ka[:], scalar1=-1.0, scalar2=1.0, op0=ALU.mult, op1=ALU.add), r=["ka_s"], w=["omka"])
    ST = K.sb("ST", [64, H, 64], F32)
    P.op("pool", lambda e: e.memset(ST[:], 0.0), w=[("ST", g) for g in range(NG)])

    A3 = [64, HG, TQ]
    def arr(name):
        return K.sb(name, A3, F32)
    zl = [[K.sb("zl%d_%d" % (i, j), [64, HG, TQ + 2], F32) for j in range(3)] for i in range(2)]
    rs, ks, vs = arr("rs"), arr("ks"), arr("vs")
    t1, t2 = arr("t1"), arr("t2")
    lgs, ag, cin, cex, E3 = arr("lgs"), arr("ag"), arr("cin"), arr("cex"), arr("E3")
    kkn, rn, kd = arr("kkn"), arr("rn"), arr("kd")
    AR = K.sb("AR", [64, HG, NCH, 2, 64], F32)
    Bt, Kt = arr("Bt"), arr("Kt")
    gst = [arr("gst%d" % i) for i in range(2)]
    bst = [arr("bst%d" % i) for i in range(2)]
    lw = [K.sb("lw%d" % i, [64, TQ + 2], F32) for i in range(2)]
    lg3 = K.sb("lg3", [64, 3, TQ + 2], F32)
    twl = K.sb("twl", [64, TQ], F32); als = K.sb("als", [64, TQ], F32); sgl = K.sb("sgl", [64, 3, TQ], F32)
    Yst = [K.sb("Yst%d" % i, [64, NCH, HG * 64], F32) for i in range(2)]
    B = [K.ps("B%d" % i, [64, 512], F32) for i in range(8)]

    def sm(name, shape):
        return K.sb(name, shape, F32)
    TS = []
    for par in range(2):
        sfx = "_%d" % par
        TS.append(dict(BKT=sm("BKT" + sfx, [64, 2, HG, 64]), VTs=sm("VTs" + sfx, [64, HG, 64]), MabT=sm("MabT" + sfx, [64, HG, 64]),
                       M1=sm("M1" + sfx, [64, HG, 128]), M2=sm("M2" + sfx, [64, HG, 128]),
                       Nb=[sm("Nb%d%s" % (i, sfx), [64, HG, 64]) for i in range(2)], NTb=[sm("NTb%d%s" % (i, sfx), [64, HG, 64]) for i in range(2)],
                       Q=sm("Qs" + sfx, [64, HG, 64])))
    XT = sm("XTs", [64, HG, 64]); UT = sm("UTs", [64, HG, 64]); tS = sm("tS", [64, HG, 64])

    def tt(eng, out, a, b, op, r, w):
        P.op(eng, lambda e: e.tensor_tensor(out=out, in0=a, in1=b, op=op), r=r, w=w)

    def act(out, in_, func, r, w, **kw):
        P.op("act", lambda e: e.activation(out=out, in_=in_, func=func, **kw), r=r, w=w)

    def bc(t, col0, n=HG, width=TQ):
        return t[:, col0:col0 + n, None].to_broadcast([64, n, width])

    def shift1(dst, src, col, r, w):
        P.op("dve", lambda e: e.tensor_scalar(out=dst, in0=src[:, 1:TQ + 1], scalar1=c0[:, col:col + 1], scalar2=None, op0=ALU.mult), r=r + ["c0"], w=w)
        P.op("dve", lambda e: e.scalar_tensor_tensor(out=dst, in0=src[:, 0:TQ], scalar=mu[:, 0, col:col + 1], in1=dst, op0=ALU.mult, op1=ALU.add), r=r + w + ["mu_s"], w=w)
        P.op("dve", lambda e: e.scalar_tensor_tensor(out=dst, in0=src[:, 2:TQ + 2], scalar=mu[:, 1, col:col + 1], in1=dst, op0=ALU.mult, op1=ALU.add), r=r + w + ["mu_s"], w=w)

    def mm(out, lhsT, rhs, start=True, stop=True):
        return lambda e: e.matmul(out, lhsT=lhsT, rhs=rhs, start=start, stop=stop, skip_group_check=True)

    pl_i = [0]

    def next_pl():
        pl_i[0] = (pl_i[0] + 1) % 2
        return B[pl_i[0]], ("B", pl_i[0])

    for tq in range(NTQ if ntq_lim is None else ntq_lim):
        t0 = tq * TQ
        i = tq % 2
        P.dma(K.dq(), "lw%d" % i, lw[i][:], zwl[:, t0:t0 + TQ + 2], w=[("lw", i)])
        shift1(twl[:], lw[i], 60, [("lw", i)], ["twl"])
        act(twl[:], twl[:], AF.Tanh, ["twl"], ["twl"])
        j = 1 - i
        P.dma(K.dq(), "lwb%d" % i, lw[j][:], zal[:, t0:t0 + TQ + 2], w=[("lw", j)])
        shift1(als[:], lw[j], 61, [("lw", j)], ["als"])
        P.dma(K.dq(), "lg3", lg3[:], zgl[:, :, t0:t0 + TQ + 2], w=["lg3"])
        for jj in range(3):
            shift1(sgl[:, jj, :], lg3[:, jj, :], 62 + jj, ["lg3"], [("sgl", jj)])
            act(sgl[:, jj, :], sgl[:, jj, :], AF.Sigmoid, [("sgl", jj)], [("sgl", jj)])
        for hg in range(NG if ng_lim is None else ng_lim):
            h0 = hg * HG
            zi = K.nxt("zl", 2)
            srcs = (zr, zk, zv)
            dsts = (rs, ks, vs)
            for a in range(3):
                P.dma(K.dq(), "zl%d_%d" % (zi, a), zl[zi][a][:], srcs[a][:, h0:h0 + HG, t0:t0 + TQ + 2], w=[("zl", zi, a)])
                z = zl[zi][a]
                col = a * H + h0
                nm = ("rs", "ks", "vs")[a]
                tt("dve", t1[:], z[:, :, 1:TQ + 1], bc(c0, col), ALU.mult, [("zl", zi, a), "c0"], ["t1"])
                tt("pool", t2[:], z[:, :, 0:TQ], bc(mu[:, 0, :], col), ALU.mult, [("zl", zi, a), "mu_s"], ["t2"])
                tt("dve", t1[:], t1[:], t2[:], ALU.add, ["t1", "t2"], ["t1"])
                tt("pool", t2[:], z[:, :, 2:TQ + 2], bc(mu[:, 1, :], col), ALU.mult, [("zl", zi, a), "mu_s"], ["t2"])
                tt("dve", dsts[a][:], t1[:], t2[:], ALU.add, ["t1", "t2"], [nm])
            for pr in range(HG // 2):
                for (wt, wk, src, skey, bias, bkey, dst, dkey) in ((w2, "w2_s", twl, "twl", w0, "w0_s", lgs, "lgs"), (a2, "a2_s", als, "als", a0, "a0_s", ag, "ag")):
                    pl, plk = next_pl()
                    P.op("pe", [mm(pl[:, q * TQ:(q + 1) * TQ], wt[:, (h0 + pr * 2 + q) * 64:(h0 + pr * 2 + q + 1) * 64], src[:]) for q in range(2)], r=[wk, skey], w=[plk])
                    for q in range(2):
                        hh = pr * 2 + q
                        act(dst[:, hh, :], pl[:, q * TQ:(q + 1) * TQ], AF.Sigmoid, [plk, bkey], [dkey], bias=bias[:, h0 + hh:h0 + hh + 1])
                pl, plk = next_pl()
                fns = []
                for q in range(2):
                    h = h0 + pr * 2 + q
                    for jj in range(3):
                        fns.append(mm(pl[:, q * TQ:(q + 1) * TQ], g2[:, jj, h * 64:(h + 1) * 64], sgl[:, jj, :], start=(jj == 0 and q == 0), stop=(jj == 2)))
                P.op("pe", fns, r=["g2_s"] + [("sgl", jj) for jj in range(3)], w=[plk])
                gi = K.nxt("gst", 2) if pr == 0 else K.rot["gst"]
                act(gst[gi][:, pr * 2:pr * 2 + 2, :], pl[:].rearrange("p (q t) -> p q t", q=2), AF.Identity, [plk], [("gst", gi)])
            K.outs.append(P.dma("sp", "go%d" % gi, g_o[:, h0:h0 + HG, t0:t0 + TQ], gst[gi][:], r=[("gst", gi)]))
            f2 = lambda t: t[:].rearrange("p h t -> p (h t)")
            P.op("dve", lambda e: e.tensor_tensor_scan(out=f2(cin), data0=resetm[:], data1=f2(lgs), initial=0.0, op0=ALU.mult, op1=ALU.add), r=["lgs", "resetm_s"], w=["cin"])
            tt("pool", cex[:], cin[:], lgs[:], ALU.subtract, ["cin", "lgs"], ["cex"])
            act(E3[:], cin[:], AF.Exp, ["cin"], ["E3"], scale=RW_LN_A)
            act(cin[:], cin[:], AF.Exp, ["cin"], ["cin"], scale=-RW_LN_A)
            act(cex[:], cex[:], AF.Exp, ["cex"], ["cex"], scale=-RW_LN_A)
            E2, E1 = cin, cex
            tt("pool", kkn[:], ks[:], bc(kkp, h0), ALU.mult, ["ks", "kk_s"], ["kkn"])
            act(t1[:], kkn[:], AF.Square, ["kkn"], ["t1"])
            for pr in range(HG // 2):
                pl, plk = next_pl()
                P.op("pe", [mm(pl[:, q * TQ:(q + 1) * TQ], ones[:], t1[:, pr * 2 + q, :]) for q in range(2)], r=["ones64", "t1"], w=[plk])
                act(rn[:, pr * 2:pr * 2 + 2, :], pl[:].rearrange("p (q t) -> p q t", q=2), AF.Sqrt, [plk], ["rn"])
            P.op("dve", lambda e: e.tensor_scalar(out=rn[:], in0=rn[:], scalar1=1e-12, scalar2=None, op0=ALU.max), r=["rn"], w=["rn"])
            P.op("dve", lambda e: e.reciprocal(out=rn[:], in_=rn[:]), r=["rn"], w=["rn"])
            tt("dve", kkn[:], kkn[:], rn[:], ALU.mult, ["kkn", "rn"], ["kkn"])
            v4 = lambda t: t[:].rearrange("p h (c t) -> p h c t", t=64)
            P.op("dve", lambda e: e.scalar_tensor_tensor(out=AR[:, :, :, 0, :], in0=v4(kkn), scalar=-1.0, in1=v4(E1), op0=ALU.mult, op1=ALU.mult), r=["kkn", "cex"], w=["AR"])
            tt("pool", AR[:, :, :, 1, :], v4(rs), v4(E2), ALU.mult, ["rs", "cin"], ["AR"])
            tt("pool", t2[:], kkn[:], ag[:], ALU.mult, ["kkn", "ag"], ["t2"])
            tt("dve", Bt[:], t2[:], E3[:], ALU.mult, ["t2", "E3"], ["Bt"])
            tt("pool", t2[:], ag[:], bc(ka, h0), ALU.mult, ["ag", "ka_s"], ["t2"])
            tt("pool", t2[:], t2[:], bc(omka, h0), ALU.add, ["t2", "omka"], ["t2"])
            tt("dve", kd[:], ks[:], t2[:], ALU.mult, ["ks", "t2"], ["kd"])
            tt("pool", Kt[:], kd[:], E3[:], ALU.mult, ["kd", "E3"], ["Kt"])
            tt("dve", t1[:], rs[:], kd[:], ALU.mult, ["rs", "kd"], ["t1"])
            tt("pool", t1[:], t1[:], bc(rk, h0), ALU.mult, ["t1", "rk_s"], ["t1"])
            bi = K.nxt("bst", 2)
            for pr in range(HG // 2):
                pl, plk = next_pl()
                P.op("pe", [mm(pl[:, q * TQ:(q + 1) * TQ], ones[:], t1[:, pr * 2 + q, :]) for q in range(2)], r=["ones64", "t1"], w=[plk])
                tt("dve", bst[bi][:, pr * 2:pr * 2 + 2, :], pl[:].rearrange("p (q t) -> p q t", q=2), vs[:, pr * 2:pr * 2 + 2, :], ALU.mult, [plk, "vs"], [("bst", bi)])
            K.outs.append(P.dma("sp", "bo%d" % bi, bo_o[:, h0:h0 + HG, t0:t0 + TQ], bst[bi][:], r=[("bst", bi)]))
            yi = K.nxt("Yst", 2)
            skey = ("ST", hg)

            def precompute(c, par):
                cs = slice(c * 64, (c + 1) * 64)
                t = TS[par]
                BKT, VTs, MabT, M1, M2, Q = t["BKT"], t["VTs"], t["MabT"], t["M1"], t["M2"], t["Q"]
                k = lambda n: (n, par)
                ba, bb = 2 + 2 * par, 3 + 2 * par
                bT = B[ba][:].rearrange("p (a h k) -> p a h k", a=2, h=HG)
                bV = B[bb][:].rearrange("p (a h k) -> p a h k", a=2, h=HG)
                P.op("pe", [mm(bT[:, 0, hh, :], Bt[:, hh, cs], ident[:, 0, :]) for hh in range(HG)] + [mm(bT[:, 1, hh, :], Kt[:, hh, cs], ident[:, 0, :]) for hh in range(HG)],
                     r=["Bt", "Kt", "ident_s"], w=[("B", ba)])
                act(BKT[:], bT, AF.Identity, [], [k("BKT"), ("B", ba)])
                P.op("pe", [mm(bV[:, 0, hh, :], vs[:, hh, cs], ident[:, 0, :]) for hh in range(HG)] + [mm(bV[:, 1, hh, :], AR[:, hh, c, 0, :], Bt[:, hh, cs]) for hh in range(HG)],
                     r=["vs", "AR", "Bt", "ident_s"], w=[("B", bb)])
                yield
                act(VTs[:], bV[:, 0, :, :], AF.Identity, [], [k("VTs"), ("B", bb)])
                tt("dve", MabT[:], bV[:, 1, :, :], mlt[:], ALU.mult, ["mlt_s"], [k("MabT"), ("B", bb)])
                g1 = B[ba][:].rearrange("p (h x) -> p h x", h=HG)
                g2p = B[bb][:].rearrange("p (h x) -> p h x", h=HG)
                P.op("pe", [mm(g1[:, hh, :], Bt[:, hh, cs], AR[:, hh, c, :, :].rearrange("p a t -> p (a t)")) for hh in range(HG)], r=["Bt", "AR"], w=[("B", ba)])
                P.op("pe", [mm(g2p[:, hh, :], Kt[:, hh, cs], AR[:, hh, c, :, :].rearrange("p a t -> p (a t)")) for hh in range(HG)], r=["Kt", "AR"], w=[("B", bb)])
                yield
                tt("dve", M1[:], g1, msi[:], ALU.mult, ["msi_s"], [k("M1"), ("B", ba)])
                tt("dve", M2[:], g2p, msi[:], ALU.mult, ["msi_s"], [k("M2"), ("B", bb)])
                tt("pool", Q[:], M1[:, :, 0:64], ident[:], ALU.add, [k("M1"), "ident_s"], [k("Q")])
                N_ap, N_key, NT_ap, NT_key = M1[:, :, 0:64], k("M1"), MabT[:], k("MabT")
                bI = B[ba][:].rearrange("p (a h k) -> p a h k", a=2, h=HG)
                bQ = B[bb][:, 0:HG * 64].rearrange("p (h k) -> p h k", h=HG)
                for lvl in range(nlvl):
                    nb, ntb = t["Nb"][lvl % 2], t["NTb"][lvl % 2]
                    P.op("pe", [mm(bI[:, 0, hh, :], NT_ap[:, hh, :], N_ap[:, hh, :]) for hh in range(HG)] + [mm(bI[:, 1, hh, :], N_ap[:, hh, :], NT_ap[:, hh, :]) for hh in range(HG)],
                         r=[N_key, NT_key], w=[("B", ba)])
                    yield
                    act(nb[:], bI[:, 0, :, :], AF.Identity, [], [k(("Nb", lvl % 2)), ("B", ba)])
                    P.op("act", lambda e, ntb=ntb, bI=bI: e.activation(out=ntb[:], in_=bI[:, 1, :, :], func=AF.Identity), r=[], w=[k(("NTb", lvl % 2)), ("B", ba)])
                    N_ap, N_key, NT_ap, NT_key = nb[:], k(("Nb", lvl % 2)), ntb[:], k(("NTb", lvl % 2))
                    P.op("pe", [mm(bQ[:, hh, :], NT_ap[:, hh, :], Q[:, hh, :]) for hh in range(HG)], r=[NT_key, k("Q")], w=[("B", bb)])
                    yield
                    tt("dve", Q[:], Q[:], bQ, ALU.add, [], [k("Q"), ("B", bb)])

            def sequential(c, par):
                t = TS[par]
                BKT, VTs, M1, M2, Q = t["BKT"], t["VTs"], t["M1"], t["M2"], t["Q"]
                k = lambda n: (n, par)
                s1 = B[6][:].rearrange("p (a h k) -> p a h k", a=2, h=HG)
                s2 = B[7][:].rearrange("p (a h k) -> p a h k", a=2, h=HG)
                fns = []
                for hh in range(HG):
                    fns.append(mm(s1[:, 0, hh, :], AR[:, hh, c, 0, :], ST[:, h0 + hh, :], start=(hh == 0), stop=False))
                    fns.append(mm(s1[:, 0, hh, :], M2[:, hh, 0:64], VTs[:, hh, :], start=False, stop=True))
                P.op("pe", fns, r=["AR", skey, k("M2"), k("VTs")], w=[("B", 6)])
                act(XT[:], s1[:, 0, :, :], AF.Identity, [], ["XT", ("B", 6)])
                P.op("pe", [mm(s1[:, 1, hh, :], Q[:, hh, :], XT[:, hh, :], start=False, stop=True) for hh in range(HG)], r=[k("Q"), "XT"], w=[("B", 6)])
                P.op("act", lambda e: e.activation(out=UT[:], in_=s1[:, 1, :, :], func=AF.Identity), r=[], w=["UT", ("B", 6)])
                fns = []
                for hh in range(HG):
                    fns.append(mm(s2[:, 0, hh, :], AR[:, hh, c, 1, :], ST[:, h0 + hh, :], start=(hh == 0), stop=False))
                    fns.append(mm(s2[:, 0, hh, :], M1[:, hh, 64:128], UT[:, hh, :], start=False, stop=False))
                    fns.append(mm(s2[:, 0, hh, :], M2[:, hh, 64:128], VTs[:, hh, :], start=False, stop=True))
                for hh in range(HG):
                    fns.append(mm(s2[:, 1, hh, :], BKT[:, 0, hh, :], UT[:, hh, :], start=False, stop=False))
                    fns.append(mm(s2[:, 1, hh, :], BKT[:, 1, hh, :], VTs[:, hh, :], start=False, stop=True))
                P.op("pe", fns, r=["AR", skey, k("M1"), k("M2"), "UT", k("VTs"), k("BKT")], w=[("B", 7)])
                act(Yst[yi][:, c, :].rearrange("p (h v) -> p h v", h=HG), s2[:, 0, :, :], AF.Identity, [], [("Yst", yi), ("B", 7)])
                tt("dve", tS[:], s2[:, 1, :, :], ST[:, h0:h0 + HG, :], ALU.add, [skey], ["tS", ("B", 7)])
                tt("dve", ST[:, h0:h0 + HG, :], tS[:], E2[:, :, c * 64 + 63:c * 64 + 64].to_broadcast([64, HG, 64]), ALU.mult, ["tS", "cin"], [skey])

            nch = NCH if nch_lim is None else nch_lim
            for cp in range(0, nch, 2):
                cl = [cc for cc in (cp, cp + 1) if cc < nch]
                gens = [precompute(cc, cc % 2) for cc in cl]
                while gens:
                    for gnr in list(gens):
                        try:
                            next(gnr)
                        except StopIteration:
                            gens.remove(gnr)
                for cc in cl:
                    sequential(cc, cc % 2)
            K.outs.append(P.dma("sp", "yo%d" % yi, y_o[t0:t0 + TQ, h0 * 64:(h0 + HG) * 64].rearrange("(c t) n -> t c n", t=64), Yst[yi][:], r=[("Yst", yi)]))
    return K.done()


A_W, B_W = 768, 1280
RW_OFF = dict(r=0, k=1280, v=2560, wl=3840, al=3968, gl=4096)


def rw_consts():
    TQ = 256
    j = np.arange(64)[:, None]
    t = np.arange(64)[None, :]
    su = (j < t).astype(np.float32)
    iu = (j <= t).astype(np.float32)
    msi = np.concatenate([su, iu], 1)
    ident = np.eye(64, dtype=np.float32)
    rep = lambda m: np.ascontiguousarray(np.broadcast_to(m[:, None, :], (64, 4, m.shape[1])))
    resetm = np.ones((64, 4 * TQ), np.float32)
    resetm[:, ::64] = 0.0
    return {"ident": rep(ident), "maskSI": rep(msi), "maskLT": rep((j > t).astype(np.float32)), "resetm": resetm}


def rw_host_inputs(zbT, prm, d):
    T = zbT.shape[1]
    z = zbT[:, ::-1] if d else zbT
    zp = np.pad(z, ((0, 0), (1, 1)))
    hv = lambda a: np.ascontiguousarray(a.reshape(RW_H, 64, -1).transpose(1, 0, 2))
    pc = lambda v: v.reshape(-1, 64).T
    mup, mun = (prm["rw_mu_next"], prm["rw_mu_prev"]) if d else (prm["rw_mu_prev"], prm["rw_mu_next"])
    mu = np.zeros((64, 2, 65), np.float32)
    for i, m in enumerate((mup, mun)):
        mu[:, i, 0:20] = pc(m[0:1280]); mu[:, i, 20:40] = pc(m[1280:2560]); mu[:, i, 40:60] = pc(m[2560:3840])
        mu[:, i, 60] = m[3840 + 64 * d:3840 + 64 * (d + 1)]; mu[:, i, 61] = m[3968 + 64 * d:3968 + 64 * (d + 1)]
        mu[:, i, 62:65] = pc(m[4096:4288])
    ins = {
        "zr": hv(zp[0:1280]), "zk": hv(zp[1280:2560]), "zv": hv(zp[2560:3840]),
        "zwl": np.ascontiguousarray(zp[3840 + 64 * d:3840 + 64 * (d + 1)]), "zal": np.ascontiguousarray(zp[3968 + 64 * d:3968 + 64 * (d + 1)]),
        "zgl": np.ascontiguousarray(zp[4096:4288].reshape(3, 64, T + 2).transpose(1, 0, 2)),
        "mu": mu, "w0": np.ascontiguousarray(pc(prm["rw_w0"][d])), "w2": np.ascontiguousarray(prm["rw_w2"][d]),
        "a0": np.ascontiguousarray(pc(prm["rw_a0"][d])), "a2": np.ascontiguousarray(prm["rw_a2"][d]),
        "g2": np.ascontiguousarray(prm["rw_g2"].reshape(3, 64, 1280).transpose(1, 0, 2)),
        "kk": np.ascontiguousarray(pc(prm["rw_k_k"])), "ka": np.ascontiguousarray(pc(prm["rw_k_a"])), "rk": np.ascontiguousarray(prm["rw_r_k"].T),
    }
    ins.update(rw_consts())
    return ins


RW_GN_EPS = 64e-5


def tok_rw_post(K, y0, y1, b0, b1, g, lnw, lnb):
    P = K.P
    bo = K.sb("blockones", [128, 128], F32)
    P.op("pool", lambda e: e.memset(bo[:], 0.0), w=["bo"])
    P.op("pool", lambda e: e.memset(bo[0:64, 0:64], 1.0 / 64), w=["bo"])
    P.op("pool", lambda e: e.memset(bo[64:128, 64:128], 1.0 / 64), w=["bo"])
    epsg = K.sb("epsg", [128, 1], F32)
    P.op("pool", lambda e: e.memset(epsg[:], RW_GN_EPS), w=["epsg"])
    st = [K.sb("rwp%d" % i, [128, 512], F32) for i in range(5)]
    srcs = (y0, y1, b0, b1, g)
    for cc in range(10):
        for tc in range(K.ntc):
            ts = slice(tc * 512, (tc + 1) * 512)
            for i in range(5):
                P.dma(K.dq(), "rwp%d" % i, st[i][:], srcs[i][cc * 128:(cc + 1) * 128, ts], w=[("rwp", i)])
            A, Bq, Cq, Dq, G = st
            s0, s1 = K.sq

            def tt(eng, out, a, b, op, r, w):
                P.op(eng, lambda e: e.tensor_tensor(out=out, in0=a, in1=b, op=op), r=r, w=w)
            tt("dve", A[:], A[:], Bq[:], ALU.add, [("rwp", 0), ("rwp", 1)], [("rwp", 0)])
            tt("pool", Cq[:], Cq[:], Dq[:], ALU.add, [("rwp", 2), ("rwp", 3)], [("rwp", 2)])
            P.op("pe", lambda e, A=A: e.matmul(K.ps_ss[:], lhsT=bo[:], rhs=A[:], start=True, stop=True), r=["bo", ("rwp", 0)], w=["ps_ss"])
            tt("dve", A[:], A[:], K.ps_ss[:], ALU.subtract, [("rwp", 0)], [("rwp", 0), "ps_ss"])
            P.op("act", lambda e, A=A, s0=s0: e.activation(out=s0[:], in_=A[:], func=AF.Square), r=[("rwp", 0)], w=[("sq", 0)])
            P.op("pe", lambda e, s0=s0: e.matmul(K.ps_ss[:], lhsT=bo[:], rhs=s0[:], start=True, stop=True), r=["bo", ("sq", 0)], w=["ps_ss"])
            P.op("act", lambda e, s1=s1: e.activation(out=s1[:], in_=K.ps_ss[:], func=AF.Sqrt, bias=epsg[:, 0:1], scale=1.0), r=["epsg"], w=[("sq", 1), "ps_ss"])
            P.op("dve", lambda e, s1=s1: e.reciprocal(out=s1[:], in_=s1[:]), r=[("sq", 1)], w=[("sq", 1)])
            tt("dve", A[:], A[:], s1[:], ALU.mult, [("rwp", 0), ("sq", 1)], [("rwp", 0)])
            P.op("dve", lambda e, A=A, cc=cc: e.tensor_scalar(out=A[:], in0=A[:], scalar1=lnw[:, cc:cc + 1], scalar2=lnb[:, cc:cc + 1], op0=ALU.mult, op1=ALU.add),
                 r=[("rwp", 0), "lnw", "lnb"], w=[("rwp", 0)])
            tt("dve", A[:], A[:], Cq[:], ALU.add, [("rwp", 0), ("rwp", 2)], [("rwp", 0)])
            tt("dve", K.hT[:, 6 + cc, ts], A[:], G[:], ALU.mult, [("rwp", 0), ("rwp", 4)], [("h", 6 + cc, tc)])


def _run(nc, in_maps):
    res = run_bass_kernel_spmd(nc, in_maps, core_ids=list(range(NCORES)))
    return res.results


def _pk(v):
    return np.ascontiguousarray(np.asarray(v, np.float32).reshape(-1, 128).T)


PAIRS = [[0, 1], [2, 3], [4, 5], [6, 7]]


def stage_attn_A(nc, P, zm, oex, cos, sin, ident_d, mask_d, nh=6):
    K = MixKernel(nc, P)
    T = SEQ
    K.attn_setup(cos, sin, ident_d)
    mask = K.sb("maskA", [128, 2 * T - 128], BF16)
    P.dma("sp", "mask", mask[:], mask_d, w=["mask"])
    ost = K.sb("ostA", [128, 16, nh * 64], F32)
    def prep(h):
        i = h % 2
        Qa, Ka = K.Qa[i], K.Ka[i]
        K.rope_aug2(Ka, ("Ka", i), zm, nh * 64 + h * 64)
        K.prep_k(Ka, ("Ka", i))
        K.rope_aug2(Qa, ("Qa", i), zm, h * 64)
        K.prep_q(Qa, ("Qa", i))
        K.load_v_T(zm, 2 * nh * 64 + h * 64, K.Vb[i], ("Vb", i))
    prep(0)
    for h in range(nh):
        i = h % 2
        Qa, Ka = K.Qa[i], K.Ka[i]
        if h + 1 < nh:
            K.set_drip(K.record(lambda: prep(h + 1)), 50)
        K.attn_toeplitz(Qa, ("Qa", i), Ka, ("Ka", i), K.Vb[i], ("Vb", i), mask, "mask", T - 128, 1024, ost, "ost", h * 64)
        K.flush()
    K.out_T(ost, "ost", nh * 64, oex, 0)
    K.done()


def stage_attn_C(nc, P, zm, oex, cos, sin, ident_d, mask_d, sink_d, nq=8):
    K = MixKernel(nc, P)
    T = SEQ
    K.attn_setup(cos, sin, ident_d)
    mask = K.sb("maskC", [128, 2 * T - 128], BF16)
    P.dma("sp", "mask", mask[:], mask_d, w=["mask"])
    sink = K.sb("sinkS", [128, nq], F32)
    P.dma("sp", "sink", sink[:], sink_d, w=["sink"])
    ost = K.sb("ostC", [128, 16, nq * 64], F32)
    Ka = K.Ka[0]
    K.rope_aug2(Ka, ("Ka", 0), zm, nq * 64)
    K.prep_k(Ka, ("Ka", 0))
    K.load_v_T(zm, nq * 64 + 64, K.Vb[0], ("Vb", 0))
    def prep(h):
        i = h % 2
        K.rope_aug2(K.Qa[i], ("Qa", i), zm, h * 64)
        K.prep_q(K.Qa[i], ("Qa", i))
    prep(0)
    for h in range(nq):
        i = h % 2
        Qa = K.Qa[i]
        if h + 1 < nq:
            K.set_drip(K.record(lambda: prep(h + 1)), 20)
        K.attn_toeplitz(Qa, ("Qa", i), Ka, ("Ka", 0), K.Vb[0], ("Vb", 0), mask, "mask", T - 128, 128, ost, "ost", h * 64, sink_col=sink[:, h:h + 1])
        K.flush()
    K.out_T(ost, "ost", nq * 64, oex, 0)
    K.done()


def stage_attn_D(nc, P, zm, oex, cos, sin, ident_d, wl, nh=8, zrow0=640, orow0=512):
    K = MixKernel(nc, P)
    T = SEQ
    K.attn_setup(cos, sin, ident_d)
    wst = [K.sb("wlst%d" % i, [128, 2048], F32) for i in range(2)]
    Wt = [K.sb("Wt%d" % i, [128, 2048], BF16) for i in range(2)]
    Vb2 = [K.sb("Vb2_%d" % i, [128, 15, 65], BF16) for i in range(2)]
    ostl = [K.sb("ostD%d" % i, [64, NA_ROWS, 64], F32) for i in range(2)]
    def prep(h):
        i = h % 2
        Qa, Ka = K.Qa[i], K.Ka[i]
        sa, sb_ = K.ld[i]
        P.dma(K.dq(), "ka%d" % i, sa[:], zm[zrow0 + 512 + h * 64:zrow0 + 512 + (h + 1) * 64, 1:T + 1], w=[("ld", i, 0)])
        P.op("act", lambda e, sa=sa, Ka=Ka: e.activation(out=Ka[0:64, :], in_=sa[:], func=AF.Identity), r=[("ld", i, 0)], w=[("Ka", i)])
        K.prep_k(Ka, ("Ka", i))
        P.dma(K.dq(), "qa%d" % i, sb_[:], zm[zrow0 + h * 64:zrow0 + (h + 1) * 64, 1:T + 1], w=[("ld", i, 1)])
        P.op("dve", lambda e, sb_=sb_, Qa=Qa: e.tensor_copy(out=Qa[0:64, :], in_=sb_[:]), r=[("ld", i, 1)], w=[("Qa", i)])
        K.prep_q(Qa, ("Qa", i))
        K.load_v_T(zm, zrow0 + 1024 + h * 64, K.Vb[i], ("Vb", i))
        K.load_v_T(zm, zrow0 + 1024 + h * 64, Vb2[i], ("Vb2", i), shift=64, nblk=15)
        P.dma(K.dq(), "wl%d" % i, wst[i][:], wl[h], w=[("wlst", i)])
        P.op("act", lambda e, i=i: e.activation(out=Wt[i][:], in_=wst[i][:], func=AF.Exp), r=[("wlst", i)], w=[("Wt", i)])
    prep(0)
    for h in range(nh):
        i = h % 2
        Qa, Ka = K.Qa[i], K.Ka[i]
        Vb, vkey = K.Vb[i], ("Vb", i)
        if h + 1 < nh:
            K.set_drip(K.record(lambda: prep(h + 1)), 30)
        pend = []

        def emit_pv(item, i=i):
            pt, ptk, vt, vk, b0, j, pO, oi, r0 = item
            P.op("pe", [lambda e, b=b, pt=pt, pO=pO, j=j, vt=vt, b0=b0: e.matmul(pO[0:64, j, 0:65], lhsT=pt[:, b * 64:(b + 1) * 64], rhs=vt[:, b0 + b, :], start=(b == 0 and j == 0), stop=(b == 3), skip_group_check=True) for b in range(4)],
                 r=[ptk, vk], w=[("pO", oi)])
            if j == 3:
                ri = K.nxt("rec", 2)
                rec = K.rec[ri]
                P.op("dve", lambda e, rec=rec, pO=pO: e.reciprocal(out=rec[0:64, :], in_=pO[0:64, :, 64]), r=[("pO", oi)], w=[("rec", ri)])
                for jj in range(4):
                    P.op("dve", lambda e, jj=jj, rec=rec, pO=pO, r0=r0, i=i: e.tensor_scalar(out=ostl[i][:, r0 + jj, :], in0=pO[0:64, jj, 0:64], scalar1=rec[0:64, jj:jj + 1], scalar2=None, op0=ALU.mult),
                         r=[("pO", oi), ("rec", ri)], w=[("ostD", i)])
        for r0 in range(0, NA_ROWS, 4):
            oi = K.nxt("pO", 2)
            pO = K.pO[oi]
            for j in range(4):
                r = r0 + j
                ws = min(max(r - 4, 0), NA_ROWS - 8)
                var = ws - r + 7
                segs = [(Ka[0:65, ws * 64 + b * 128: ws * 64 + (b + 1) * 128], Qa[0:65, r * 64:(r + 1) * 64], 64) for b in range(4)]
                pt, ptk, _ = K.score_group(segs, Wt[i][:, var * 256:(var + 1) * 256], [("Wt", i)], [("Qa", i), ("Ka", i)])
                if ws % 2 == 0:
                    vt, vk, b0 = Vb, vkey, ws // 2
                else:
                    vt, vk, b0 = Vb2[i], ("Vb2", i), (ws - 1) // 2
                pend.append((pt, ptk, vt, vk, b0, j, pO, oi, r0))
                if len(pend) > 2:
                    emit_pv(pend.pop(0))
        while pend:
            emit_pv(pend.pop(0))
        K.flush()
        for g in range(4):
            P.op("pe", [(lambda e, b=b, g=g, i=i: e.matmul(K.pN[0:64, b * 64:(b + 1) * 64], lhsT=ostl[i][:, g * 8 + b, :], rhs=K.ident[0:64, 0:64], start=True, stop=True)) for b in range(8)],
                 r=[("ostD", i), "ident"], w=["pN"])
            si = K.nxt("stg", 2)
            st = K.stg[si]
            P.op("act", lambda e, st=st: e.activation(out=st[0:64, :], in_=K.pN[0:64, :], func=AF.Identity), r=[], w=[("stg", si), "pN"])
            P.dma("sp", "oT%d" % si, oex[orow0 + h * 64:orow0 + (h + 1) * 64, g * 512:(g + 1) * 512], st[0:64, :], r=[("stg", si)])
    K.done()


def stage_rwkv(nc, P, zm, oex, prm, z_row0=1152, o_row0=384):
    K = MixKernel(nc, P)
    T = SEQ
    H, TQ = 10, 256
    NTQ, NCH = T // TQ, TQ // 64
    C = H * 64
    GROUPS = [(0, 4), (4, 4), (8, 2)]
    HGM = 4
    hv = lambda lo: zm[z_row0 + lo:z_row0 + lo + 640, :].rearrange("(h p) t -> p h t", p=64)
    zr, zk, zv = hv(0), hv(640), hv(1280)
    zwl = [zm[z_row0 + 1920 + 64 * d:z_row0 + 1984 + 64 * d, :] for d in range(2)]
    zal = [zm[z_row0 + 2048 + 64 * d:z_row0 + 2112 + 64 * d, :] for d in range(2)]
    zgl = zm[z_row0 + 2176:z_row0 + 2368, :].rearrange("(j p) t -> p j t", p=64)
    yd = [dram_reg(nc, "rw_y%d" % d, [64, H, T], F32, "Internal") for d in range(2)]
    bd = [dram_reg(nc, "rw_b%d" % d, [64, H, T], F32, "Internal") for d in range(2)]
    gd = dram_reg(nc, "rw_g", [64, H, T], F32, "Internal")
    oexv = oex[o_row0:o_row0 + C, :].rearrange("(h p) t -> p h t", p=64)

    def const(name, ap, shape):
        t = K.sb(name, shape, F32)
        P.dma(K.dq(), "cst", t[:], ap, w=[name])
        return t
    NMU = 37
    mu = const("mu_s", prm["mu"], [64, 2, NMU]); w0 = const("w0_s", prm["w0"], [64, 2, H]); w2 = const("w2_s", prm["w2"], [64, 2, C])
    a0 = const("a0_s", prm["a0"], [64, 2, H]); a2 = const("a2_s", prm["a2"], [64, 2, C]); g2 = const("g2_s", prm["g2"], [64, 3, C])
    kkp = const("kk_s", prm["kk"], [64, H]); ka = const("ka_s", prm["ka"], [64, H]); rk = const("rk_s", prm["rk"], [64, H])
    lnw = const("lnw_s", prm["lnw"], [64, H]); lnb = const("lnb_s", prm["lnb"], [64, H])
    ident = const("ident_s", prm["ident4"], [64, 4, 64])
    msi = [const("msi%d_s" % d, prm["maskSI"][d], [64, 4, 128]) for d in range(2)]
    mlt = [const("mlt%d_s" % d, prm["maskLT"][d], [64, 4, 64]) for d in range(2)]
    resetm = const("resetm_s", prm["resetm"], [64, 4 * TQ])
    CK = ["mu_s", "w0_s", "w2_s", "a0_s", "a2_s", "g2_s", "kk_s", "ka_s", "rk_s", "lnw_s", "lnb_s", "ident_s", "msi0_s", "msi1_s", "mlt0_s", "mlt1_s", "resetm_s"]
    ones = K.sb("ones64", [64, 64], F32)
    P.op("dve", lambda e: e.memset(ones[:], 1.0), r=CK, w=["ones64"])
    onesm = K.sb("onesm64", [64, 64], F32)
    P.op("dve", lambda e: e.memset(onesm[:], 1.0 / 64), w=["onesm"])
    epsg = K.sb("epsg", [64, 1], F32)
    P.op("dve", lambda e: e.memset(epsg[:], RW_GN_EPS), w=["epsg"])
    c0 = K.sb("c0", [64, NMU], F32)
    P.op("dve", lambda e: e.tensor_tensor(out=c0[:], in0=mu[:, 0, :], in1=mu[:, 1, :], op=ALU.add), r=CK, w=["c0"])
    P.op("dve", lambda e: e.tensor_scalar(out=c0[:], in0=c0[:], scalar1=-1.0, scalar2=1.0, op0=ALU.mult, op1=ALU.add), r=["c0"], w=["c0"])
    omka = K.sb("omka", [64, H], F32)
    P.op("dve", lambda e: e.tensor_scalar(out=omka[:], in0=ka[:], scalar1=-1.0, scalar2=1.0, op0=ALU.mult, op1=ALU.add), r=CK, w=["omka"])
    P.op("act", lambda e: e.activation(out=epsg[:], in_=epsg[:], func=AF.Identity), r=CK + ["epsg"], w=["epsg"])
    ST = K.sb("ST", [64, H, 64], F32)

    A3 = [64, HGM, TQ]
    arr = lambda name: K.sb(name, A3, F32)
    zl = [K.sb("zl%d" % j, [64, HGM, TQ + 2], F32) for j in range(3)]
    F32R = mybir.dt.float32r
    arr_r = lambda name: K.sb(name, A3, F32R)
    rs, ks, vs = arr("rs"), arr("ks"), arr_r("vs")
    ep_b0 = arr("ep_b0")
    t1, t2 = arr("t1"), arr("t2")
    lgs, ag, cin, cex, E3 = arr("lgs"), arr("ag"), arr("cin"), arr("cex"), arr("E3")
    kkn, rn, kd = arr("kkn"), arr("rn"), arr("kd")
    AR = K.sb("AR", [64, HGM, NCH, 2, 64], F32R)
    Bt, Kt = arr_r("Bt"), arr_r("Kt")
    gst, bst, Yst = arr("gst"), arr("bst"), arr("Yst")
    lw = [K.sb("lw%d" % i, [64, TQ + 2], F32) for i in range(2)]
    lg3 = K.sb("lg3", [64, 3, TQ + 2], F32)
    twl = K.sb("twl", [64, TQ], F32); als = K.sb("als", [64, TQ], F32); sgl = K.sb("sgl", [64, 3, TQ], F32)
    B = [K.ps("B%d" % i, [64, 512], F32) for i in range(8)]
    sm = lambda name, shape: K.sb(name, shape, F32R)
    identr = K.sb("identr", [64, 4, 64], F32R)
    P.op("act", lambda e: e.activation(out=identr[:], in_=ident[:], func=AF.Identity), r=CK, w=["identr"])
    STr = K.sb("STr", [64, H, 64], F32R)
    TS = []
    for par in range(2):
        sfx = "_%d" % par
        TS.append(dict(BKT=sm("BKT" + sfx, [64, 2, HGM, 64]), VTs=sm("VTs" + sfx, [64, HGM, 64]), MabT=sm("MabT" + sfx, [64, HGM, 64]),
                       M1=sm("M1" + sfx, [64, HGM, 128]), M2=sm("M2" + sfx, [64, HGM, 128]),
                       NN=[sm("NN%d%s" % (i, sfx), [64, 2, HGM, 64]) for i in range(2)],
                       Q=sm("Qs" + sfx, [64, HGM, 64])))
    XT = sm("XTs", [64, HGM, 64]); UT = sm("UTs", [64, HGM, 64]); tS = K.sb("tS", [64, HGM, 64], F32)

    def tt(eng, out, a, b, op, r, w):
        P.op(eng, lambda e: e.tensor_tensor(out=out, in0=a, in1=b, op=op), r=r, w=w)

    def act(out, in_, func, r, w, **kw):
        P.op("act", lambda e: e.activation(out=out, in_=in_, func=func, **kw), r=r, w=w)

    def bc(t, col0, n, width=TQ):
        return t[:, col0:col0 + n, None].to_broadcast([64, n, width])

    def shift1(dst, src, col, r, w):
        P.op("dve", lambda e: e.tensor_scalar(out=dst, in0=src[:, 1:TQ + 1], scalar1=c0[:, col:col + 1], scalar2=None, op0=ALU.mult), r=r + ["c0"], w=w)
        P.op("dve", lambda e: e.scalar_tensor_tensor(out=dst, in0=src[:, 0:TQ], scalar=mu[:, 0, col:col + 1], in1=dst, op0=ALU.mult, op1=ALU.add), r=r + w, w=w)
        P.op("dve", lambda e: e.scalar_tensor_tensor(out=dst, in0=src[:, 2:TQ + 2], scalar=mu[:, 1, col:col + 1], in1=dst, op0=ALU.mult, op1=ALU.add), r=r + w, w=w)

    def mm(out, lhsT, rhs, start=True, stop=True):
        return lambda e: e.matmul(out, lhsT=lhsT, rhs=rhs, start=start, stop=stop, skip_group_check=True)

    pl_i = [0]

    def next_pl():
        pl_i[0] = (pl_i[0] + 1) % 2
        return B[pl_i[0]], ("B", pl_i[0])

    vs2 = [vs, arr_r("vsB")]
    cin2 = [cin, arr("cinB")]
    AR2 = [AR, K.sb("ARB", [64, HGM, NCH, 2, 64], F32R)]
    Bt2 = [Bt, arr_r("BtB")]
    Kt2 = [Kt, arr_r("KtB")]
    tq_order_of = lambda d: list(range(NTQ)) if d == 0 else list(range(NTQ - 1, -1, -1))
    rwc = [dram_reg(nc, "rw_c%d" % i, [64, H, T], F32, "Internal") for i in range(4)]

    def lora_prep(d, tq):
        t0 = tq * TQ
        P.dma(K.dq(), "lw0", lw[0][:], zwl[d][:, t0:t0 + TQ + 2], w=[("lw", 0)])
        shift1(twl[:], lw[0], 30 + d, [("lw", 0)], ["twl"])
        act(twl[:], twl[:], AF.Tanh, ["twl"], ["twl"])
        P.dma(K.dq(), "lw1", lw[1][:], zal[d][:, t0:t0 + TQ + 2], w=[("lw", 1)])
        shift1(als[:], lw[1], 32 + d, [("lw", 1)], ["als"])
        if d == 0:
            P.dma(K.dq(), "lg3", lg3[:], zgl[:, :, t0:t0 + TQ + 2], w=["lg3"])
            for jj in range(3):
                shift1(sgl[:, jj, :], lg3[:, jj, :], 34 + jj, ["lg3"], [("sgl", jj)])
                act(sgl[:, jj, :], sgl[:, jj, :], AF.Sigmoid, [("sgl", jj)], [("sgl", jj)])

    def prep_visit(d, tq, hg, par, with_lora):
        if with_lora:
            lora_prep(d, tq)
        t0 = tq * TQ
        h0, n = GROUPS[hg]
        vs, cin, AR, Bt, Kt = vs2[par], cin2[par], AR2[par], Bt2[par], Kt2[par]
        if d == 0:
            srcs = (zr, zk, zv)
            dsts = (rs, ks, vs)
            for a in range(3):
                P.dma(K.dq(), "zl%d" % a, zl[a][:, 0:n, :], srcs[a][:, h0:h0 + n, t0:t0 + TQ + 2], w=[("zl", a)])
                z = zl[a]
                col = a * H + h0
                nm = ("rs", "ks", ("vs", par))[a]
                dd = dsts[a]
                tt("dve", dd[:, 0:n, :], z[:, 0:n, 1:TQ + 1], bc(c0, col, n), ALU.mult, [("zl", a), "c0"], [nm])
                tt("dve", t2[:, 0:n, :], z[:, 0:n, 0:TQ], bc(mu[:, 0, :], col, n), ALU.mult, [("zl", a)], ["t2"])
                tt("dve", dd[:, 0:n, :], dd[:, 0:n, :], t2[:, 0:n, :], ALU.add, [nm, "t2"], [nm])
                tt("dve", t1[:, 0:n, :], z[:, 0:n, 2:TQ + 2], bc(mu[:, 1, :], col, n), ALU.mult, [("zl", a)], ["t1"])
                tt("dve", dd[:, 0:n, :], dd[:, 0:n, :], t1[:, 0:n, :], ALU.add, [nm, "t1"], [nm])
            for a_, (arr_, nm_) in enumerate(((rs, "rs"), (ks, "ks"), (vs, ("vs", par)))):
                src_ = arr_[:, 0:n, :].bitcast(F32) if a_ == 2 else arr_[:, 0:n, :]
                P.dma("sp", "cs%d" % a_, rwc[a_][:, h0:h0 + n, t0:t0 + TQ], src_, r=[nm_], w=[("rwc", a_, tq, hg)])
        else:
            P.dma("sp", "cl0", rs[:, 0:n, :], rwc[0][:, h0:h0 + n, t0:t0 + TQ], r=[("rwc", 0, tq, hg)], w=["rs"])
            P.dma("sp", "cl1", ks[:, 0:n, :], rwc[1][:, h0:h0 + n, t0:t0 + TQ], r=[("rwc", 1, tq, hg)], w=["ks"])
            P.dma("sp", "cl2", t1[:, 0:n, :], rwc[2][:, h0:h0 + n, t0:t0 + TQ], r=[("rwc", 2, tq, hg)], w=["t1"])
            P.op("dve", lambda e, n=n, vs=vs: e.tensor_copy(out=vs[:, 0:n, :], in_=t1[:, 0:n, :]), r=["t1"], w=[("vs", par)])
        for pr in range(n // 2):
            for (wt, src, skey, bias, dst, dkey) in ((w2, twl, "twl", w0, lgs, "lgs"), (a2, als, "als", a0, ag, "ag")):
                pl, plk = next_pl()
                P.op("pe", [mm(pl[:, q * TQ:(q + 1) * TQ], wt[:, d, (h0 + pr * 2 + q) * 64:(h0 + pr * 2 + q + 1) * 64], src[:]) for q in range(2)], r=[skey], w=[plk])
                for q in range(2):
                    hh = pr * 2 + q
                    act(dst[:, hh, :], pl[:, q * TQ:(q + 1) * TQ], AF.Sigmoid, [], [dkey, plk], bias=bias[:, d, h0 + hh:h0 + hh + 1])
            if d == 0:
                pl, plk = next_pl()
                fns = []
                for q in range(2):
                    h = h0 + pr * 2 + q
                    for jj in range(3):
                        fns.append(mm(pl[:, q * TQ:(q + 1) * TQ], g2[:, jj, h * 64:(h + 1) * 64], sgl[:, jj, :], start=(jj == 0 and q == 0), stop=(jj == 2)))
                P.op("pe", fns, r=[("sgl", jj) for jj in range(3)], w=[plk])
                act(gst[:, pr * 2:pr * 2 + 2, :], pl[:].rearrange("p (q t) -> p q t", q=2), AF.Identity, [], ["gst", plk])
        if d == 0:
            P.dma("sp", "go", gd[:, h0:h0 + n, t0:t0 + TQ], gst[:, 0:n, :], r=["gst"])
        f2 = lambda t, n=n: t[:, 0:n, :].rearrange("p h t -> p (h t)")
        v4 = lambda t, n=n: t[:, 0:n, :].rearrange("p h (c t) -> p h c t", t=64)
        P.op("dve", lambda e, n=n, f2=f2: e.tensor_tensor_scan(out=f2(cin), data0=resetm[:, 0:n * TQ], data1=f2(lgs), initial=0.0, op0=ALU.mult, op1=ALU.add), r=["lgs"], w=[("cin", par)])
        if d == 0:
            tt("dve", cex[:, 0:n, :], cin[:, 0:n, :], lgs[:, 0:n, :], ALU.subtract, [("cin", par), "lgs"], ["cex"])
        else:
            tot = v4(cin)[:, :, :, 63:64].to_broadcast([64, n, NCH, 64])
            tt("dve", v4(cex), tot, v4(cin), ALU.subtract, [("cin", par)], ["cex"])
            tt("dve", cin[:, 0:n, :], cex[:, 0:n, :], lgs[:, 0:n, :], ALU.add, ["cex", "lgs"], [("cin", par)])
        act(E3[:, 0:n, :], cin[:, 0:n, :], AF.Exp, [("cin", par)], ["E3"], scale=RW_LN_A)
        act(cin[:, 0:n, :], cin[:, 0:n, :], AF.Exp, [("cin", par)], [("cin", par)], scale=-RW_LN_A)
        act(cex[:, 0:n, :], cex[:, 0:n, :], AF.Exp, ["cex"], ["cex"], scale=-RW_LN_A)
        E2, E1 = cin, cex
        gcol = 63 if d == 0 else 0
        if d == 0:
            tt("dve", kkn[:, 0:n, :], ks[:, 0:n, :], bc(kkp, h0, n), ALU.mult, ["ks"], ["kkn"])
            act(t1[:, 0:n, :], kkn[:, 0:n, :], AF.Square, ["kkn"], ["t1"])
            for pr in range(n // 2):
                pl, plk = next_pl()
                P.op("pe", [mm(pl[:, q * TQ:(q + 1) * TQ], ones[:], t1[:, pr * 2 + q, :]) for q in range(2)], r=["ones64", "t1"], w=[plk])
                act(rn[:, pr * 2:pr * 2 + 2, :], pl[:].rearrange("p (q t) -> p q t", q=2), AF.Sqrt, [], ["rn", plk])
            P.op("dve", lambda e, n=n: e.tensor_scalar(out=rn[:, 0:n, :], in0=rn[:, 0:n, :], scalar1=1e-12, scalar2=None, op0=ALU.max), r=["rn"], w=["rn"])
            P.op("dve", lambda e, n=n: e.reciprocal(out=rn[:, 0:n, :], in_=rn[:, 0:n, :]), r=["rn"], w=["rn"])
            tt("dve", kkn[:, 0:n, :], kkn[:, 0:n, :], rn[:, 0:n, :], ALU.mult, ["kkn", "rn"], ["kkn"])
            P.dma("sp", "cs3", rwc[3][:, h0:h0 + n, t0:t0 + TQ], kkn[:, 0:n, :], r=["kkn"], w=[("rwc", 3, tq, hg)])
        else:
            P.dma("sp", "cl3", kkn[:, 0:n, :], rwc[3][:, h0:h0 + n, t0:t0 + TQ], r=[("rwc", 3, tq, hg)], w=["kkn"])
        P.op("dve", lambda e, n=n, v4=v4, E1=E1: e.scalar_tensor_tensor(out=AR[:, 0:n, :, 0, :], in0=v4(kkn), scalar=-1.0, in1=v4(E1), op0=ALU.mult, op1=ALU.mult), r=["kkn", "cex"], w=[("AR", par)])
        tt("dve", AR[:, 0:n, :, 1, :], v4(rs), v4(E2), ALU.mult, ["rs", ("cin", par)], [("AR", par)])
        tt("dve", t2[:, 0:n, :], kkn[:, 0:n, :], ag[:, 0:n, :], ALU.mult, ["kkn", "ag"], ["t2"])
        tt("dve", Bt[:, 0:n, :], t2[:, 0:n, :], E3[:, 0:n, :], ALU.mult, ["t2", "E3"], [("Bt", par)])
        tt("dve", t2[:, 0:n, :], ag[:, 0:n, :], bc(ka, h0, n), ALU.mult, ["ag"], ["t2"])
        tt("dve", t2[:, 0:n, :], t2[:, 0:n, :], bc(omka, h0, n), ALU.add, ["t2", "omka"], ["t2"])
        tt("dve", kd[:, 0:n, :], ks[:, 0:n, :], t2[:, 0:n, :], ALU.mult, ["ks", "t2"], ["kd"])
        tt("dve", Kt[:, 0:n, :], kd[:, 0:n, :], E3[:, 0:n, :], ALU.mult, ["kd", "E3"], [("Kt", par)])
        tt("dve", t1[:, 0:n, :], rs[:, 0:n, :], kd[:, 0:n, :], ALU.mult, ["rs", "kd"], ["t1"])
        tt("dve", t1[:, 0:n, :], t1[:, 0:n, :], bc(rk, h0, n), ALU.mult, ["t1"], ["t1"])
        for pr in range(n // 2):
            pl, plk = next_pl()
            P.op("pe", [mm(pl[:, q * TQ:(q + 1) * TQ], ones[:], t1[:, pr * 2 + q, :]) for q in range(2)], r=["ones64", "t1"], w=[plk])
            tt("dve", bst[:, pr * 2:pr * 2 + 2, :], pl[:].rearrange("p (q t) -> p q t", q=2), vs[:, pr * 2:pr * 2 + 2, :], ALU.mult, [("vs", par)], ["bst", plk])
        P.dma("sp", "bo", bd[d][:, h0:h0 + n, t0:t0 + TQ], bst[:, 0:n, :], r=["bst"])

    def chunks_visit(d, tq, hg, par, first):
        vpar = par
        t0 = tq * TQ
        h0, n = GROUPS[hg]
        vs, cin, AR, Bt, Kt = vs2[par], cin2[par], AR2[par], Bt2[par], Kt2[par]
        E2 = cin
        gcol = 63 if d == 0 else 0
        if first:
            P.op("dve", lambda e: e.memset(ST[:, h0:h0 + n, :], 0.0), r=[("ST", hg)], w=[("ST", hg)])
            act(STr[:, h0:h0 + n, :], ST[:, h0:h0 + n, :], AF.Identity, [("ST", hg)], [("STr", ("ST", hg))])
        skey = ("ST", hg)

        def precompute(c, par, n=n):
            cs = slice(c * 64, (c + 1) * 64)
            t = TS[c % 2]
            BKT, VTs, MabT, M1, M2, Q = t["BKT"], t["VTs"], t["MabT"], t["M1"], t["M2"], t["Q"]
            k = lambda nm: (nm, c % 2)
            ba, bb = 2 + 2 * par, 3 + 2 * par
            bT = B[ba][:].rearrange("p (a h k) -> p a h k", a=2, h=HGM)
            bV = B[bb][:].rearrange("p (a h k) -> p a h k", a=2, h=HGM)
            P.op("pe", [mm(bT[:, 0, hh, :], Bt[:, hh, cs], identr[:, 0, :]) for hh in range(n)] + [mm(bT[:, 1, hh, :], Kt[:, hh, cs], identr[:, 0, :]) for hh in range(n)],
                 r=[("Bt", vpar), ("Kt", vpar)], w=[("B", ba)])
            act(BKT[:, :, 0:n, :], bT[:, :, 0:n, :], AF.Identity, [], [k("BKT"), ("B", ba)])
            P.op("pe", [mm(bV[:, 0, hh, :], vs[:, hh, cs], identr[:, 0, :]) for hh in range(n)] + [mm(bV[:, 1, hh, :], AR[:, hh, c, 0, :], Bt[:, hh, cs]) for hh in range(n)],
                 r=[("vs", vpar), ("AR", vpar), ("Bt", vpar)], w=[("B", bb)])
            yield
            act(VTs[:, 0:n, :], bV[:, 0, 0:n, :], AF.Identity, [], [k("VTs"), ("B", bb)])
            tt("dve", MabT[:, 0:n, :], bV[:, 1, 0:n, :], mlt[d][:, 0:n, :], ALU.mult, [], [k("MabT"), ("B", bb)])
            g1 = B[ba][:].rearrange("p (h x) -> p h x", h=HGM)
            g2p = B[bb][:].rearrange("p (h x) -> p h x", h=HGM)
            P.op("pe", [mm(g1[:, hh, :], Bt[:, hh, cs], AR[:, hh, c, :, :].rearrange("p a t -> p (a t)")) for hh in range(n)], r=[("Bt", vpar), ("AR", vpar)], w=[("B", ba)])
            P.op("pe", [mm(g2p[:, hh, :], Kt[:, hh, cs], AR[:, hh, c, :, :].rearrange("p a t -> p (a t)")) for hh in range(n)], r=[("Kt", vpar), ("AR", vpar)], w=[("B", bb)])
            yield
            tt("dve", M1[:, 0:n, :], g1[:, 0:n, :], msi[d][:, 0:n, :], ALU.mult, [], [k("M1"), ("B", ba)])
            tt("dve", M2[:, 0:n, :], g2p[:, 0:n, :], msi[d][:, 0:n, :], ALU.mult, [], [k("M2"), ("B", bb)])
            tt("pool", Q[:, 0:n, :], M1[:, 0:n, 0:64], ident[:, 0:n, :], ALU.add, [k("M1")], [k("Q")])
            N_ap, N_key, NT_ap, NT_key = M1[:, :, 0:64], k("M1"), MabT[:], k("MabT")
            bI = B[ba][:].rearrange("p (a h k) -> p a h k", a=2, h=HGM)
            bQ = B[bb][:, 0:HGM * 64].rearrange("p (h k) -> p h k", h=HGM)
            for lvl in range(5):
                nn = t["NN"][lvl % 2]
                fns = [mm(bI[:, 0, hh, :], NT_ap[:, hh, :], N_ap[:, hh, :]) for hh in range(n)] + [mm(bI[:, 1, hh, :], N_ap[:, hh, :], NT_ap[:, hh, :]) for hh in range(n)]
                wk = [("B", ba)]
                rk_ = [N_key, NT_key]
                if lvl >= 1:
                    fns += [mm(bQ[:, hh, :], NT_ap[:, hh, :], Q[:, hh, :]) for hh in range(n)]
                    wk.append(("B", bb))
                    rk_.append(k("Q"))
                P.op("pe", fns, r=rk_, w=wk)
                yield
                act(nn[:, :, 0:n, :], bI[:, :, 0:n, :], AF.Identity, [], [k(("NN", lvl % 2)), ("B", ba)])
                if lvl >= 1:
                    tt("dve", Q[:, 0:n, :], Q[:, 0:n, :], bQ[:, 0:n, :], ALU.add, [], [k("Q"), ("B", bb)])
                N_ap, N_key, NT_ap, NT_key = nn[:, 0, :, :], k(("NN", lvl % 2)), nn[:, 1, :, :], k(("NN", lvl % 2))
                yield
            P.op("pe", [mm(bQ[:, hh, :], NT_ap[:, hh, :], Q[:, hh, :]) for hh in range(n)], r=[NT_key, k("Q")], w=[("B", bb)])
            yield
            tt("dve", Q[:, 0:n, :], Q[:, 0:n, :], bQ[:, 0:n, :], ALU.add, [], [k("Q"), ("B", bb)])

        def sequential(c, par, n=n, h0=h0, skey=skey):
            t = TS[c % 2]
            BKT, VTs, M1, M2, Q = t["BKT"], t["VTs"], t["M1"], t["M2"], t["Q"]
            k = lambda nm: (nm, c % 2)
            s1 = B[6][:].rearrange("p (a h k) -> p a h k", a=2, h=HGM)
            s2 = B[7][:].rearrange("p (a h k) -> p a h k", a=2, h=HGM)
            fns = []
            for hh in range(n):
                fns.append(mm(s1[:, 0, hh, :], AR[:, hh, c, 0, :], STr[:, h0 + hh, :], start=(hh == 0), stop=False))
                fns.append(mm(s1[:, 0, hh, :], M2[:, hh, 0:64], VTs[:, hh, :], start=False, stop=True))
            P.op("pe", fns, r=[("AR", vpar), ("STr", skey), k("M2"), k("VTs")], w=[("B", 6)])
            yield
            act(XT[:, 0:n, :], s1[:, 0, 0:n, :], AF.Identity, [], ["XT", ("B", 6)])
            P.op("pe", [mm(s1[:, 1, hh, :], Q[:, hh, :], XT[:, hh, :], start=False, stop=True) for hh in range(n)], r=[k("Q"), "XT"], w=[("B", 6)])
            yield
            act(UT[:, 0:n, :], s1[:, 1, 0:n, :], AF.Identity, [], ["UT", ("B", 6)])
            fns = []
            for hh in range(n):
                fns.append(mm(s2[:, 0, hh, :], STr[:, h0 + hh, :], AR[:, hh, c, 1, :], start=(hh == 0), stop=False))
                fns.append(mm(s2[:, 0, hh, :], UT[:, hh, :], M1[:, hh, 64:128], start=False, stop=False))
                fns.append(mm(s2[:, 0, hh, :], VTs[:, hh, :], M2[:, hh, 64:128], start=False, stop=True))
            for hh in range(n):
                fns.append(mm(s2[:, 1, hh, :], identr[:, 0, :], STr[:, h0 + hh, :], start=False, stop=False))
                fns.append(mm(s2[:, 1, hh, :], BKT[:, 0, hh, :], UT[:, hh, :], start=False, stop=False))
                fns.append(mm(s2[:, 1, hh, :], BKT[:, 1, hh, :], VTs[:, hh, :], start=False, stop=True))
            P.op("pe", fns, r=[("AR", vpar), ("STr", skey), k("M1"), k("M2"), "UT", k("VTs"), k("BKT"), "identr"], w=[("B", 7)])
            yield
            tt("dve", STr[:, h0:h0 + n, :], s2[:, 1, 0:n, :], E2[:, 0:n, c * 64 + gcol:c * 64 + gcol + 1].to_broadcast([64, n, 64]), ALU.mult, [("cin", vpar)], [("STr", skey), ("B", 7)])
            act(Yst[:, 0:n, c * 64:(c + 1) * 64], s2[:, 0, 0:n, :], AF.Identity, [], ["Yst", ("B", 7)])
            yield

        corder = [0, 1, 2, 3] if d == 0 else [3, 2, 1, 0]

        def rr(gens):
            while gens:
                for gnr in list(gens):
                    try:
                        next(gnr)
                    except StopIteration:
                        gens.remove(gnr)
                    K.drip()

        def seq_pair(cl):
            for cc in cl:
                for _ in sequential(cc, 0):
                    yield
        pairs = [corder[0:2], corder[2:4]]
        if RW_SKIP_CHUNKS:
            return
        rr([precompute(cc, j) for j, cc in enumerate(pairs[0])])
        rr([seq_pair(pairs[0])])
        rr([precompute(cc, j) for j, cc in enumerate(pairs[1])])
        rr([seq_pair(pairs[1])])
        P.dma("sp", "yo", yd[d][:, h0:h0 + n, t0:t0 + TQ], Yst[:, 0:n, :], r=["Yst"])

    visits = [(d, tq, hg) for d in range(2) for tq in tq_order_of(d) for hg in range(len(GROUPS))]
    prep_visit(*visits[0], 0, True)
    for vi, v in enumerate(visits):
        if vi + 1 < len(visits):
            nv = visits[vi + 1]
            K.set_drip(K.record(lambda: prep_visit(nv[0], nv[1], nv[2], (vi + 1) % 2, nv[2] == 0)), 100)
        chunks_visit(v[0], v[1], v[2], vi % 2, v[1] == tq_order_of(v[0])[0])
        K.flush()
    P.barrier()
    EPS = [dict(Y=rs, Y1=ks, B0=ep_b0, B1=t1, G=t2, yc=kkn, sq=rn, sd=kd, out=Yst),
           dict(Y=lgs, Y1=ag, B0=cex, B1=E3, G=gst, yc=bst, sq=cin2[0], sd=cin2[1], out=zl[0])]

    def ep_gen(tq, hg, S):
        t0 = tq * TQ
        h0, n = GROUPS[hg]
        b_ = EPS[S]
        k = lambda nm: ("ep", nm, S)
        Y, Y1, B0, B1, G, yc, sq, sd, out = (b_[x] for x in ("Y", "Y1", "B0", "B1", "G", "yc", "sq", "sd", "out"))
        srcs5 = (yd[0], yd[1], bd[0], bd[1], gd)
        for i5, (dst5, nm5) in enumerate(((Y, "Y"), (Y1, "Y1"), (B0, "B0"), (B1, "B1"), (G, "G"))):
            P.dma("sp", "ep%d_%d" % (i5, S), dst5[:, 0:n, 0:TQ], srcs5[i5][:, h0:h0 + n, t0:t0 + TQ], w=[k(nm5)])
        yield
        tt("dve", Y[:, 0:n, 0:TQ], Y[:, 0:n, 0:TQ], Y1[:, 0:n, 0:TQ], ALU.add, [k("Y"), k("Y1")], [k("Y")])
        tt("dve", B0[:, 0:n, 0:TQ], B0[:, 0:n, 0:TQ], B1[:, 0:n, 0:TQ], ALU.add, [k("B0"), k("B1")], [k("B0")])
        for pr in range(n // 2):
            sl = slice(pr * 2, pr * 2 + 2)
            pl, plk = next_pl()
            P.op("pe", [mm(pl[:, q * TQ:(q + 1) * TQ], onesm[:], Y[:, pr * 2 + q, 0:TQ]) for q in range(2)], r=["onesm", k("Y")], w=[plk])
            yield
            tt("dve", yc[:, sl, 0:TQ], Y[:, sl, 0:TQ], pl[:].rearrange("p (q t) -> p q t", q=2), ALU.subtract, [k("Y")], [k("yc"), plk])
        act(sq[:, 0:n, 0:TQ], yc[:, 0:n, 0:TQ], AF.Square, [k("yc")], [k("sq")])
        yield
        for pr in range(n // 2):
            sl = slice(pr * 2, pr * 2 + 2)
            pl, plk = next_pl()
            P.op("pe", [mm(pl[:, q * TQ:(q + 1) * TQ], onesm[:], sq[:, pr * 2 + q, 0:TQ]) for q in range(2)], r=["onesm", k("sq")], w=[plk])
            yield
            act(sd[:, sl, 0:TQ], pl[:].rearrange("p (q t) -> p q t", q=2), AF.Sqrt, ["epsg"], [k("sd"), plk], bias=epsg[:, 0:1])
        yield
        P.op("dve", lambda e, n=n, sd=sd: e.reciprocal(out=sd[:, 0:n, 0:TQ], in_=sd[:, 0:n, 0:TQ]), r=[k("sd")], w=[k("sd")])
        tt("dve", yc[:, 0:n, 0:TQ], yc[:, 0:n, 0:TQ], sd[:, 0:n, 0:TQ], ALU.mult, [k("yc"), k("sd")], [k("yc")])
        yield
        tt("dve", yc[:, 0:n, 0:TQ], yc[:, 0:n, 0:TQ], bc(lnw, h0, n), ALU.mult, [k("yc")], [k("yc")])
        tt("dve", yc[:, 0:n, 0:TQ], yc[:, 0:n, 0:TQ], bc(lnb, h0, n), ALU.add, [k("yc")], [k("yc")])
        yield
        tt("dve", yc[:, 0:n, 0:TQ], yc[:, 0:n, 0:TQ], B0[:, 0:n, 0:TQ], ALU.add, [k("yc"), k("B0")], [k("yc")])
        tt("dve", out[:, 0:n, 0:TQ], yc[:, 0:n, 0:TQ], G[:, 0:n, 0:TQ], ALU.mult, [k("yc"), k("G")], [k("out")])
        P.dma("sp", "oo%d" % S, oexv[:, h0:h0 + n, t0:t0 + TQ], out[:, 0:n, 0:TQ], r=[k("out")])

    ep_visits = [(tq, hg) for tq in range(NTQ) for hg in range(len(GROUPS))]
    for i0 in range(0, len(ep_visits), 2):
        gens = [ep_gen(tq, hg, S) for S, (tq, hg) in enumerate(ep_visits[i0:i0 + 2])]
        while gens:
            for gnr in list(gens):
                try:
                    next(gnr)
                except StopIteration:
                    gens.remove(gnr)
    K.done()


def rw_stage_consts():
    TQ = 256
    j = np.arange(64)[:, None]
    t = np.arange(64)[None, :]
    rep = lambda m: np.ascontiguousarray(np.broadcast_to(m[:, None, :].astype(np.float32), (64, 4, m.shape[1])))
    msi = np.stack([rep(np.concatenate([j < t, j <= t], 1)), rep(np.concatenate([j > t, j >= t], 1))])
    mlt = np.stack([rep(j > t), rep(j < t)])
    resetm = np.ones((64, 4 * TQ), np.float32)
    resetm[:, ::64] = 0.0
    return {"rw_ident4": rep(np.eye(64)), "rw_maskSI": np.ascontiguousarray(msi), "rw_maskLT": np.ascontiguousarray(mlt), "rw_resetm": resetm}


def rw_stage_params(inp_rw, hf):
    H = 10
    hs = slice(hf * H * 64, (hf + 1) * H * 64)
    pc = lambda v: np.ascontiguousarray(np.asarray(v, np.float32).reshape(-1, 64).T)
    mu = np.zeros((64, 2, 37), np.float32)
    for i, m in enumerate((inp_rw["rw_mu_prev"], inp_rw["rw_mu_next"])):
        mu[:, i, 0:10] = pc(m[0:1280][hs]); mu[:, i, 10:20] = pc(m[1280:2560][hs]); mu[:, i, 20:30] = pc(m[2560:3840][hs])
        mu[:, i, 30:32] = pc(m[3840:3968]); mu[:, i, 32:34] = pc(m[3968:4096]); mu[:, i, 34:37] = pc(m[4096:4288])
    two = lambda a: np.ascontiguousarray(np.stack([pc(a[0][hs]), pc(a[1][hs])], 1))
    return {
        "rw_mu": mu, "rw_w0": two(inp_rw["rw_w0"]), "rw_a0": two(inp_rw["rw_a0"]),
        "rw_w2": np.ascontiguousarray(np.asarray(inp_rw["rw_w2"], np.float32)[:, :, hs].transpose(1, 0, 2)),
        "rw_a2": np.ascontiguousarray(np.asarray(inp_rw["rw_a2"], np.float32)[:, :, hs].transpose(1, 0, 2)),
        "rw_g2": np.ascontiguousarray(np.asarray(inp_rw["rw_g2"], np.float32)[:, hs].reshape(3, 64, 640).transpose(1, 0, 2)),
        "rw_kk": pc(inp_rw["rw_k_k"][hs]), "rw_ka": pc(inp_rw["rw_k_a"][hs]), "rw_rk": np.ascontiguousarray(np.asarray(inp_rw["rw_r_k"], np.float32)[hf * H:(hf + 1) * H].T),
        "rw_lnw": pc(inp_rw["rw_ln_w"][hs]), "rw_lnb": pc(inp_rw["rw_ln_b"][hs]),
    }


def rw_stage_dram(nc):
    sh = {"rw_mu": [64, 2, 37], "rw_w0": [64, 2, 10], "rw_a0": [64, 2, 10], "rw_w2": [64, 2, 640], "rw_a2": [64, 2, 640], "rw_g2": [64, 3, 640],
          "rw_kk": [64, 10], "rw_ka": [64, 10], "rw_rk": [64, 10], "rw_lnw": [64, 10], "rw_lnb": [64, 10],
          "rw_ident4": [64, 4, 64], "rw_maskSI": [2, 64, 4, 128], "rw_maskLT": [2, 64, 4, 64], "rw_resetm": [64, 1024]}
    aps = {k: dram_reg(nc, k, v, F32, "ExternalInput") for k, v in sh.items()}
    prm = {k[3:]: v for k, v in aps.items()}
    prm["maskSI"] = [aps["rw_maskSI"][d] for d in range(2)]
    prm["maskLT"] = [aps["rw_maskLT"][d] for d in range(2)]
    return prm


ZCR = 512
OCR = 256


def gathered_pieces(rank, r0, n, R, CR):
    out = []
    r = r0
    while r < r0 + n:
        c = r // CR
        nc_rows = min(CR, R - c * CR)
        cnt = min(r0 + n, c * CR + nc_rows) - r
        out.append((c * 2 * CR + rank * nc_rows + (r - c * CR), cnt, r - r0))
        r += cnt
    return out


def stage_repack(nc, P, zg, zm, SH, sel_d):
    K = MixKernel(nc, P)
    T = SEQ
    R = 2 * SH
    sel = K.sb("selR", [128, 2], F32)
    P.dma("sp", "cst", sel[:], sel_d, w=["sel"])
    zero = K.sb("zeroR", [128, 2], F32)
    P.op("pool", lambda e: e.memset(zero[:], 0.0), w=["zero"])
    ta = [K.sb("rpa%d" % i, [128, 1024], F32) for i in range(3)]
    tb = [K.sb("rpb%d" % i, [128, 1024], F32) for i in range(3)]
    for c0 in range(0, SH, 128):
        n = min(128, SH - c0)
        P.dma("sp", "pad0", zm[c0:c0 + n, 0:1], zero[0:n, 0:1], r=["zero"], slow=True)
        P.dma("act", "pad1", zm[c0:c0 + n, T + 1:T + 2], zero[0:n, 1:2], r=["zero"], slow=True)
    for rank in range(2):
        for c0 in range(0, SH, 128):
            n = min(128, SH - c0)
            i = K.nxt("rp", 3)
            a, b = ta[i], tb[i]
            for pi, (dr, cnt, off) in enumerate(gathered_pieces(rank, c0, n, R, ZCR)):
                P.dma("sp", "rpa%d_%d" % (i, pi), a[off:off + cnt, :], zg[dr:dr + cnt, :], w=[("rpa", i, pi)])
            for pi, (dr, cnt, off) in enumerate(gathered_pieces(rank, SH + c0, n, R, ZCR)):
                P.dma("act", "rpb%d_%d" % (i, pi), b[off:off + cnt, :], zg[dr:dr + cnt, :], w=[("rpb", i, pi)])
            rk = [("rpa", i, 0), ("rpa", i, 1), ("rpb", i, 0), ("rpb", i, 1)]
            P.op("dve", lambda e, a=a, n=n: e.tensor_scalar(out=a[0:n, :], in0=a[0:n, :], scalar1=sel[0:n, 0:1], scalar2=None, op0=ALU.mult), r=["sel"], w=rk[0:2])
            P.op("dve", lambda e, a=a, b=b, n=n: e.scalar_tensor_tensor(out=a[0:n, :], in0=b[0:n, :], scalar=sel[0:n, 1:2], in1=a[0:n, :], op0=ALU.mult, op1=ALU.add),
                 r=["sel"], w=rk)
            P.dma("sp", "rpo%d" % i, zm[c0:c0 + n, 1 + rank * 1024:1 + (rank + 1) * 1024], a[0:n, :], r=[], w=rk)
    K.done()


def tok_load_sel(K, og, sel_d):
    P = K.P
    sel = K.sb("selT", [128, 2], F32)
    P.dma("sp", "vec_sel", sel[:], sel_d, w=["selT"])
    ostb = [K.sb("ostb%d" % i, [128, K.ntok], F32) for i in range(2)]
    for kc in range(K.kc):
        i = K.ost_i = (K.ost_i + 1) % 2
        a, b = K.ost[i], ostb[i]
        (dr, cnt, off), = gathered_pieces(kc // 8, (kc % 8) * 128, 128, 1024, OCR)
        P.dma("sp", "osa%d" % i, a[:], og[dr:dr + 128, 0:K.ntok], w=[("ost", i)])
        P.dma("act", "osb%d" % i, b[:], og[dr:dr + 128, K.ntok:2 * K.ntok], w=[("ostb", i)])
        P.op("act", lambda e, a=a: e.activation(out=a[:], in_=a[:], func=AF.Identity, scale=sel[:, 0:1]), r=[("ost", i), "selT"], w=[("ost", i)])
        P.op("dve", lambda e, a=a, b=b, kc=kc: e.scalar_tensor_tensor(out=K.hT[:, kc, :], in0=b[:], scalar=sel[:, 1:2], in1=a[:], op0=ALU.mult, op1=ALU.add),
             r=[("ost", i), ("ostb", i), "selT"], w=[("h", kc, tc) for tc in range(K.ntc)])


SH1, SH2 = 3520, 2176


def build_mega(upto=99):
    nc = bass.Bass("TRN2", target_bir_lowering=False)
    P = Prog(nc)
    D, F, NT, T = D_MODEL, FFN_HIDDEN, 1024, SEQ
    I = lambda name, shape: dram_reg(nc, name, shape, F32, "Internal")
    E = lambda name, shape, dt=F32: dram_reg(nc, name, shape, dt, "ExternalInput")
    z1, zg1, zm1 = I("z1", [2 * SH1, NT]), I("zg1", [-(-2 * SH1 // ZCR) * 2 * ZCR, NT]), I("zm1", [SH1, T + 2])
    z2, zg2, zm2 = I("z2", [2 * SH2, NT]), I("zg2", [-(-2 * SH2 // ZCR) * 2 * ZCR, NT]), I("zm2", [SH2, T + 2])
    oex1, og1, oex2, og2 = I("oex1", [1024, T]), I("og1", [2048, T]), I("oex2", [1024, T]), I("og2", [2048, T])
    xs_d = I("xs_d", [D, NT])
    sel = E("sel", [128, 2]); cos = E("cos", [64, T]); sin = E("sin", [64, T]); ident = E("ident", [128, 128])
    maskA = E("maskA", [128, 2 * T - 128], BF16); maskC = E("maskC", [128, 2 * T - 128], BF16)
    sink = E("sink", [128, 8]); wlog = E("wlog", [8, 128, 2048])

    def ffn_block(K, pfx):
        g = K.load_vec(pfx + "g_s", E(pfx + "g", [128, 16]))
        K.rmsnorm(pfx + "g_s", g)
        K.ffn(E(pfx + "w1", [F // 128, 128, 16, 128]), E(pfx + "w3", [F // 128, 128, 16, 128]), E(pfx + "w2", [4, 16, 128, 11, 128]))

    def ple_block(K, l):
        g = K.load_vec("pg%d_s" % l, E("pg%d" % l, [128, 16]))
        K.rmsnorm("pg%d_s" % l, g)
        K.ple(E("pwg%d" % l, [16, 128, 16, 128]), E("pwp%d" % l, [16, 128, 2, 128]), E("pT%d" % l, [256, NT]))

    def proj_block(K, l, n, zout, zg):
        g = K.load_vec("mg%d_s" % l, E("mg%d" % l, [128, 16]))
        K.rmsnorm("mg%d_s" % l, g)
        K.proj_out(E("win%d" % l, [n // 128, 128, 16, 128]), zout, n, cc=(zg, ZCR))

    K = TokKernel(NT, nc=nc, P=P)
    K.load_x(E("xT", [D, NT]))
    ffn_block(K, "L0f1")
    K.store_x(xs_d)
    proj_block(K, 0, 2 * SH1, z1, zg1)
    K.done()
    if upto <= 0:
        K0 = MixKernel(nc, P)
        tdbg = K0.sb("dbg", [128, 1024], F32)
        P.dma("sp", "dbg", tdbg[:], z1[0:128, :], w=["dbg"])
        P.dma("sp", "dbg", dram_reg(nc, "out", [D, NT], F32, "ExternalOutput")[0:128, :], tdbg[:], r=["dbg"])
        K0.done()
        return nc
    if upto <= 1:
        K0 = MixKernel(nc, P)
        tdbg = K0.sb("dbg", [128, 1024], F32)
        P.dma("sp", "dbg", tdbg[:], zg1[0:128, :], w=["dbg"])
        P.dma("sp", "dbg", dram_reg(nc, "out", [D, NT], F32, "ExternalOutput")[0:128, :], tdbg[:], r=["dbg"])
        K0.done()
        return nc
    stage_repack(nc, P, zg1, zm1, SH1, sel)
    stage_attn_A(nc, P, zm1, oex1, cos, sin, ident, maskA, 6)
    P.cc_chunk("AllGather", PAIRS, oex1, og1, 0, OCR, OCR, [])
    stage_rwkv(nc, P, zm1, oex1, rw_stage_dram(nc))
    P.cc("AllGather", PAIRS, oex1, og1, 1024, OCR, chunks=[1, 2, 3])
    K = TokKernel(NT, nc=nc, P=P)
    K.load_x(xs_d)
    tok_load_sel(K, og1, sel)
    K.addmm_noload(E("wout0", [16, 128, 16, 128]), D)
    ffn_block(K, "L0f2")
    ple_block(K, 0)
    ffn_block(K, "L1f1")
    K.store_x(xs_d)
    proj_block(K, 1, 2 * SH2, z2, zg2)
    K.done()
    stage_repack(nc, P, zg2, zm2, SH2, sel)
    stage_attn_C(nc, P, zm2, oex2, cos, sin, ident, maskC, sink, 8)
    P.cc_chunk("AllGather", PAIRS, oex2, og2, 0, OCR, OCR, [])
    P.cc_chunk("AllGather", PAIRS, oex2, og2, OCR, OCR, OCR, [])
    stage_attn_D(nc, P, zm2, oex2, cos, sin, ident, wlog, 8, 640, 512)
    P.cc("AllGather", PAIRS, oex2, og2, 1024, OCR, chunks=[2, 3])
    K = TokKernel(NT, nc=nc, P=P)
    K.load_x(xs_d)
    tok_load_sel(K, og2, sel)
    K.addmm_noload(E("wout1", [16, 128, 16, 128]), D)
    ffn_block(K, "L1f2")
    ple_block(K, 1)
    gf = K.load_vec("fg_s", E("fg", [128, 16]))
    K.rmsnorm("fg_s", gf, out_ap=dram_reg(nc, "out", [D, NT], F32, "ExternalOutput"))
    K.done()
    P.barrier()
    P.emit()
    return nc


def ab_col_perm():
    idx = []
    for j in range(2):
        for base in (0, 768, 1536):
            idx += list(range(base + j * 384, base + (j + 1) * 384))
        for base in (2304, 2304 + 1280, 2304 + 2560):
            idx += list(range(base + j * 640, base + (j + 1) * 640))
        idx += list(range(2304 + 3840, 2304 + 4288))
    return np.array(idx)


def cd_col_perm():
    idx = []
    for j in range(2):
        idx += list(range(j * 512, (j + 1) * 512))
        idx += list(range(1024 + j * 64, 1024 + (j + 1) * 64))
        idx += list(range(1152 + j * 64, 1152 + (j + 1) * 64))
        for base in (1280, 2304, 3328):
            idx += list(range(base + j * 512, base + (j + 1) * 512))
    return np.array(idx)


def out_row_perm(layer):
    idx = []
    for rank in range(2):
        if layer == 0:
            idx += list(range(rank * 384, (rank + 1) * 384)) + list(range(768 + rank * 640, 768 + (rank + 1) * 640))
        else:
            idx += list(range(rank * 512, (rank + 1) * 512)) + list(range(1024 + rank * 512, 1024 + (rank + 1) * 512))
    return np.array(idx)


def tile_w(w):
    K_, N_ = w.shape
    return np.ascontiguousarray(np.asarray(w, np.float32).reshape(K_ // 128, 128, N_ // 128, 128).transpose(2, 1, 0, 3))


def tile_w2(w, npass=4, fpass=11):
    F_, D_ = w.shape
    return np.ascontiguousarray(np.asarray(w, np.float32).reshape(npass, fpass, 128, D_ // 128, 128).transpose(0, 3, 2, 1, 4))


def kernel(**inp):
    inp = {k: np.asarray(v) for k, v in inp.items()}
    x = inp["x"].astype(np.float32)
    Bn, S, D = x.shape
    NTK = Bn * S
    per = NTK // NCORES
    c = lambda a: np.ascontiguousarray(a, dtype=np.float32)
    shards = lambda aT: [c(aT[:, i * per:(i + 1) * per]) for i in range(NCORES)]
    cos, sin = rope_tables()
    maskA, _ = toeplitz_mask(f_dil)
    maskC, _ = toeplitz_mask(f_win)
    com = {"cos": cos, "sin": sin, "ident": np.eye(128, dtype=np.float32), "maskA": maskA, "maskC": maskC}
    for l in range(2):
        for nm, key in (("f1", "ffn1"), ("f2", "ffn2")):
            pfx = "L%d%s" % (l, nm)
            com[pfx + "g"] = _pk(inp[key + "_norm"][l])
            com[pfx + "w1"] = tile_w(inp[key + "_w1"][l]); com[pfx + "w3"] = tile_w(inp[key + "_w3"][l]); com[pfx + "w2"] = tile_w2(inp[key + "_w2"][l])
        com["pg%d" % l] = _pk(inp["ple_norm"][l]); com["pwg%d" % l] = tile_w(inp["ple_w_gate"][l]); com["pwp%d" % l] = tile_w(inp["ple_w_proj"][l])
        com["mg%d" % l] = _pk(inp["mix_norm"][l])
    com["win0"] = tile_w(inp["ab_w_in"][0][:, ab_col_perm()])
    com["win1"] = tile_w(inp["cd_w_in"][0][:, cd_col_perm()])
    com["wout0"] = tile_w(inp["ab_w_out"][0][out_row_perm(0), :])
    com["wout1"] = tile_w(inp["cd_w_out"][0][out_row_perm(1), :])
    com["fg"] = _pk(inp["final_norm"])
    com.update(rw_stage_consts())
    rw = {k: inp[k][0] for k in inp if k.startswith("rw_")}
    xs = shards(x.reshape(NTK, D).T)
    p0 = shards(inp["p"][0].reshape(NTK, -1).T)
    p1 = shards(inp["p"][1].reshape(NTK, -1).T)
    rwp = [rw_stage_params(rw, hf) for hf in range(2)]
    ins = []
    for i in range(NCORES):
        j = i % 2
        d = dict(com, xT=xs[i], pT0=p0[i], pT1=p1[i])
        d["sel"] = c(np.broadcast_to(np.array([1.0 - j, float(j)], np.float32)[None, :], (128, 2)))
        d["sink"] = c(np.broadcast_to(inp["c_sink"][0][j][None, :], (128, 8)))
        d["wlog"] = gather_rpb(inp["d_rpb"][0][j * 8:(j + 1) * 8].astype(np.float32))
        d.update(rwp[j])
        ins.append(d)
    res = run_bass_kernel_spmd(build_mega(), ins, core_ids=list(range(NCORES))).results
    outT = np.concatenate([r["out"] for r in res], axis=1)
    return np.ascontiguousarray(outT.T).reshape(Bn, S, D).astype(np.float32)
```
